# Optimizing a Trainium2 kernel written in Bass

```python
import math
import jax
import jax.numpy as jnp
from jax import lax
import numpy as np

D_MODEL = 1024
BATCH = 16
SEQ = 256
DEPTH = 2
DEC_BATCH = 4
DEC_SEQ = 1024
PAST_LEN = 512

GRID_W = 64
NORM_EPS = 1e-6
N_EVEN = (DEPTH + 1) // 2
N_ODD = DEPTH // 2
N_MOD = 6
D_FF = 4 * D_MODEL

GDN_HEADS = 4
GDN_DK = 128
GDN_DV = 128
GDN_CHUNK = 64
QKV_CONV_W = 3
GDN_QK_W = GDN_HEADS * GDN_DK
GDN_V_W = GDN_HEADS * GDN_DV
QKV_W = 2 * GDN_QK_W + GDN_V_W
SC_WIDTH = D_MODEL - GDN_V_W
SC_CONV_W = 3
EV_SPLITS = (QKV_W, GDN_V_W, 2 * GDN_HEADS, 2 * GDN_HEADS, SC_WIDTH, SC_WIDTH, SC_WIDTH)
EV_IN = sum(EV_SPLITS)
EV_OUT = GDN_V_W + SC_WIDTH

ATT_HEADS = 8
ATT_KV_HEADS = 2
ATT_GROUP = ATT_HEADS // ATT_KV_HEADS
ATT_HD = 64
ATT_Q_W = ATT_HEADS * ATT_HD
ATT_KV_W = ATT_KV_HEADS * ATT_HD
WINDOW = 128
ATT_BLOCK = 128
ROPE_BASE = 10000.0
NEG_INF = -1e30
RWKV_HEADS = 8
RWKV_HD = 64
RWKV_W = RWKV_HEADS * RWKV_HD
W_LORA = 64
A_LORA = 64
G_LORA = 128
GN_EPS = 64e-5
RWKV_SPLITS = (RWKV_W, RWKV_W, RWKV_W, 2 * W_LORA, 2 * A_LORA, G_LORA)
RWKV_IN = sum(RWKV_SPLITS)
OD_SPLITS = (ATT_Q_W, ATT_KV_W, ATT_KV_W, RWKV_IN)
OD_IN = sum(OD_SPLITS)
OD_OUT = ATT_Q_W + RWKV_W

kernel_name = "hybrid_diffusion_deltanet_conv_swa_rwkv7_step"

F32 = jnp.float32


def split_cols(x, sizes):
    idx = np.cumsum(sizes)[:-1].tolist()
    return jnp.split(x, idx, axis=-1)


def rmsnorm(x, g):
    x32 = x.astype(F32)
    return (x32 * lax.rsqrt(jnp.mean(x32 * x32, -1, keepdims=True) + NORM_EPS) * g).astype(x.dtype)


def l2norm(x):
    x32 = x.astype(F32)
    return x32 * lax.rsqrt(jnp.sum(x32 * x32, -1, keepdims=True) + 1e-6)


def dwconv_centred(x, w):
    K = w.shape[0]
    p = K // 2
    L = x.shape[1]
    xp = jnp.pad(x, ((0, 0), (p, p), (0, 0)))
    return sum(xp[:, i:i + L] * w[i] for i in range(K))


def token_shift_bi(x, mu):
    x_prev = jnp.pad(x, ((0, 0), (1, 0), (0, 0)))[:, :-1]
    x_next = jnp.pad(x, ((0, 0), (0, 1), (0, 0)))[:, 1:]
    return x + mu[0] * (x_prev - x) + mu[1] * (x_next - x)


def softmax_with_sink(s, sink):
    m = jnp.maximum(jnp.max(s, -1, keepdims=True), sink)
    e = jnp.exp(s - m)
    return e / (jnp.sum(e, -1, keepdims=True) + jnp.exp(sink - m))


def rope_2d(x):
    L = x.shape[1]
    rows = L // GRID_W
    row = jnp.repeat(jnp.arange(rows), GRID_W).astype(F32)
    col = jnp.tile(jnp.arange(GRID_W), rows).astype(F32)
    half = x.shape[-1] // 2
    inv = ROPE_BASE ** (-jnp.arange(0, half, 2, dtype=F32) / half)

    def rot(seg, pos):
        ang = pos[:, None] * inv
        cos, sin = jnp.cos(ang)[None, :, None, :], jnp.sin(ang)[None, :, None, :]
        s1, s2 = jnp.split(seg.astype(F32), 2, axis=-1)
        return jnp.concatenate([s1 * cos - s2 * sin, s2 * cos + s1 * sin], -1)

    return jnp.concatenate([rot(x[..., :half], row), rot(x[..., half:], col)], -1).astype(x.dtype)


def gdn_chunked(q, k, v, g, beta, s0):
    B, L, H, _ = q.shape
    DV = v.shape[-1]
    C = GDN_CHUNK
    n = L // C

    def to_chunks(t):
        t = t.astype(F32).reshape(B, n, C, H, *t.shape[3:])
        return jnp.moveaxis(t, (1, 3), (0, 2))

    qc, kc, vc, bc = to_chunks(q), to_chunks(k), to_chunks(v), to_chunks(beta)
    gc = jnp.cumsum(to_chunks(g), axis=-1)
    incl = jnp.tril(jnp.ones((C, C), dtype=bool))
    strict = jnp.tril(jnp.ones((C, C), dtype=bool), -1)
    diff = gc[..., :, None] - gc[..., None, :]
    decay = jnp.where(incl, jnp.exp(jnp.where(incl, diff, 0.0)), 0.0)
    kb = kc * bc[..., None]
    a_mat = jnp.where(strict, jnp.einsum('nbhid,nbhjd->nbhij', kb, kc) * decay, 0.0)
    lhs = a_mat + jnp.eye(C, dtype=F32)
    rhs = jnp.concatenate([vc * bc[..., None], kb * jnp.exp(gc)[..., None]], axis=-1)
    sol = lax.linalg.triangular_solve(lhs, rhs, left_side=True, lower=True, unit_diagonal=True)
    u, w = sol[..., :DV], sol[..., DV:]
    intra = jnp.where(incl, jnp.einsum('nbhid,nbhjd->nbhij', qc, kc) * decay, 0.0)
    q_dec = qc * jnp.exp(gc)[..., None]
    k_dec = kc * jnp.exp(gc[..., -1:] - gc)[..., None]
    g_end = jnp.exp(gc[..., -1])[..., None, None]

    def step(S, xs):
        u_c, w_c, intra_c, qd_c, kd_c, ge_c = xs
        e = u_c - jnp.einsum('bhck,bhkv->bhcv', w_c, S)
        o = jnp.einsum('bhck,bhkv->bhcv', qd_c, S) + jnp.einsum('bhij,bhjv->bhiv', intra_c, e)
        S = S * ge_c + jnp.einsum('bhck,bhcv->bhkv', kd_c, e)
        return S, o

    s_fin, o = lax.scan(step, s0.astype(F32), (u, w, intra, q_dec, k_dec, g_end))
    o = jnp.moveaxis(o, (0, 2), (1, 3)).reshape(B, L, H, DV)
    return o, s_fin


def rwkv7_scan(r, decay, k, v, kk, a, s0):
    xs = tuple(jnp.moveaxis(t.astype(F32), 1, 0) for t in (r, decay, k, v, kk, kk * a))

    def step(S, xt):
        r_t, w_t, k_t, v_t, kk_t, b_t = xt
        sa = jnp.einsum('bhvk,bhk->bhv', S, kk_t)
        S = S * w_t[:, :, None, :] - sa[..., None] * b_t[:, :, None, :] + v_t[..., None] * k_t[:, :, None, :]
        return S, jnp.einsum('bhvk,bhk->bhv', S, r_t)

    s_fin, y = lax.scan(step, s0.astype(F32), xs)
    return jnp.moveaxis(y, 0, 1), s_fin


def dirs(t_fwd, t_bwd):
    return jnp.stack([t_fwd, t_bwd[:, ::-1]])


def even_mixer(h, w_in, w_out, gdn_conv, a_log, dt_bias, gdn_norm, sc_conv, s0_f, s0_b):
    B, L, _ = h.shape
    qkv, z, beta_raw, a_raw, sc_b, sc_c, sc_h = split_cols(h @ w_in, EV_SPLITS)
    qkv = jax.nn.silu(dwconv_centred(qkv, gdn_conv))
    q, k, v = split_cols(qkv, (GDN_QK_W, GDN_QK_W, GDN_V_W))
    q = l2norm(q.reshape(B, L, GDN_HEADS, GDN_DK)) * GDN_DK ** -0.5
    k = l2norm(k.reshape(B, L, GDN_HEADS, GDN_DK))
    v = v.reshape(B, L, GDN_HEADS, GDN_DV).astype(F32)
    beta = jax.nn.sigmoid(beta_raw.reshape(B, L, 2, GDN_HEADS).astype(F32))
    g = -jnp.exp(a_log.astype(F32)) * jax.nn.softplus(a_raw.reshape(B, L, 2, GDN_HEADS).astype(F32) + dt_bias)
    o2, s2 = jax.vmap(gdn_chunked)(dirs(q, q), dirs(k, k), dirs(v, v), dirs(g[:, :, 0], g[:, :, 1]),
                                   dirs(beta[:, :, 0], beta[:, :, 1]), jnp.stack([s0_f, s0_b]).astype(F32))
    o = o2[0] + o2[1][:, ::-1]
    o = rmsnorm(o, gdn_norm) * jax.nn.silu(z.reshape(B, L, GDN_HEADS, GDN_DV).astype(F32))
    sc = sc_b * dwconv_centred(sc_c * sc_h, sc_conv)
    y = jnp.concatenate([o.reshape(B, L, GDN_V_W).astype(h.dtype), sc], -1) @ w_out
    return y, s2[0], s2[1]


def rwkv7_mixer(x, rw, s0_f, s0_b):
    mu, w0, w_up, a0, a_up, g_up, k_k, k_a, r_k, ln_w, ln_b = rw
    B, L, _ = x.shape
    x = token_shift_bi(x, mu).astype(F32)
    r, k, v, wl, al, gl = split_cols(x, RWKV_SPLITS)
    w_log = -jax.nn.softplus(-(w0 + jnp.einsum('bldr,drc->bldc', jnp.tanh(wl.reshape(B, L, 2, W_LORA)), w_up))) - 0.5
    decay = jnp.exp(-jnp.exp(w_log))
    a = jax.nn.sigmoid(a0 + jnp.einsum('bldr,drc->bldc', al.reshape(B, L, 2, A_LORA), a_up))
    gate = jax.nn.sigmoid(gl) @ g_up

    def heads(t):
        return t.reshape(*t.shape[:-1], RWKV_HEADS, RWKV_HD)

    kk = l2norm(heads(k * k_k))
    k2 = k[:, :, None, :] * (1 + (a - 1) * k_a)
    rh, vh = heads(r), heads(v)
    y2, s2 = jax.vmap(rwkv7_scan)(dirs(rh, rh), dirs(heads(decay[:, :, 0]), heads(decay[:, :, 1])),
                                  dirs(heads(k2[:, :, 0]), heads(k2[:, :, 1])), dirs(vh, vh), dirs(kk, kk),
                                  dirs(heads(a[:, :, 0]), heads(a[:, :, 1])), jnp.stack([s0_f, s0_b]).astype(F32))
    y = y2[0] + y2[1][:, ::-1]
    mean = jnp.mean(y, -1, keepdims=True)
    var = jnp.mean(jnp.square(y - mean), -1, keepdims=True)
    y = ((y - mean) * lax.rsqrt(var + GN_EPS)).reshape(B, L, RWKV_W) * ln_w + ln_b
    bonus = jnp.sum(rh[:, :, None] * heads(k2) * r_k, axis=-1, keepdims=True).sum(axis=2) * vh
    y = (y + bonus.reshape(B, L, RWKV_W)) * gate
    return y, s2[0], s2[1]


def odd_project(h, w_in):
    B, L, _ = h.shape
    q, k, v, rw = split_cols(h @ w_in, OD_SPLITS)
    return (q.reshape(B, L, ATT_HEADS, ATT_HD), k.reshape(B, L, ATT_KV_HEADS, ATT_HD),
            v.reshape(B, L, ATT_KV_HEADS, ATT_HD), rw)


def attend_context(q, kc, vc, sink):
    B, L = q.shape[:2]
    qg = q.reshape(B, L, ATT_KV_HEADS, ATT_GROUP, ATT_HD)
    s = jnp.einsum('bqkgd,bkmd->bkgqm', qg, kc, preferred_element_type=F32) * ATT_HD ** -0.5
    p = softmax_with_sink(s, sink.astype(F32).reshape(ATT_KV_HEADS, ATT_GROUP, 1, 1))
    o = jnp.einsum('bkgqm,bkmd->bqkgd', p.astype(vc.dtype), vc)
    return o.reshape(B, L, ATT_Q_W)


def attend_latent(q, k, v, ck, cv, sink):
    B, L = q.shape[:2]
    nb = L // ATT_BLOCK
    qb = q.reshape(B, nb, ATT_BLOCK, ATT_KV_HEADS, ATT_GROUP, ATT_HD)

    def band(t):
        tb = jnp.pad(t, ((0, 0), (ATT_BLOCK, ATT_BLOCK), (0, 0), (0, 0))).reshape(B, nb + 2, ATT_BLOCK, ATT_KV_HEADS, ATT_HD)
        return jnp.concatenate([tb[:, :-2], tb[:, 1:-1], tb[:, 2:]], axis=2)

    kw, vw = band(k), band(v)
    scale = ATT_HD ** -0.5
    s_loc = jnp.einsum('bnqkgd,bnmkd->bnkgqm', qb, kw, preferred_element_type=F32) * scale
    s_ctx = jnp.einsum('bnqkgd,bkmd->bnkgqm', qb, ck, preferred_element_type=F32) * scale
    qi = jnp.arange(ATT_BLOCK)[:, None]
    kj = jnp.arange(3 * ATT_BLOCK)[None, :]
    key_pos = jnp.arange(nb)[:, None, None] * ATT_BLOCK - ATT_BLOCK + kj
    valid = (jnp.abs(kj - ATT_BLOCK - qi) <= WINDOW) & (key_pos >= 0) & (key_pos < L)
    s_loc = jnp.where(valid[None, :, None, None], s_loc, NEG_INF)
    p = softmax_with_sink(jnp.concatenate([s_loc, s_ctx], -1),
                          sink.astype(F32).reshape(ATT_KV_HEADS, ATT_GROUP, 1, 1))
    p_loc, p_ctx = p[..., :3 * ATT_BLOCK], p[..., 3 * ATT_BLOCK:]
    o = (jnp.einsum('bnkgqm,bnmkd->bnqkgd', p_loc.astype(v.dtype), vw)
         + jnp.einsum('bnkgqm,bkmd->bnqkgd', p_ctx.astype(cv.dtype), cv))
    return o.reshape(B, L, ATT_Q_W)


def odd_mixer_context(h, w_in, w_out, sink, rw):
    B = h.shape[0]
    q, k, v, x_rw = odd_project(h, w_in)
    kc, vc = jnp.swapaxes(k, 1, 2), jnp.swapaxes(v, 1, 2)
    att = attend_context(q, kc, vc, sink)
    z0 = jnp.zeros((B, RWKV_HEADS, RWKV_HD, RWKV_HD), F32)
    rwo, s_f, s_b = rwkv7_mixer(x_rw, rw, z0, z0)
    y = jnp.concatenate([att, rwo.astype(att.dtype)], -1) @ w_out
    return y, kc, vc, s_f, s_b


def odd_mixer_latent(h, w_in, w_out, sink, rw, ck, cv, s0_f, s0_b):
    q, k, v, x_rw = odd_project(h, w_in)
    att = attend_latent(rope_2d(q), rope_2d(k), v, ck, cv, sink)
    rwo, _, _ = rwkv7_mixer(x_rw, rw, s0_f, s0_b)
    return jnp.concatenate([att, rwo.astype(att.dtype)], -1) @ w_out


def sq_relu_mlp(h, w1, w2):
    return jnp.square(jax.nn.relu(h @ w1)) @ w2


def setup_inputs(seed: int = 0) -> dict:
    key = jax.random.key(seed)
    ks = iter(jax.random.split(key, 48))

    def nrm(shape, scale=1.0):
        return jax.random.normal(next(ks), shape, F32) * scale

    def unif(shape, lo, hi):
        return jax.random.uniform(next(ks), shape, F32, minval=lo, maxval=hi)

    dt = jnp.exp(unif((N_EVEN, 2, GDN_HEADS), math.log(1e-3), math.log(1e-1)))
    return {
        "x_prompt": nrm((BATCH, SEQ, D_MODEL)),
        "x_sample": nrm((DEC_BATCH, DEC_SEQ, D_MODEL)),
        "state_gdn_fwd": nrm((DEC_BATCH, N_EVEN, GDN_HEADS, GDN_DK, GDN_DV), 0.3),
        "state_gdn_bwd": nrm((DEC_BATCH, N_EVEN, GDN_HEADS, GDN_DK, GDN_DV), 0.3),
        "cache_attn_k": nrm((DEC_BATCH, N_ODD, ATT_KV_HEADS, PAST_LEN, ATT_HD)),
        "cache_attn_v": nrm((DEC_BATCH, N_ODD, ATT_KV_HEADS, PAST_LEN, ATT_HD)),
        "state_rwkv_fwd": nrm((DEC_BATCH, N_ODD, RWKV_HEADS, RWKV_HD, RWKV_HD), 0.5),
        "state_rwkv_bwd": nrm((DEC_BATCH, N_ODD, RWKV_HEADS, RWKV_HD, RWKV_HD), 0.5),
        "c": nrm((DEC_BATCH, D_MODEL)),
        "c_ctx": nrm((D_MODEL,)),
        "mod_w": nrm((DEPTH, D_MODEL, N_MOD * D_MODEL), D_MODEL ** -0.5),
        "mod_b": nrm((DEPTH, N_MOD * D_MODEL), 0.02),
        "norm_mix": 1.0 + nrm((DEPTH, D_MODEL), 0.02),
        "norm_mlp": 1.0 + nrm((DEPTH, D_MODEL), 0.02),
        "mlp_w1": nrm((DEPTH, D_MODEL, D_FF), D_MODEL ** -0.5),
        "mlp_w2": nrm((DEPTH, D_FF, D_MODEL), D_FF ** -0.5),
        "norm_final": 1.0 + nrm((D_MODEL,), 0.02),
        "ev_w_in": nrm((N_EVEN, D_MODEL, EV_IN), D_MODEL ** -0.5),
        "ev_w_out": nrm((N_EVEN, EV_OUT, D_MODEL), EV_OUT ** -0.5),
        "gdn_conv": nrm((N_EVEN, QKV_CONV_W, QKV_W), QKV_CONV_W ** -0.5),
        "gdn_a_log": jnp.log(unif((N_EVEN, 2, GDN_HEADS), 1.0, 16.0)),
        "gdn_dt_bias": dt + jnp.log(-jnp.expm1(-dt)),
        "gdn_norm": 1.0 + nrm((N_EVEN, GDN_DV), 0.02),
        "sc_conv": nrm((N_EVEN, SC_CONV_W, SC_WIDTH), SC_CONV_W ** -0.5),
        "od_w_in": nrm((N_ODD, D_MODEL, OD_IN), D_MODEL ** -0.5),
        "od_w_out": nrm((N_ODD, OD_OUT, D_MODEL), OD_OUT ** -0.5),
        "attn_sink": nrm((N_ODD, ATT_HEADS), 0.5),
        "rwkv_mu": unif((N_ODD, 2, RWKV_IN), 0.0, 0.5),
        "rwkv_w0": unif((N_ODD, 2, RWKV_W), -6.0, -1.0),
        "rwkv_w_up": nrm((N_ODD, 2, W_LORA, RWKV_W), 0.5 * W_LORA ** -0.5),
        "rwkv_a0": nrm((N_ODD, 2, RWKV_W), 0.1),
        "rwkv_a_up": nrm((N_ODD, 2, A_LORA, RWKV_W), A_LORA ** -0.5),
        "rwkv_g_up": nrm((N_ODD, G_LORA, RWKV_W), G_LORA ** -0.5),
        "rwkv_k_k": 0.85 + nrm((N_ODD, RWKV_W), 0.02),
        "rwkv_k_a": 1.0 + nrm((N_ODD, RWKV_W), 0.02),
        "rwkv_r_k": nrm((N_ODD, RWKV_HEADS, RWKV_HD), 0.1),
        "rwkv_ln_w": 1.0 + nrm((N_ODD, RWKV_W), 0.02),
        "rwkv_ln_b": nrm((N_ODD, RWKV_W), 0.02),
    }


def reference(x_prompt, x_sample, state_gdn_fwd, state_gdn_bwd, cache_attn_k, cache_attn_v,
              state_rwkv_fwd, state_rwkv_bwd, c, c_ctx, mod_w, mod_b, norm_mix, norm_mlp, mlp_w1, mlp_w2,
              norm_final, ev_w_in, ev_w_out, gdn_conv, gdn_a_log, gdn_dt_bias, gdn_norm, sc_conv,
              od_w_in, od_w_out, attn_sink, rwkv_mu, rwkv_w0, rwkv_w_up, rwkv_a0, rwkv_a_up, rwkv_g_up,
              rwkv_k_k, rwkv_k_a, rwkv_r_k, rwkv_ln_w, rwkv_ln_b):
    xp, xs = x_prompt, x_sample
    Bp = x_prompt.shape[0]
    gdn_f, gdn_b, att_k, att_v, rw_f, rw_b = [], [], [], [], [], []
    for layer in range(DEPTH):
        mod_p = jnp.split(jax.nn.silu(c_ctx) @ mod_w[layer] + mod_b[layer], N_MOD, axis=-1)
        mod_s = jnp.split((jax.nn.silu(c) @ mod_w[layer] + mod_b[layer])[:, None, :], N_MOD, axis=-1)
        hp = rmsnorm(xp, norm_mix[layer]) * (1 + mod_p[1]) + mod_p[0]
        hs = rmsnorm(xs, norm_mix[layer]) * (1 + mod_s[1]) + mod_s[0]
        if layer % 2 == 0:
            e = layer // 2
            ev = (ev_w_in[e], ev_w_out[e], gdn_conv[e], gdn_a_log[e], gdn_dt_bias[e], gdn_norm[e], sc_conv[e])
            z0 = jnp.zeros((Bp, GDN_HEADS, GDN_DK, GDN_DV), F32)
            yp, s_f, s_b = even_mixer(hp, *ev, z0, z0)
            ys, _, _ = even_mixer(hs, *ev, state_gdn_fwd[:, e], state_gdn_bwd[:, e])
            gdn_f.append(s_f)
            gdn_b.append(s_b)
        else:
            o = layer // 2
            rw = (rwkv_mu[o], rwkv_w0[o], rwkv_w_up[o], rwkv_a0[o], rwkv_a_up[o], rwkv_g_up[o],
                  rwkv_k_k[o], rwkv_k_a[o], rwkv_r_k[o], rwkv_ln_w[o], rwkv_ln_b[o])
            yp, kc, vc, s_f, s_b = odd_mixer_context(hp, od_w_in[o], od_w_out[o], attn_sink[o], rw)
            ys = odd_mixer_latent(hs, od_w_in[o], od_w_out[o], attn_sink[o], rw,
                                  cache_attn_k[:, o], cache_attn_v[:, o], state_rwkv_fwd[:, o], state_rwkv_bwd[:, o])
            att_k.append(kc)
            att_v.append(vc)
            rw_f.append(s_f)
            rw_b.append(s_b)
        xp = xp + mod_p[2] * yp
        xs = xs + mod_s[2] * ys
        hp = rmsnorm(xp, norm_mlp[layer]) * (1 + mod_p[4]) + mod_p[3]
        hs = rmsnorm(xs, norm_mlp[layer]) * (1 + mod_s[4]) + mod_s[3]
        xp = xp + mod_p[5] * sq_relu_mlp(hp, mlp_w1[layer], mlp_w2[layer])
        xs = xs + mod_s[5] * sq_relu_mlp(hs, mlp_w1[layer], mlp_w2[layer])
    y_prompt = rmsnorm(xp, norm_final)
    y_sample = rmsnorm(xs, norm_final)
    return (y_prompt, y_sample, jnp.stack(gdn_f, axis=1), jnp.stack(gdn_b, axis=1),
            jnp.stack(att_k, axis=1), jnp.stack(att_v, axis=1), jnp.stack(rw_f, axis=1), jnp.stack(rw_b, axis=1))
```

```python
import contextlib
import os
import numpy as np
import concourse.bass as bass
import concourse.mybir as mybir
from concourse.bass_utils import run_bass_kernel_spmd

F32 = mybir.dt.float32
BF16 = mybir.dt.bfloat16
AF = mybir.ActivationFunctionType
ALU = mybir.AluOpType

ENGS = ['pe', 'act', 'dve', 'pool', 'sp']
DMAQ = ('sp', 'pool')
KRING = 6
NT = 1024
EPS = 1e-6


class Sched:
    def __init__(self, nc):
        self.nc = nc
        self.prog = {e: [] for e in ENGS}
        self.cnt = {e: 0 for e in ENGS}
        self.known = {e: {} for e in ENGS}
        self.res = {}
        self.dma_i = {q: 0 for q in DMAQ}
        self.sems = {}
        self.eng_obj = {'pe': nc.tensor, 'act': nc.scalar, 'dve': nc.vector,
                        'pool': nc.gpsimd, 'sp': nc.sync}

    def alloc_sems(self, stack):
        for e in ['pe', 'act', 'dve', 'pool']:
            self.sems[e] = stack.enter_context(self.nc.semaphore('c_' + e))
        for q in DMAQ:
            for j in range(KRING):
                self.sems[(q, j)] = stack.enter_context(self.nc.semaphore('d_%s%d' % (q, j)))

    def _r(self, k):
        if k not in self.res:
            self.res[k] = {'w': None, 'r': {}}
        return self.res[k]

    def _need(self, eng, waits, dep, own):
        if dep is None:
            return
        sk, val = dep
        if sk == own:
            return
        if self.known[eng].get(sk, 0) >= val:
            return
        if waits.get(sk, 0) < val:
            waits[sk] = val

    def op(self, eng, fn, reads=(), writes=()):
        own = eng
        skip = eng if eng == 'pe' else None
        waits = {}
        for k in reads:
            self._need(eng, waits, self._r(k)['w'], skip)
        for k in writes:
            r = self._r(k)
            self._need(eng, waits, r['w'], skip)
            for sk, v in r['r'].items():
                self._need(eng, waits, (sk, v), skip)
        self.cnt[eng] += 1
        v = self.cnt[eng]
        for sk, val in waits.items():
            self.known[eng][sk] = val
        self.prog[eng].append((fn, list(waits.items()), (own, 1)))
        for k in reads:
            r = self._r(k)
            if r['r'].get(own, 0) < v:
                r['r'][own] = v
        for k in writes:
            r = self._r(k)
            r['w'] = (own, v)
            r['r'] = {}

    def dma(self, q, fn, reads=(), writes=()):
        i = self.dma_i[q]
        self.dma_i[q] += 1
        own = (q, i % KRING)
        val = 16 * (i // KRING + 1)
        waits = {}
        if i >= KRING:
            self._need(q, waits, (own, val - 16), None)
        for k in reads:
            self._need(q, waits, self._r(k)['w'], None)
        for k in writes:
            r = self._r(k)
            self._need(q, waits, r['w'], None)
            for sk, v in r['r'].items():
                self._need(q, waits, (sk, v), None)
        for sk, v in waits.items():
            self.known[q][sk] = v
        self.prog[q].append((fn, list(waits.items()), (own, 16)))
        for k in reads:
            r = self._r(k)
            if r['r'].get(own, 0) < val:
                r['r'][own] = val
        for k in writes:
            r = self._r(k)
            r['w'] = (own, val)
            r['r'] = {}

    def _all_done(self):
        waits = []
        for q in DMAQ:
            n = self.dma_i[q]
            for j in range(KRING):
                cntj = len(range(j, n, KRING))
                if cntj:
                    waits.append(((q, j), 16 * cntj))
        for e in ['pe', 'act', 'dve', 'pool']:
            if self.cnt[e]:
                waits.append((e, self.cnt[e]))
        return waits

    def barrier(self):
        waits = self._all_done()
        for e in ENGS:
            w2 = [(sk, v) for sk, v in waits if sk != e and self.known[e].get(sk, 0) < v]
            for sk, v in w2:
                self.known[e][sk] = v
            self.prog[e].append((None, w2, None))
        self.res = {}

    def finish(self, eng='sp'):
        self.prog[eng].append((None, self._all_done(), None))

    def emit(self, block):
        S = self

        def replay(e):
            def body(_eng):
                eo = S.eng_obj[e]
                for fn, waits, inc in S.prog[e]:
                    for sk, v in waits:
                        eo.wait_ge(S.sems[sk], v)
                    if fn is not None:
                        ins = fn(eo)
                        ins.then_inc(S.sems[inc[0]], inc[1])
            return body
        block.tensor(replay('pe'))
        block.scalar(replay('act'))
        block.vector(replay('dve'))
        block.gpsimd(replay('pool'))
        block.sync(replay('sp'))

    def flush(self):
        self.barrier()
        with self.nc.Block() as block:
            self.emit(block)
        self.prog = {e: [] for e in ENGS}


def _pp_layout():
    ent = [('mod_b', 96), ('norm_mix', 16), ('norm_mlp', 16), ('norm_final', 8),
           ('gdn_conv', 36), ('sc_conv', 12), ('gdn_norm', 1), ('a_log', 1), ('dt_bias', 1),
           ('mu', 30), ('w0', 8), ('a0', 8), ('k_k', 4), ('k_a', 4), ('ln_w', 4), ('ln_b', 4), ('r_k', 4), ('sink', 8)]
    off = {}
    o = 0
    for n, w in ent:
        off[n] = (o, w)
        o += w
    return off, o


PP_OFF, PP_N = _pp_layout()
C_ID, C_U, C_L, C_SU, C_SL, C_N = 0, 128, 192, 256, 320, 768
C_S16L, C_O32L, C_O64L, C_S16U, C_O32U, C_O64U = 384, 448, 512, 576, 640, 704


def make_consts():
    c = np.zeros((128, C_N), np.float32)
    c[:, C_ID:C_ID + 128] = np.eye(128, dtype=np.float32)
    p = np.arange(64)[:, None]
    f = np.arange(64)[None, :]
    c[:64, C_U:C_U + 64] = (p <= f)
    c[:64, C_L:C_L + 64] = (p >= f)
    c[:64, C_SU:C_SU + 64] = (p < f)
    c[:64, C_SL:C_SL + 64] = (p > f)
    c[:64, C_S16L:C_S16L + 64] = (p > f) & (p // 16 == f // 16)
    c[:64, C_O32L:C_O32L + 64] = (p > f) & (p // 32 == f // 32) & (p // 16 != f // 16)
    c[:64, C_O64L:C_O64L + 64] = (p > f) & (p // 32 != f // 32)
    c[:64, C_S16U:C_S16U + 64] = (p < f) & (p // 16 == f // 16)
    c[:64, C_O32U:C_O32U + 64] = (p < f) & (p // 32 == f // 32) & (p // 16 != f // 16)
    c[:64, C_O64U:C_O64U + 64] = (p < f) & (p // 32 != f // 32)
    return c


def fm(v, nch):
    return np.ascontiguousarray(np.asarray(v, np.float32).reshape(nch, 128).T)


def make_pp(inp):
    pp = np.zeros((128, PP_N), np.float32)

    def put(name, arr):
        o, w = PP_OFF[name]
        assert arr.shape == (128, w), (name, arr.shape, w)
        pp[:, o:o + w] = arr
    put('mod_b', np.concatenate([fm(inp['mod_b'][l], 48) for l in range(2)], 1))
    put('norm_mix', np.concatenate([fm(inp['norm_mix'][l], 8) for l in range(2)], 1))
    put('norm_mlp', np.concatenate([fm(inp['norm_mlp'][l], 8) for l in range(2)], 1))
    put('norm_final', fm(inp['norm_final'], 8))
    put('gdn_conv', np.concatenate([fm(inp['gdn_conv'][0][i], 12) for i in range(3)], 1))
    put('sc_conv', np.concatenate([fm(inp['sc_conv'][0][i], 4) for i in range(3)], 1))
    put('gdn_norm', np.asarray(inp['gdn_norm'][0], np.float32).reshape(128, 1))
    a = np.zeros((128, 1), np.float32)
    a[:8, 0] = np.asarray(inp['gdn_a_log'][0], np.float32).reshape(8)
    put('a_log', a)
    a = np.zeros((128, 1), np.float32)
    a[:8, 0] = np.asarray(inp['gdn_dt_bias'][0], np.float32).reshape(8)
    put('dt_bias', a)
    put('mu', np.concatenate([fm(inp['rwkv_mu'][0][d], 15) for d in range(2)], 1))
    put('w0', np.concatenate([fm(inp['rwkv_w0'][0][d], 4) for d in range(2)], 1))
    put('a0', np.concatenate([fm(inp['rwkv_a0'][0][d], 4) for d in range(2)], 1))
    put('k_k', fm(inp['rwkv_k_k'][0], 4))
    put('k_a', fm(inp['rwkv_k_a'][0], 4))
    put('ln_w', fm(inp['rwkv_ln_w'][0], 4))
    put('ln_b', fm(inp['rwkv_ln_b'][0], 4))
    put('r_k', fm(np.asarray(inp['rwkv_r_k'][0]).reshape(512), 4))
    put('sink', np.tile(np.asarray(inp['attn_sink'][0], np.float32).reshape(1, 8), (128, 1)))
    return pp


STAGE = 9
L1PART = int(os.environ.get('L1PART', '9'))
SUB = 9


def build_nc(do_l1=False):
    nc = bass.Bass("TRN2", target_bir_lowering=False)

    def din(name, shape):
        return nc.dram_tensor(name, list(shape), F32, kind="ExternalInput").ap()

    def dout(name, shape):
        return nc.dram_tensor(name, list(shape), F32, kind="ExternalOutput").ap()

    xT_d = din("xT", [1024, NT])
    cv_d = din("cv", [128, 8])
    flag_d = din("flag", [128, 1])
    s0_d = din("gs0", [2, 4, 128, 128])
    pp_d = din("pp", [128, PP_N])
    cst_d = din("cst", [128, C_N])
    modw_d = din("mod_w", [2, 1024, 6144])
    w1_d = din("mlp_w1", [2, 1024, 4096])
    w2_d = din("mlp_w2", [2, 4096, 1024])
    evin_d = din("ev_w_in", [1024, 3600])
    evout_d = din("ev_w_out", [1024, 1024])
    odin_d = din("od_w_in_x", [1024, 2816])
    odout_d = din("od_w_out", [1024, 1024])
    wup_d = din("w_up", [128, 512])
    aup_d = din("a_up", [128, 512])
    gup_d = din("g_up", [128, 512])
    bones_d = din("bones", [128, 128])
    rm_d = din("rm", [128, 128])
    ropec_d = din("ropec", [128, NT])
    ropes_d = din("ropes", [128, NT])
    maskb_d = din("maskb", [8, 128, 896])
    ckT_d = din("ckT", [2, 128, 512])
    cvt_d = din("cvt", [128, 4, 128])
    rs0_d = din("rs0", [2, 128, 4, 64])
    kvo_d = dout("kvo", [2, 128, NT])
    rso_d = dout("rso", [2, 4, 128, 4, 64])
    yT_d = dout("yT", [1024, NT])
    gso_d = dout("gso", [2, 4, 4, 128, 128])

    with contextlib.ExitStack() as st:
        S = Sched(nc)
        S.alloc_sems(st)
        with nc.Block() as blk0:
            def _clr(_e):
                for sm in S.sems.values():
                    nc.sync.sem_clear(sm)
            blk0.sync(_clr)

        def tile(name, shape, dt=F32):
            return st.enter_context(nc.sbuf_tensor('sb_' + name, list(shape), dt))

        ps = [st.enter_context(nc.psum_tensor("ps%d" % i, [128, 512], F32)) for i in range(8)]
        psk = ["ps%d" % i for i in range(8)]
        state = {'ps': 0, 'wb': 0}

        def nextps():
            b = state['ps']
            state['ps'] = (b + 1) % 8
            return b

        def pe_mode(mode):
            if state.get('pemode') != mode:
                state['pemode'] = mode
                if S.cnt['pe'] > 0:
                    S.prog['pe'].append((None, [('pe', S.cnt['pe'])], None))

        def rnd(n):
            return 32 if n <= 32 else (64 if n <= 64 else 128)

        def mm(out, lhsT, rhs, start=True, stop=True, r=(), w=()):
            pe_mode(('mm', rnd(lhsT.shape[0]), rnd(lhsT.shape[-1]), lhsT.start_partition(), out.start_partition()))
            S.op('pe', lambda e: e.matmul(out, lhsT, rhs, start=start, stop=stop), r, w)

        def tr(out, in_, ident, r=(), w=()):
            pe_mode(('tr', rnd(in_.shape[0]), rnd(in_.shape[-1]), in_.start_partition(), out.start_partition()))
            S.op('pe', lambda e: e.transpose(out, in_, ident), r, w)

        def act(out, in_, func, r=(), w=(), bias=None, scale=None):
            kw = {}
            if bias is not None:
                kw['bias'] = bias
            if scale is not None:
                kw['scale'] = scale
            S.op('act', lambda e: e.activation(out, in_, func, **kw), r, w)

        def tt(out, a, b, op, r=(), w=(), eng='dve'):
            S.op(eng, lambda e: e.tensor_tensor(out, a, b, op), r, w)

        def ts(out, a, s1, op0, r=(), w=(), s2=None, op1=None, eng='dve'):
            if op1 is None:
                S.op(eng, lambda e: e.tensor_scalar(out, a, s1, None, op0), r, w)
            else:
                S.op(eng, lambda e: e.tensor_scalar(out, a, s1, s2, op0, op1), r, w)

        def stt(out, a, s, b, op0, op1, r=(), w=()):
            S.op('dve', lambda e: e.scalar_tensor_tensor(out, a, s, b, op0, op1), r, w)

        def cp(out, in_, r=(), w=(), eng='dve'):
            if eng == 'act':
                S.op('act', lambda e: e.copy(out, in_), r, w)
            else:
                S.op(eng, lambda e: e.tensor_scalar(out, in_, 1.0, None, ALU.mult), r, w)

        def rcp(out, in_, r=(), w=()):
            S.op('dve', lambda e: e.reciprocal(out, in_), r, w)

        def scan(out, d0, d1, r=(), w=()):
            S.op('dve', lambda e: e.tensor_tensor_scan(out, d0, d1, 0.0, ALU.mult, ALU.add), r, w)

        def mset(ap, val, r=(), w=(), eng='dve'):
            S.op(eng, lambda e: e.memset(ap, val), r, w)

        def dma(q, out, in_, r=(), w=()):
            S.dma(q, lambda e: e.dma_start(out=out, in_=in_), r, w)

        x_sb = tile("x_sb", [128, 8, NT])
        hT = None
        wb = None
        cst = tile("cst", [128, C_N])
        pp = tile("pp", [128, PP_N])
        cv = tile("cv", [128, 8])
        cvs = tile("cvs", [128, 8], BF16)
        flag = tile("flag", [128, 1])
        ident_bf = tile("ident_bf", [128, 128], BF16)
        ones_bf = tile("ones_bf", [128, 128], BF16)
        ones_f = tile("ones_f", [128, 128])
        modv = tile("modv", [128, 2, 48])
        gsA = tile("gsA", [128, 2, 8])
        gsB = tile("gsB", [128, 2, 8])
        sqb = rstd = ntmp = None

        def alloc_work(sc, tag, with_h=True):
            nonlocal hT, wb, sqb, rstd, ntmp
            A = lambda n, s_, d=F32: sc.enter_context(nc.sbuf_tensor('sb_%s_%s' % (n, tag), list(s_), d))
            if with_h:
                hT = A("hT", [128, 8, NT], BF16)
            wb = [A("wb%d" % i, [128, 4096], BF16) for i in range(3)]
            sqb = A("sqb", [128, 2, 512], BF16)
            rstd = A("rstd", [128, 512])
            ntmp = [A("ntmp%d" % i, [128, 512]) for i in range(2)]

        sc0 = contextlib.ExitStack()
        alloc_work(sc0, 'p0', with_h=False)

        ident_f = cst[:, C_ID:C_ID + 128]

        def ppc(name, j0=0, n=None):
            o, wd = PP_OFF[name]
            if n is None:
                n = wd - j0
            return pp[:, o + j0:o + j0 + n]

        dma('sp', cst[:, :], cst_d[:, :], w=['cst'])
        dma('sp', pp[:, :], pp_d[:, :], w=['pp'])
        dma('sp', cv[:, :], cv_d[:, :], w=['cv'])
        dma('sp', flag[:, :], flag_d[:, :], w=['flag'])
        for fc in range(8):
            dma('sp', x_sb[:, fc, :], xT_d[fc * 128:(fc + 1) * 128, :], w=[('x', fc)])
        act(cvs[:, :], cv[:, :], AF.Silu, r=['cv'], w=['cvs'])
        cp(ident_bf[:, :], ident_f, r=['cst'], w=['ident_bf'])
        mset(ones_bf[:, :], 1.0, w=['ones_bf'])
        mset(ones_f[:, :], 1.0, w=['ones_f'])

        def load_piece(wd_ap, kcn, wdth):
            slot = state['wb']
            state['wb'] = (slot + 1) % 3
            view = wb[slot][:, 0:kcn * wdth].rearrange("p (k n) -> p k n", k=kcn)
            dma('pool', view, wd_ap.rearrange("(k p) n -> p k n", p=128), w=[('wb', slot)])
            return slot, view

        for l in range(2):
            bm = nextps()
            for oc in range(12):
                slot, wv = load_piece(modw_d[l, :, oc * 512:(oc + 1) * 512], 8, 512)
                for c4 in range(4):
                    ocn = oc * 4 + c4
                    for kc in range(8):
                        mm(ps[bm][:, ocn:ocn + 1], wv[:, kc, c4 * 128:(c4 + 1) * 128], cvs[:, kc:kc + 1],
                           start=(kc == 0), stop=(kc == 7), r=[('wb', slot), 'cvs'], w=[psk[bm]])
            tt(modv[:, l, :], ps[bm][:, 0:48], ppc('mod_b', l * 48, 48), ALU.add,
               r=[psk[bm], 'pp'], w=['modv'])
            stt(gsA[:, l, :], modv[:, l, 8:16], 1.0, ppc('norm_mix', l * 8, 8), ALU.add, ALU.mult,
                r=['modv', 'pp'], w=['gsA'])
            stt(gsB[:, l, :], modv[:, l, 32:40], 1.0, ppc('norm_mlp', l * 8, 8), ALU.add, ALU.mult,
                r=['modv', 'pp'], w=['gsB'])

        S.flush()
        sc0.close()

        def norm_mod(gs_ap, shift_ap, dst, dst_key, final=False):
            for th in range(2):
                tsl = slice(th * 512, (th + 1) * 512)
                b = nextps()
                for fc in range(8):
                    act(sqb[:, fc % 2, :], x_sb[:, fc, tsl], AF.Square, r=[('x', fc)], w=[('sqb', fc % 2)])
                    mm(ps[b][:, :], ones_bf[:, :], sqb[:, fc % 2, :], start=(fc == 0), stop=(fc == 7),
                       r=['ones_bf', ('sqb', fc % 2)], w=[psk[b]])
                act(rstd[:, :], ps[b][:, :], AF.Sqrt, r=[psk[b], 'epsb'], w=['rstd'], bias=epsb[:, 0:1], scale=1.0 / 1024)
                rcp(rstd[:, :], rstd[:, :], r=['rstd'], w=['rstd'])
                for fc in range(8):
                    k = fc % 2
                    tt(ntmp[k][:, :], x_sb[:, fc, tsl], rstd[:, :], ALU.mult,
                       r=[('x', fc), 'rstd'], w=[('ntmp', k)])
                    if final:
                        act(dst[:, k, :], ntmp[k][:, :], AF.Identity, r=[('ntmp', k), 'pp'],
                            w=[(dst_key, k)], scale=gs_ap[:, fc:fc + 1])
                        dma('sp', yT_d[fc * 128:(fc + 1) * 128, tsl], dst[:, k, :], r=[(dst_key, k)])
                    else:
                        act(dst[:, fc, tsl], ntmp[k][:, :], AF.Identity, r=[('ntmp', k), 'modv', 'gsA', 'gsB'],
                            w=[(dst_key, fc)], scale=gs_ap[:, fc:fc + 1], bias=shift_ap[:, fc:fc + 1])

        epsb = tile("epsb", [128, 4])
        mset(epsb[:, 0:1], EPS, w=['epsb'])
        mset(epsb[:, 1:2], 1e-6, w=['epsb'])
        mset(epsb[:, 2:3], 64e-5, w=['epsb'])

        def linear(wd, kcn, pieces, src, src_key, consumer):
            for (c0, wdth) in pieces:
                slot, wv = load_piece(wd[:, c0:c0 + wdth], kcn, wdth)
                for cs in range(0, wdth, 128):
                    cw = min(128, wdth - cs)
                    for th in range(2):
                        b = nextps()
                        for kc in range(kcn):
                            mm(ps[b][0:cw, :], wv[:, kc, cs:cs + cw], src[:, kc, th * 512:(th + 1) * 512],
                               start=(kc == 0), stop=(kc == kcn - 1),
                               r=[('wb', slot), (src_key, kc)], w=[psk[b]])
                        consumer(c0 + cs, cw, th, b)

        def mlp(l, uT):
            norm_mod(gsB[:, l, :], modv[:, l, 24:32], hT, 'hT')

            def c1(c0, cw, th, b):
                oc = c0 // 128
                act(ntmp[th][:, :], ps[b][:, :], AF.Relu, r=[psk[b]], w=[('ntmp', th)])
                tt(uT[:, oc, th * 512:(th + 1) * 512], ntmp[th][:, :], ntmp[th][:, :], ALU.mult,
                   r=[('ntmp', th)], w=[('uT', oc)])
            linear(w1_d[l], 8, [(i * 512, 512) for i in range(8)], hT, 'hT', c1)

            def c2(c0, cw, th, b):
                fc = c0 // 128
                tsl = slice(th * 512, (th + 1) * 512)
                stt(x_sb[:, fc, tsl], ps[b][:, :], modv[:, l, 40 + fc:41 + fc], x_sb[:, fc, tsl],
                    ALU.mult, ALU.add, r=[psk[b], 'modv', ('x', fc)], w=[('x', fc)])
            linear(w2_d[l], 32, [(i * 128, 128) for i in range(8)], uT, 'uT', c2)

        def layer0_mixer(sc):
            T = lambda n, s, d=F32: sc.enter_context(nc.sbuf_tensor('sb_' + n, list(s), d))
            qT = T("qT", [128, 4, NT], BF16)
            kT = T("kT", [128, 4, NT], BF16)
            vT = T("vT", [128, 4, NT], BF16)
            mixT = T("mixT", [128, 8, NT], BF16)
            oacc = T("oacc", [128, 4, NT])
            pad = [T("pad%d" % i, [128, 4, 258]) for i in range(2)]
            cacc1 = T("cacc", [128, 4, 256])
            cacc = [cacc1, cacc1]
            qsq = T("qsq", [128, NT], BF16)
            rinv = T("rinv", [128, NT])
            sctmp = T("sctmp", [128, NT])
            betaT = T("betaT", [8, NT])
            gT = T("gT", [8, NT])
            negA = T("negA", [8, 1])
            Sf = T("Sf", [128, 4, 128])
            Sb = T("Sb", [128, 4, 128], BF16)

            for i in range(2):
                mset(pad[i][:, :, :], 0.0, w=[('pad', i)])
            act(negA[:, :], ppc('a_log')[0:8, :], AF.Exp, r=['pp'], w=['negA'])
            ts(negA[:, :], negA[:, :], -1.0, ALU.mult, r=['negA'], w=['negA'])

            norm_mod(gsA[:, 0, :], modv[:, 0, 0:8], hT, 'hT')

            cstate = {'i': 0}
            if SUB < 1:
                return

            def conv3(pi, wname, nch, ch):
                p = pad[pi]
                ts(p[:, 1:4, 0:1], p[:, 0:3, 256:257], flag[:, 0:1], ALU.mult,
                   r=[('pad', pi), 'flag'], w=[('pad', pi)])
                ts(p[:, 0:3, 257:258], p[:, 1:4, 1:2], flag[:, 0:1], ALU.mult,
                   r=[('pad', pi), 'flag'], w=[('pad', pi)])
                o, _ = PP_OFF[wname]
                w0 = pp[:, o + 0 * nch + ch:o + 0 * nch + ch + 1]
                w1 = pp[:, o + 1 * nch + ch:o + 1 * nch + ch + 1]
                w2 = pp[:, o + 2 * nch + ch:o + 2 * nch + ch + 1]
                ts(cacc[pi][:, :, :], p[:, :, 0:256], w0, ALU.mult, r=[('pad', pi), 'pp'], w=['cacc'])
                stt(cacc[pi][:, :, :], p[:, :, 1:257], w1, cacc[pi][:, :, :], ALU.mult, ALU.add,
                    r=[('pad', pi), 'pp', 'cacc'], w=['cacc'])
                stt(cacc[pi][:, :, :], p[:, :, 2:258], w2, cacc[pi][:, :, :], ALU.mult, ALU.add,
                    r=[('pad', pi), 'pp', 'cacc'], w=['cacc'])

            def pad_in(pi, th):
                return pad[pi][:, 2 * th:2 * th + 2, 1:257]

            def ps3(b):
                return ps[b][:, :].rearrange("p (s t) -> p s t", s=2)

            def c_qkv(c0, cw, th, b):
                ch = c0 // 128
                pi = ch % 2
                cp(pad_in(pi, th), ps3(b), r=[psk[b]], w=[('pad', pi)], eng='act')
                if th == 0:
                    return
                conv3(pi, 'gdn_conv', 12, ch)
                flat = cacc[pi][:, :, :].rearrange("p s t -> p (s t)")
                h = ch % 4
                if ch >= 8:
                    act(vT[:, h, :], flat, AF.Silu, r=['cacc'], w=[('vT', h)])
                    return
                act(sctmp[:, :], flat, AF.Silu, r=['cacc'], w=['sctmp'])
                act(qsq[:, :], sctmp[:, :], AF.Square, r=['sctmp'], w=['qsq'])
                for t2 in range(2):
                    bb = nextps()
                    mm(ps[bb][:, :], ones_bf[:, :], qsq[:, t2 * 512:(t2 + 1) * 512], r=['ones_bf', 'qsq'], w=[psk[bb]])
                    act(rinv[:, t2 * 512:(t2 + 1) * 512], ps[bb][:, :], AF.Sqrt, r=[psk[bb], 'epsb'],
                        w=['rinv'], bias=epsb[:, 1:2])
                rcp(rinv[:, :], rinv[:, :], r=['rinv'], w=['rinv'])
                dst, key = (qT, 'qT') if ch < 4 else (kT, 'kT')
                scl = (128.0 ** -0.5) if ch < 4 else 1.0
                stt(dst[:, h, :], sctmp[:, :], scl, rinv[:, :], ALU.mult, ALU.mult,
                    r=['sctmp', 'rinv'], w=[(key, h)])
            linear(evin_d, 8, [(i * 512, 512) for i in range(3)], hT, 'hT', c_qkv)

            if SUB < 2:
                return
            def c_z(c0, cw, th, b):
                h = (c0 - 1536) // 128
                act(mixT[:, h, th * 512:(th + 1) * 512], ps[b][:, :], AF.Silu, r=[psk[b]], w=[('mixT', h)])
            linear(evin_d, 8, [(1536, 512)], hT, 'hT', c_z)

            if SUB < 3:
                return
            def c_beta(c0, cw, th, b):
                act(betaT[:, th * 512:(th + 1) * 512], ps[b][0:8, :], AF.Sigmoid, r=[psk[b]], w=['betaT'])

            def c_a(c0, cw, th, b):
                tsl = slice(th * 512, (th + 1) * 512)
                act(rinv[0:8, tsl], ps[b][0:8, :], AF.Exp, r=[psk[b], 'pp'], w=['rinv'], bias=ppc('dt_bias')[0:8, :])
                act(rinv[0:8, tsl], rinv[0:8, tsl], AF.Ln, r=['rinv'], w=['rinv'], bias=1.0)
                ts(gT[:, tsl], rinv[0:8, tsl], negA[:, 0:1], ALU.mult, r=['rinv', 'negA'], w=['gT'])
            linear(evin_d, 8, [(2048, 8)], hT, 'hT', c_beta)
            linear(evin_d, 8, [(2056, 8)], hT, 'hT', c_a)

            if SUB < 4:
                return
            def mk_sc(j):
                def c_c(c0, cw, th, b):
                    cp(sctmp[:, th * 512:(th + 1) * 512], ps[b][:, :], r=[psk[b]], w=['sctmp'], eng='act')

                def c_h(c0, cw, th, b):
                    pi = j % 2
                    tt(pad_in(pi, th), ps3(b), sctmp[:, th * 512:(th + 1) * 512].rearrange("p (s t) -> p s t", s=2),
                       ALU.mult, r=[psk[b], 'sctmp'], w=[('pad', pi)])
                    if th == 1:
                        conv3(pi, 'sc_conv', 4, j)

                def c_b(c0, cw, th, b):
                    pi = j % 2
                    tt(mixT[:, 4 + j, th * 512:(th + 1) * 512].rearrange("p (s t) -> p s t", s=2), ps3(b),
                       cacc[pi][:, 2 * th:2 * th + 2, :], ALU.mult, r=[psk[b], 'cacc'], w=[('mixT', 4 + j)])
                return c_c, c_h, c_b
            import os
            SCJ = int(os.environ.get('SCJ', '4'))
            SCP = int(os.environ.get('SCP', '3'))
            for j in range(SCJ):
                c_c, c_h, c_b = mk_sc(j)
                linear(evin_d, 8, [(2064 + 512 + j * 128, 128)], hT, 'hT', c_c)
                if SCP >= 2:
                    linear(evin_d, 8, [(2064 + 1024 + j * 128, 128)], hT, 'hT', c_h)
                if SCP >= 3:
                    linear(evin_d, 8, [(2064 + j * 128, 128)], hT, 'hT', c_b)

            if STAGE < 2:
                return
            def T2(n, s, d=F32):
                t_ = T(n, s, d)
                return [t_, t_]
            gbtok = T2("gbtok", [64, 16])
            gcc = T2("gcc", [64, 4])
            Dgb = T2("Dgb", [64, 2, 4, 64])
            Dm = T2("Dm", [64, 4, 64])
            t1 = T2("t1", [64, 4, 64])
            t2_ = T2("t2", [64, 4, 64])
            E1 = T2("E1", [64, 4, 64])
            E2 = T2("E2", [64, 4, 64])
            decIT = T2("decIT", [64, 4, 64])
            Nn = [T2("Nn%d" % k, [64, 4, 64]) for k in range(2)]
            NTn = [T2("NTn%d" % k, [64, 4, 64]) for k in range(2)]
            TT = T2("TT", [64, 4, 64])
            TTb = T2("TTb", [64, 4, 64], BF16)
            intraT = T2("intraT", [64, 4, 64], BF16)
            ktok = T2("ktok", [64, 4, 128], BF16)
            vtok = T2("vtok", [64, 4, 128], BF16)
            vb = T2("vb", [64, 4, 128], BF16)
            kbg = T2("kbg", [64, 4, 128], BF16)
            kd = T2("kd", [64, 4, 128], BF16)
            bg = T2("bg", [64, 4])
            ekd = T2("ekd", [64, 4])
            egr = T2("egr", [128, 4, 64])
            qdT = T2("qdT", [128, 4, 64], BF16)
            u_sb = T2("u_sb", [64, 4, 128])
            wTb = T2("wTb", [128, 4, 64], BF16)
            e_b = T2("e_b", [64, 4, 128], BF16)

            def v3(ap, a):
                return ap.rearrange("p (a b) -> p a b", a=a)

            for d in range(2):
                mS = cst[0:64, C_SL:C_SL + 64] if d == 0 else cst[0:64, C_SU:C_SU + 64]
                mST = cst[0:64, C_SU:C_SU + 64] if d == 0 else cst[0:64, C_SL:C_SL + 64]
                mIT = cst[0:64, C_U:C_U + 64] if d == 0 else cst[0:64, C_L:C_L + 64]
                cum = cst[0:64, C_U:C_U + 64] if d == 0 else cst[0:64, C_L:C_L + 64]
                last = 63 if d == 0 else 0
                I64 = cst[0:64, C_ID:C_ID + 64]

                def bc1(ap2):
                    return ap2.unsqueeze(1).to_broadcast([64, 4, 64])

                def bc2(ap2, n=64):
                    return ap2.unsqueeze(2).to_broadcast([64, 4, n])
                dma('sp', Sf[:, :, :], s0_d[d].rearrange("h k v -> k h v"), w=['Sf'])
                cp(Sb[:, :, :], Sf[:, :, :], r=['Sf'], w=['Sb'], eng='act')
                order = list(range(16)) if d == 0 else list(range(15, -1, -1))
                GCH = int(os.environ.get('GCH', '16'))
                GP = int(os.environ.get('GP', '99'))
                for step, c in enumerate(order[:GCH]):
                    p = 0
                    K = lambda n: (n, p)
                    csl = slice(c * 64, (c + 1) * 64)
                    b0 = nextps()
                    tr(ps[b0][0:64, 0:8], gT[0:8, csl], ident_f[0:8, 0:8], r=['gT', 'cst'], w=[psk[b0]])
                    tr(ps[b0][0:64, 8:16], betaT[0:8, csl], ident_f[0:8, 0:8], r=['betaT', 'cst'], w=[psk[b0]])
                    cp(gbtok[p][:, :], ps[b0][0:64, 0:16], r=[psk[b0]], w=[K('gbtok')], eng='act')
                    gtok = gbtok[p][:, d * 4:d * 4 + 4]
                    btok = gbtok[p][:, 8 + d * 4:8 + d * 4 + 4]
                    b1 = nextps()
                    mm(ps[b1][0:64, 0:4], cum, gtok, r=['cst', K('gbtok')], w=[psk[b1]])
                    cp(gcc[p][:, :], ps[b1][0:64, 0:4], r=[psk[b1]], w=[K('gcc')], eng='act')
                    if GP < 1:
                        continue
                    tt(Dgb[p][:, 0, :, :], bc1(I64), bc2(gcc[p][:, :]), ALU.mult, r=['cst', K('gcc')], w=[K('Dgb')])
                    tt(Dgb[p][:, 1, :, :], bc1(I64), bc2(btok), ALU.mult, r=['cst', K('gbtok')], w=[K('Dgb')])
                    bR = nextps()
                    mm(ps[bR][:, 0:256], ones_f[0:64, 0:128], Dgb[p][:, 0, :, :].rearrange('p a b -> p (a b)'), r=['ones_f', K('Dgb')], w=[psk[bR]])
                    mm(ps[bR][0:64, 256:512], ones_f[0:64, 0:64], Dgb[p][:, 1, :, :].rearrange('p a b -> p (a b)'), r=['ones_f', K('Dgb')], w=[psk[bR]])
                    grow = v3(ps[bR][:, 0:256], 4)
                    brow = v3(ps[bR][0:64, 256:512], 4)
                    tt(Dm[p][:, :, :], bc2(gcc[p][:, :]), grow[0:64], ALU.subtract, r=[K('gcc'), psk[bR]], w=[K('Dm')])
                    ts(t1[p][:, :, :], Dm[p][:, :, :], 0.0, ALU.min, r=[K('Dm')], w=[K('t1')])
                    ts(t2_[p][:, :, :], Dm[p][:, :, :], -1.0, ALU.mult, r=[K('Dm')], w=[K('t2')], s2=0.0, op1=ALU.min)
                    act(E1[p][:, :, :], t1[p][:, :, :], AF.Exp, r=[K('t1')], w=[K('E1')])
                    act(E2[p][:, :, :], t2_[p][:, :, :], AF.Exp, r=[K('t2')], w=[K('E2')])
                    act(egr[p][:, :, :], grow, AF.Exp, r=[psk[bR]], w=[K('egr')])
                    tt(ekd[p][:, :], grow[0:64, :, last], gcc[p][:, :], ALU.subtract, r=[psk[bR], K('gcc')], w=[K('ekd')])
                    act(ekd[p][:, :], ekd[p][:, :], AF.Exp, r=[K('ekd')], w=[K('ekd')])
                    act(bg[p][:, :], gcc[p][:, :], AF.Exp, r=[K('gcc')], w=[K('bg')])
                    tt(bg[p][:, :], bg[p][:, :], btok, ALU.mult, r=[K('bg'), K('gbtok')], w=[K('bg')])
                    tt(decIT[p][:, :, :], E2[p][:, :, :], bc1(mIT), ALU.mult, r=[K('E2'), 'cst'], w=[K('decIT')])
                    tt(E1[p][:, :, :], E1[p][:, :, :], bc1(mS), ALU.mult, r=[K('E1'), 'cst'], w=[K('E1')])
                    tt(E2[p][:, :, :], E2[p][:, :, :], bc1(mST), ALU.mult, r=[K('E2'), 'cst'], w=[K('E2')])
                    if GP < 2:
                        continue
                    bK = nextps()
                    for h in range(4):
                        mm(ps[bK][0:64, h * 64:(h + 1) * 64], kT[:, h, csl], kT[:, h, csl], r=[('kT', h)], w=[psk[bK]])
                        mm(ps[bK][0:64, 256 + h * 64:256 + (h + 1) * 64], kT[:, h, csl], qT[:, h, csl],
                           r=[('kT', h), ('qT', h)], w=[psk[bK]])
                    pKK = v3(ps[bK][0:64, 0:256], 4)
                    pQK = v3(ps[bK][0:64, 256:512], 4)
                    N0, NT0 = Nn[0][p], NTn[0][p]
                    tt(t1[p][:, :, :], pKK, E1[p][:, :, :], ALU.mult, r=[psk[bK], K('E1')], w=[K('t1')])
                    stt(N0[:, :, :], t1[p][:, :, :], -1.0, bc2(btok), ALU.mult, ALU.mult,
                        r=[K('t1'), K('gbtok')], w=[K('Nn0')])
                    tt(t2_[p][:, :, :], pKK, E2[p][:, :, :], ALU.mult, r=[psk[bK], K('E2')], w=[K('t2')])
                    stt(NT0[:, :, :], t2_[p][:, :, :], -1.0, brow, ALU.mult, ALU.mult,
                        r=[K('t2'), psk[bR]], w=[K('NTn0')])
                    tt(TT[p][:, :, :], NT0[:, :, :], bc1(I64), ALU.add, r=[K('NTn0'), 'cst'], w=[K('TT')])
                    tt(intraT[p][:, :, :], pQK, decIT[p][:, :, :], ALU.mult, r=[psk[bK], K('decIT')], w=[K('intraT')])
                    if GP < 3:
                        continue
                    NLEV = int(os.environ.get('NLEV', '5'))
                    NPART = int(os.environ.get('NPART', '3'))
                    for lev in range(1, NLEV + 1):
                        a, bprev = lev % 2, (lev - 1) % 2
                        Np, NTp = Nn[bprev][p], NTn[bprev][p]
                        Nc, NTc = Nn[a][p], NTn[a][p]
                        bn = nextps()
                        for h in range(4):
                            mm(ps[bn][0:64, h * 64:(h + 1) * 64], NTp[:, h, :], Np[:, h, :],
                               r=[K('Nn%d' % bprev), K('NTn%d' % bprev)], w=[psk[bn]])
                        if lev < 5 and NPART >= 2:
                            for h in range(4):
                                mm(ps[bn][0:64, 256 + h * 64:256 + (h + 1) * 64], Np[:, h, :], NTp[:, h, :],
                                   r=[K('Nn%d' % bprev), K('NTn%d' % bprev)], w=[psk[bn]])
                        cp(Nc[:, :, :], v3(ps[bn][0:64, 0:256], 4), r=[psk[bn]], w=[K('Nn%d' % a)], eng='act')
                        if lev < 5 and NPART >= 2:
                            cp(NTc[:, :, :], v3(ps[bn][0:64, 256:512], 4), r=[psk[bn]], w=[K('NTn%d' % a)], eng='act')
                        if NPART < 3:
                            continue
                        bt = nextps()
                        for h in range(4):
                            mm(ps[bt][0:64, h * 64:(h + 1) * 64], Nc[:, h, :], TT[p][:, h, :],
                               r=[K('Nn%d' % a), K('TT')], w=[psk[bt]])
                        tt(TT[p][:, :, :], TT[p][:, :, :], v3(ps[bt][0:64, 0:256], 4), ALU.add,
                           r=[K('TT'), psk[bt]], w=[K('TT')])
                    cp(TTb[p][:, :, :], TT[p][:, :, :], r=[K('TT')], w=[K('TTb')], eng='act')
                    if GP < 4:
                        continue
                    bkv = nextps()
                    pkv = ps[bkv][:, :].bitcast(BF16)
                    for h in range(4):
                        tr(pkv[0:64, h * 128:(h + 1) * 128], kT[:, h, csl], ident_bf[:, :], r=[('kT', h), 'ident_bf'], w=[psk[bkv]])
                        tr(pkv[0:64, 512 + h * 128:512 + (h + 1) * 128], vT[:, h, csl], ident_bf[:, :],
                           r=[('vT', h), 'ident_bf'], w=[psk[bkv]])
                    cp(ktok[p][:, :, :], v3(pkv[0:64, 0:512], 4), r=[psk[bkv]], w=[K('ktok')], eng='act')
                    cp(vtok[p][:, :, :], v3(pkv[0:64, 512:1024], 4), r=[psk[bkv]], w=[K('vtok')], eng='act')
                    tt(vb[p][:, :, :], vtok[p][:, :, :], bc2(btok, 128), ALU.mult, r=[K('vtok'), K('gbtok')], w=[K('vb')])
                    tt(kbg[p][:, :, :], ktok[p][:, :, :], bc2(bg[p][:, :], 128), ALU.mult, r=[K('ktok'), K('bg')], w=[K('kbg')])
                    tt(kd[p][:, :, :], ktok[p][:, :, :], bc2(ekd[p][:, :], 128), ALU.mult, r=[K('ktok'), K('ekd')], w=[K('kd')])
                    tt(qdT[p][:, :, :], qT[:, :, csl], egr[p][:, :, :], ALU.mult,
                       r=[('qT', 0), ('qT', 1), ('qT', 2), ('qT', 3), K('egr')], w=[K('qdT')])
                    if GP < 5:
                        continue
                    bu = nextps()
                    bw = nextps()
                    for h in range(4):
                        mm(ps[bu][0:64, h * 128:(h + 1) * 128], TTb[p][:, h, :], vb[p][:, h, :],
                           r=[K('TTb'), K('vb')], w=[psk[bu]])
                        mm(ps[bw][:, h * 64:(h + 1) * 64], kbg[p][:, h, :], TTb[p][:, h, :],
                           r=[K('kbg'), K('TTb')], w=[psk[bw]])
                    cp(u_sb[p][:, :, :], v3(ps[bu][0:64, :], 4), r=[psk[bu]], w=[K('u_sb')], eng='act')
                    cp(wTb[p][:, :, :], v3(ps[bw][:, 0:256], 4), r=[psk[bw]], w=[K('wTb')], eng='act')
                    if GP < 6:
                        continue
                    be = nextps()
                    for h in range(4):
                        mm(ps[be][0:64, h * 128:(h + 1) * 128], wTb[p][:, h, :], Sb[:, h, :], r=[K('wTb'), 'Sb'], w=[psk[be]])
                    tt(e_b[p][:, :, :], u_sb[p][:, :, :], v3(ps[be][0:64, :], 4), ALU.subtract,
                       r=[K('u_sb'), psk[be]], w=[K('e_b')])
                    bo = nextps()
                    for h in range(4):
                        mm(ps[bo][:, h * 64:(h + 1) * 64], Sb[:, h, :], qdT[p][:, h, :], start=True, stop=False,
                           r=['Sb', K('qdT')], w=[psk[bo]])
                        mm(ps[bo][:, h * 64:(h + 1) * 64], e_b[p][:, h, :], intraT[p][:, h, :], start=False, stop=True,
                           r=[K('e_b'), K('intraT')], w=[psk[bo]])
                    po = v3(ps[bo][:, 0:256], 4)
                    if d == 0:
                        cp(oacc[:, :, csl], po, r=[psk[bo]], w=[('oacc', c)], eng='act')
                    else:
                        tt(oacc[:, :, csl], oacc[:, :, csl], po, ALU.add, r=[psk[bo], ('oacc', c)], w=[('oacc', c)])
                    bs = nextps()
                    for h in range(4):
                        mm(ps[bs][:, h * 128:(h + 1) * 128], kd[p][:, h, :], e_b[p][:, h, :], r=[K('kd'), K('e_b')], w=[psk[bs]])
                    for h in range(4):
                        stt(Sf[:, h, :], Sf[:, h, :], egr[p][:, h, last:last + 1], ps[bs][:, h * 128:(h + 1) * 128],
                            ALU.mult, ALU.add, r=['Sf', K('egr'), psk[bs]], w=['Sf'])
                    seg_end = (c % 4 == 3) if d == 0 else (c % 4 == 0)
                    if seg_end:
                        seg = c // 4
                        dma('sp', gso_d[d, seg].rearrange("h k v -> k h v"), Sf[:, :, :], r=['Sf'])
                        ts(Sf[:, :, :], Sf[:, :, :], flag[:, 0:1], ALU.mult, r=['Sf', 'flag'], w=['Sf'])
                    cp(Sb[:, :, :], Sf[:, :, :], r=['Sf'], w=['Sb'], eng='act')

            if STAGE < 3:
                return
            for h in range(4):
                act(qsq[:, :], oacc[:, h, :], AF.Square, r=[('oacc', c) for c in range(16)], w=['qsq'])
                for t2 in range(2):
                    bb = nextps()
                    tsl = slice(t2 * 512, (t2 + 1) * 512)
                    mm(ps[bb][:, :], ones_bf[:, :], qsq[:, tsl], r=['ones_bf', 'qsq'], w=[psk[bb]])
                    act(rinv[:, tsl], ps[bb][:, :], AF.Sqrt, r=[psk[bb], 'epsb'], w=['rinv'],
                        bias=epsb[:, 0:1], scale=1.0 / 128)
                rcp(rinv[:, :], rinv[:, :], r=['rinv'], w=['rinv'])
                stt(sctmp[:, :], oacc[:, h, :], ppc('gdn_norm')[:, 0:1], rinv[:, :], ALU.mult, ALU.mult,
                    r=[('oacc', c) for c in range(16)] + ['pp', 'rinv'], w=['sctmp'])
                tt(mixT[:, h, :], sctmp[:, :], mixT[:, h, :], ALU.mult, r=['sctmp', ('mixT', h)], w=[('mixT', h)])

            def c_out(c0, cw, th, b):
                fc = c0 // 128
                tsl = slice(th * 512, (th + 1) * 512)
                stt(x_sb[:, fc, tsl], ps[b][:, :], modv[:, 0, 16 + fc:17 + fc], x_sb[:, fc, tsl],
                    ALU.mult, ALU.add, r=[psk[b], 'modv', ('x', fc)], w=[('x', fc)])
            linear(evout_d, 8, [(i * 512, 512) for i in range(2)], mixT, 'mixT', c_out)

        def layer1_mixer():
            with contextlib.ExitStack() as so:
                mix1T = so.enter_context(nc.sbuf_tensor("sb_mix1T", [128, 8, NT], BF16))
                with contextlib.ExitStack() as sc:
                    alloc_work(sc, 'p3')
                    attention_phase(sc, mix1T)
                    S.flush()
                if L1PART >= 2:
                    with contextlib.ExitStack() as sr:
                        rwkv_phases(sr, mix1T)
                with contextlib.ExitStack() as sc:
                    alloc_work(sc, 'p6')

                    def c_out(c0, cw, th, b):
                        fc = c0 // 128
                        tsl = slice(th * 512, (th + 1) * 512)
                        stt(x_sb[:, fc, tsl], ps[b][:, :], modv[:, 1, 16 + fc:17 + fc], x_sb[:, fc, tsl],
                            ALU.mult, ALU.add, r=[psk[b], 'modv', ('x', fc)], w=[('x', fc)])
                    linear(odout_d, 8, [(i * 512, 512) for i in range(2)], mix1T, 'mix1T', c_out)
                    S.flush()

        def attention_phase(sc, mix1T):
            T = lambda n, s, d=F32: sc.enter_context(nc.sbuf_tensor('sb_a_' + n, list(s), d))
            qTr = T("qTr", [128, 4, NT], BF16)
            kpad = T("kpad", [128, 2, NT + 256], BF16)
            vTb = T("vTb", [128, NT], BF16)
            vtokp = T("vtokp", [128, 10, 128], BF16)
            ckT = T("ckT", [128, 2, 512], BF16)
            cvt = T("cvt", [128, 4, 128], BF16)
            cosT = T("cosT", [128, NT])
            sinT = T("sinT", [128, NT])
            Rm = T("Rm", [128, 128])
            xf = T("xf", [128, NT])
            xr = T("xr", [128, NT])
            mb = [T("mb%d" % i, [128, 896]) for i in range(2)]
            scs = T("scs", [128, 896])
            Pb = T("Pb", [128, 896], BF16)
            PT = T("PT", [128, 7, 128], BF16)
            atok = T("atok", [128, 512], BF16)
            sm = T("sm", [128, 8])

            dma('sp', cosT[:, :], ropec_d[:, :], w=['cosT'])
            dma('sp', sinT[:, :], ropes_d[:, :], w=['sinT'])
            dma('sp', Rm[:, :], rm_d[:, :], w=['Rm'])
            for kvh in range(2):
                dma('pool', ckT[:, kvh, :], ckT_d[kvh], w=['ckT'])
            dma('pool', cvt[:, :, :], cvt_d[:, :, :], w=['cvt'])
            mset(kpad[:, :, 0:128], 0.0, w=['kpad'])
            mset(kpad[:, :, 128 + NT:256 + NT], 0.0, w=['kpad'])
            mset(vtokp[:, 0, :], 0.0, w=['vtokp'])
            mset(vtokp[:, 9, :], 0.0, w=['vtokp'])

            norm_mod(gsA[:, 1, :], modv[:, 1, 0:8], hT, 'hT')

            def rope(dst_ap_fn, dkey):
                for t2 in range(2):
                    tsl = slice(t2 * 512, (t2 + 1) * 512)
                    bb = nextps()
                    mm(ps[bb][:, :], Rm[:, :], xf[:, tsl], r=['Rm', 'xf'], w=[psk[bb]])
                    tt(xr[:, tsl], ps[bb][:, :], sinT[:, tsl], ALU.mult, r=[psk[bb], 'sinT'], w=['xr'])
                    tt(xf[:, tsl], xf[:, tsl], cosT[:, tsl], ALU.mult, r=['xf', 'cosT'], w=['xf'])
                    tt(dst_ap_fn(tsl), xf[:, tsl], xr[:, tsl], ALU.add, r=['xf', 'xr'], w=[dkey])

            def c_q(c0, cw, th, b):
                a = c0 // 128
                cp(xf[:, th * 512:(th + 1) * 512], ps[b][:, :], r=[psk[b]], w=['xf'], eng='act')
                if th == 1:
                    rope(lambda tsl: qTr[:, a, tsl], 'qTr')
            linear(odin_d, 8, [(0, 512)], hT, 'hT', c_q)

            def c_k(c0, cw, th, b):
                kvh = (c0 - 512) // 128
                cp(xf[:, th * 512:(th + 1) * 512], ps[b][:, :], r=[psk[b]], w=['xf'], eng='act')
                if th == 1:
                    dma('sp', kvo_d[0, kvh * 64:(kvh + 1) * 64, :], xf[0:64, :], r=['xf'])
                    rope(lambda tsl: kpad[:, kvh, 128 + tsl.start:128 + tsl.stop], 'kpad')
            linear(odin_d, 8, [(512, 256)], hT, 'hT', c_k)

            def c_v(c0, cw, th, b):
                cp(xf[:, th * 512:(th + 1) * 512], ps[b][:, :], r=[psk[b]], w=['xf'], eng='act')
                if th == 1:
                    dma('sp', kvo_d[1, :, :], xf[:, :], r=['xf'])
                    cp(vTb[:, :], xf[:, :], r=['xf'], w=['vTb'], eng='act')
                    for blk in range(8):
                        bb = nextps()
                        pv_ = ps[bb][:, :].bitcast(BF16)
                        tr(pv_[:, 0:128], vTb[:, blk * 128:(blk + 1) * 128], ident_bf[:, :], r=['vTb', 'ident_bf'], w=[psk[bb]])
                        cp(vtokp[:, 1 + blk, :], pv_[:, 0:128], r=[psk[bb]], w=['vtokp'], eng='act')
            linear(odin_d, 8, [(768, 128)], hT, 'hT', c_v)

            sink = ppc('sink')
            for n in range(8):
                mbi = n % 2
                dma('sp', mb[mbi][:, :], maskb_d[n], w=[('mb', mbi)])
                for h in range(8):
                    kvh = h // 4
                    hp = h % 2
                    a = h // 2
                    pb_ = slice(hp * 64, (hp + 1) * 64)
                    qap = qTr[pb_, a, n * 128:(n + 1) * 128]
                    b1 = nextps()
                    b2 = nextps()
                    mm(ps[b1][:, 0:384], qap, kpad[pb_, kvh, n * 128:n * 128 + 384], r=['qTr', 'kpad'], w=[psk[b1]])
                    mm(ps[b2][:, :], qap, ckT[pb_, kvh, :], r=['qTr', 'ckT'], w=[psk[b2]])
                    stt(scs[:, 0:384], ps[b1][:, 0:384], 0.125, mb[mbi][:, 0:384], ALU.mult, ALU.add,
                        r=[psk[b1], ('mb', mbi)], w=['scs'])
                    stt(scs[:, 384:896], ps[b2][:, :], 0.125, mb[mbi][:, 384:896], ALU.mult, ALU.add,
                        r=[psk[b2], ('mb', mbi)], w=['scs'])
                    S.op('dve', lambda e: e.reduce_max(sm[:, 0:1], scs[:, :], mybir.AxisListType.X), ['scs'], ['sm'])
                    tt(sm[:, 1:2], sm[:, 0:1], sink[:, h:h + 1], ALU.max, r=['sm', 'pp'], w=['sm'])
                    ts(sm[:, 2:3], sm[:, 1:2], -1.0, ALU.mult, r=['sm'], w=['sm'])
                    S.op('act', lambda e: e.activation(Pb[:, :], scs[:, :], AF.Exp, bias=sm[:, 2:3], accum_out=sm[:, 3:4]),
                         ['scs', 'sm'], ['Pb', 'sm'])
                    act(sm[:, 4:5], sink[:, h:h + 1], AF.Exp, r=['pp', 'sm'], w=['sm'], bias=sm[:, 2:3])
                    tt(sm[:, 5:6], sm[:, 3:4], sm[:, 4:5], ALU.add, r=['sm'], w=['sm'])
                    rcp(sm[:, 6:7], sm[:, 5:6], r=['sm'], w=['sm'])
                    bt_ = nextps()
                    ptp = ps[bt_][:, :].bitcast(BF16)
                    for kb in range(7):
                        tr(ptp[:, kb * 128:(kb + 1) * 128], Pb[:, kb * 128:(kb + 1) * 128], ident_bf[:, :],
                           r=['Pb', 'ident_bf'], w=[psk[bt_]])
                    cp(PT[:, :, :], ptp[:, 0:896].rearrange("p (a b) -> p a b", a=7), r=[psk[bt_]], w=['PT'], eng='act')
                    bo_ = nextps()
                    for kb in range(7):
                        if kb < 3:
                            vap = vtokp[:, n + kb, kvh * 64:(kvh + 1) * 64]
                            rk = 'vtokp'
                        else:
                            vap = cvt[:, kb - 3, kvh * 64:(kvh + 1) * 64]
                            rk = 'cvt'
                        mm(ps[bo_][:, 0:64], PT[:, kb, :], vap, start=(kb == 0), stop=(kb == 6), r=['PT', rk], w=[psk[bo_]])
                    act(atok[:, h * 64:(h + 1) * 64], ps[bo_][:, 0:64], AF.Identity, r=[psk[bo_], 'sm'], w=['atok'],
                        scale=sm[:, 6:7])
                ba_ = nextps()
                pa_ = ps[ba_][:, :].bitcast(BF16)
                for a in range(4):
                    tr(pa_[:, a * 128:(a + 1) * 128], atok[:, a * 128:(a + 1) * 128], ident_bf[:, :],
                       r=['atok', 'ident_bf'], w=[psk[ba_]])
                cp(mix1T[:, 0:4, n * 128:(n + 1) * 128], pa_[:, 0:512].rearrange("p (a b) -> p a b", a=4),
                   r=[psk[ba_]], w=[('mix1T', 0), ('mix1T', 1), ('mix1T', 2), ('mix1T', 3)], eng='act')

        def rwkv_phases(sr, mix1T):
            TR = lambda n, s, d=F32: sr.enter_context(nc.sbuf_tensor('sb_r_' + n, list(s), d))
            r_b = TR("r_b", [128, 4, NT], BF16)
            k_b = TR("k_b", [128, 4, NT], BF16)
            kk_b = TR("kk_b", [128, 4, NT], BF16)
            v_b = TR("v_b", [128, 4, NT], BF16)
            a_b = TR("a_b", [128, 2, 4, NT], BF16)
            wlt = TR("wlt", [128, NT], BF16)
            bon = TR("bon", [128, 4, NT], BF16)
            gate = TR("gate", [128, 4, NT], BF16)
            wup = TR("wup", [128, 512], BF16)
            bones = TR("bones", [128, 128], BF16)
            wmid = TR("wmid", [128, 15])
            dma('pool', wup[:, :], wup_d[:, :], w=['wup'])
            dma('pool', bones[:, :], bones_d[:, :], w=['bones'])
            mu0 = ppc('mu', 0, 15)
            mu1 = ppc('mu', 15, 15)
            tt(wmid[:, :], mu0, mu1, ALU.add, r=['pp'], w=['wmid'])
            ts(wmid[:, :], wmid[:, :], -1.0, ALU.mult, r=['wmid'], w=['wmid'], s2=1.0, op1=ALU.add)

            with contextlib.ExitStack() as sc:
                alloc_work(sc, 'p4')
                T = lambda n, s, d=F32: sc.enter_context(nc.sbuf_tensor('sb_rp_' + n, list(s), d))
                pad = [T("pad%d" % i, [128, 4, 258]) for i in range(2)]
                xs_ = T("xs", [128, 4, 256])
                t1 = T("t1", [128, NT])
                t2 = T("t2", [128, NT])
                sqh = T("sqh", [128, NT], BF16)
                alb = T("alb", [128, NT], BF16)
                glb = T("glb", [128, NT], BF16)
                aup = T("aup", [128, 512], BF16)
                gup = T("gup", [128, 512], BF16)
                dma('pool', aup[:, :], aup_d[:, :], w=['aup'])
                dma('pool', gup[:, :], gup_d[:, :], w=['gup'])
                for i in range(2):
                    mset(pad[i][:, :, :], 0.0, w=[('rpad', i)])
                norm_mod(gsA[:, 1, :], modv[:, 1, 0:8], hT, 'hT')
                xsf = xs_[:, :, :].rearrange("p s t -> p (s t)")

                def hsum(dst_ps_fn, src_bf):
                    for t2_ in range(2):
                        bb = nextps()
                        mm(ps[bb][:, :], bones[:, :], src_bf[:, t2_ * 512:(t2_ + 1) * 512], r=['bones', 'sqh'], w=[psk[bb]])
                        dst_ps_fn(t2_, bb)

                def c_rw(c0, cw, th, b):
                    j = (c0 - 896) // 128
                    pi = j % 2
                    p = pad[pi]
                    cp(p[:, 2 * th:2 * th + 2, 1:257], ps[b][:, :].rearrange("p (s t) -> p s t", s=2),
                       r=[psk[b]], w=[('rpad', pi)], eng='act')
                    if th == 0:
                        return
                    ts(p[:, 1:4, 0:1], p[:, 0:3, 256:257], flag[:, 0:1], ALU.mult, r=[('rpad', pi), 'flag'], w=[('rpad', pi)])
                    ts(p[:, 0:3, 257:258], p[:, 1:4, 1:2], flag[:, 0:1], ALU.mult, r=[('rpad', pi), 'flag'], w=[('rpad', pi)])
                    ts(xs_[:, :, :], p[:, :, 0:256], mu0[:, j:j + 1], ALU.mult, r=[('rpad', pi), 'pp'], w=['xs'])
                    stt(xs_[:, :, :], p[:, :, 1:257], wmid[:, j:j + 1], xs_[:, :, :], ALU.mult, ALU.add,
                        r=[('rpad', pi), 'wmid', 'xs'], w=['xs'])
                    stt(xs_[:, :, :], p[:, :, 2:258], mu1[:, j:j + 1], xs_[:, :, :], ALU.mult, ALU.add,
                        r=[('rpad', pi), 'pp', 'xs'], w=['xs'])
                    cq = j % 4
                    if j < 4:
                        cp(r_b[:, cq, :], xsf, r=['xs'], w=[('r_b', cq)], eng='act')
                    elif j < 8:
                        cp(k_b[:, cq, :], xsf, r=['xs'], w=[('k_b', cq)], eng='act')
                        ts(t1[:, :], xsf, ppc('k_k')[:, cq:cq + 1], ALU.mult, r=['xs', 'pp'], w=['t1'])
                        act(sqh[:, :], t1[:, :], AF.Square, r=['t1'], w=['sqh'])

                        def d1(t2_, bb):
                            act(t2[:, t2_ * 512:(t2_ + 1) * 512], ps[bb][:, :], AF.Sqrt, r=[psk[bb], 'epsb'], w=['t2'],
                                bias=epsb[:, 1:2])
                        hsum(d1, sqh)
                        rcp(t2[:, :], t2[:, :], r=['t2'], w=['t2'])
                        tt(kk_b[:, cq, :], t1[:, :], t2[:, :], ALU.mult, r=['t1', 't2'], w=[('kk_b', cq)])
                    elif j < 12:
                        cp(v_b[:, cq, :], xsf, r=['xs'], w=[('v_b', cq)], eng='act')
                    elif j == 12:
                        act(wlt[:, :], xsf, AF.Tanh, r=['xs'], w=['wlt'])
                    elif j == 13:
                        cp(alb[:, :], xsf, r=['xs'], w=['alb'], eng='act')
                        for d in range(2):
                            dsl = slice(d * 64, (d + 1) * 64)
                            for cq2 in range(4):
                                for t2_ in range(2):
                                    bb = nextps()
                                    mm(ps[bb][:, :], aup[dsl, cq2 * 128:(cq2 + 1) * 128], alb[dsl, t2_ * 512:(t2_ + 1) * 512],
                                       r=['aup', 'alb'], w=[psk[bb]])
                                    act(a_b[:, d, cq2, t2_ * 512:(t2_ + 1) * 512], ps[bb][:, :], AF.Sigmoid,
                                        r=[psk[bb], 'pp'], w=[('a_b', d, cq2)], bias=ppc('a0')[:, d * 4 + cq2:d * 4 + cq2 + 1])
                    else:
                        act(glb[:, :], xsf, AF.Sigmoid, r=['xs'], w=['glb'])
                        for cq2 in range(4):
                            for t2_ in range(2):
                                bb = nextps()
                                mm(ps[bb][:, :], gup[:, cq2 * 128:(cq2 + 1) * 128], glb[:, t2_ * 512:(t2_ + 1) * 512],
                                   r=['gup', 'glb'], w=[psk[bb]])
                                cp(gate[:, cq2, t2_ * 512:(t2_ + 1) * 512], ps[bb][:, :], r=[psk[bb]], w=[('gate', cq2)], eng='act')
                linear(odin_d, 8, [(896 + j * 128, 128) for j in range(15)], hT, 'hT', c_rw)
                for cq in range(4):
                    for d in range(2):
                        ts(t1[:, :], a_b[:, d, cq, :], -1.0, ALU.add, r=[('a_b', d, cq), 'pp'], w=['t1'],
                           s2=ppc('k_a')[:, cq:cq + 1], op1=ALU.mult)
                        stt(t2[:, :] if d == 0 else t1[:, :], t1[:, :], 1.0, k_b[:, cq, :], ALU.add, ALU.mult,
                            r=['t1', ('k_b', cq)], w=['t2' if d == 0 else 't1'])
                    tt(t2[:, :], t2[:, :], t1[:, :], ALU.add, r=['t1', 't2'], w=['t2'])
                    stt(sqh[:, :], t2[:, :], ppc('r_k')[:, cq:cq + 1], r_b[:, cq, :], ALU.mult, ALU.mult,
                        r=['t2', 'pp', ('r_b', cq)], w=['sqh'])

                    def d2(t2_, bb):
                        tsl = slice(t2_ * 512, (t2_ + 1) * 512)
                        tt(bon[:, cq, tsl], ps[bb][:, :], v_b[:, cq, tsl], ALU.mult, r=[psk[bb], ('v_b', cq)], w=[('bon', cq)])
                    hsum(d2, sqh)
                S.flush()

            if L1PART < 3:
                return
            with contextlib.ExitStack() as sc:
                T = lambda n, s, d=F32: sc.enter_context(nc.sbuf_tensor('sb_rs_' + n, list(s), d))
                lw = T("lw", [128, 4, NT])
                yacc = T("yacc", [128, 4, NT])
                Pf = T("Pf", [128, 4, 64])
                Pbf = T("Pbf", [128, 4, 64], BF16)
                Lf = T("Lf", [128, 4, 64])
                Lam = T("Lam", [128, 4, 64])
                e1 = T("e1", [128, 4, 64])
                e2 = T("e2", [128, 4, 64])
                e3 = T("e3", [128, 4, 64])
                e4 = T("e4", [128, 4, 64])
                eTot = T("eTot", [128, 4])
                ka = T("ka", [128, 4, 64])
                k2 = T("k2", [128, 4, 64])
                rt = T("rt", [128, 4, 64], BF16)
                bt = T("bt", [128, 4, 64], BF16)
                at = T("at", [128, 4, 64], BF16)
                kt = T("kt", [128, 4, 64], BF16)
                aG = T("aG", [128, 4, 64], BF16)
                kG = T("kG", [128, 4, 64], BF16)
                Nn = [T("Nn%d" % i, [64, 8, 64]) for i in range(2)]
                NTn = [T("NTn%d" % i, [64, 8, 64]) for i in range(2)]
                TT = T("TT", [64, 8, 64])
                TTb = T("TTb", [64, 8, 64], BF16)
                Tn = T("Tn", [64, 8, 64])
                O32a = T("O32a", [64, 8, 64])
                O32Ta = T("O32Ta", [64, 8, 64])
                O64a = T("O64a", [64, 8, 64])
                BTm = T("BTm", [64, 8, 64], BF16)
                RaT = T("RaT", [64, 8, 64], BF16)
                RkT = T("RkT", [64, 8, 64], BF16)
                btok = T("btok", [64, 4, 128], BF16)
                aGtok = T("aGtok", [64, 4, 128], BF16)
                kGtok = T("kGtok", [64, 4, 128], BF16)
                vtok = T("vtok", [64, 4, 128], BF16)
                BVb = T("BVb", [64, 8, 64], BF16)
                U0 = T("U0", [64, 8, 64])
                Ub = T("Ub", [64, 8, 64], BF16)
                WmTb = T("WmTb", [128, 4, 64], BF16)
                onesc = T("onesc", [128, 64])
                mset(onesc[:, :], 1.0, w=['onesc'])
                I64 = cst[0:64, C_ID:C_ID + 64]

                def v3(ap, a):
                    return ap.rearrange("p (a b) -> p a b", a=a)

                def bc8(ap2):
                    return ap2.unsqueeze(1).to_broadcast([64, 8, 64])

                for d in range(2):
                    dsl = slice(d * 64, (d + 1) * 64)
                    mS = cst[0:64, C_SL:C_SL + 64] if d == 0 else cst[0:64, C_SU:C_SU + 64]
                    mST = cst[0:64, C_SU:C_SU + 64] if d == 0 else cst[0:64, C_SL:C_SL + 64]
                    mIT = cst[0:64, C_U:C_U + 64] if d == 0 else cst[0:64, C_L:C_L + 64]
                    for cq in range(4):
                        for t2_ in range(2):
                            bb = nextps()
                            mm(ps[bb][:, :], wup[dsl, cq * 128:(cq + 1) * 128], wlt[dsl, t2_ * 512:(t2_ + 1) * 512],
                               r=['wup', 'wlt'], w=[psk[bb]])
                            act(lw[:, cq, t2_ * 512:(t2_ + 1) * 512], ps[bb][:, :], AF.Sigmoid, r=[psk[bb], 'pp'], w=['lw'],
                                bias=ppc('w0')[:, d * 4 + cq:d * 4 + cq + 1])
                    ts(lw[:, :, :], lw[:, :, :], -0.6065306597126334, ALU.mult, r=['lw'], w=['lw'])
                    dma('sp', Pf[:, :, :], rs0_d[d], w=['Pf'])
                    cp(Pbf[:, :, :], Pf[:, :, :], r=['Pf'], w=['Pbf'], eng='act')
                    order = list(range(16)) if d == 0 else list(range(15, -1, -1))
                    RCH = int(os.environ.get('RCH', '16'))
                    for c in order[:RCH]:
                        csl = slice(c * 64, (c + 1) * 64)
                        lwc = lw[:, :, csl]
                        for cq in range(4):
                            scan(Lf[:, cq, :], onesc[:, :], lw[:, cq, csl], r=['lw', 'onesc'], w=['Lf'])
                        tot = Lf[:, :, 63:64]
                        if d == 0:
                            LamT, lk = Lf, 'Lf'
                        else:
                            tt(Lam[:, :, :], tot.to_broadcast([128, 4, 64]), Lf[:, :, :], ALU.subtract, r=['Lf'], w=['Lam'])
                            tt(Lam[:, :, :], Lam[:, :, :], lwc, ALU.add, r=['Lam', 'lw'], w=['Lam'])
                            LamT, lk = Lam, 'Lam'
                        act(e1[:, :, :], LamT[:, :, :], AF.Exp, r=[lk], w=['e1'])
                        tt(e2[:, :, :], LamT[:, :, :], lwc, ALU.subtract, r=[lk, 'lw'], w=['e2'])
                        act(e2[:, :, :], e2[:, :, :], AF.Exp, r=['e2'], w=['e2'])
                        act(e3[:, :, :], LamT[:, :, :], AF.Exp, r=[lk], w=['e3'], scale=-1.0)
                        tt(e4[:, :, :], tot.to_broadcast([128, 4, 64]), LamT[:, :, :], ALU.subtract, r=['Lf', lk], w=['e4'])
                        act(e4[:, :, :], e4[:, :, :], AF.Exp, r=['e4'], w=['e4'])
                        act(eTot[:, :], Lf[:, :, 63], AF.Exp, r=['Lf'], w=['eTot'])
                        stt(ka[:, :, :], kk_b[:, :, csl], -1.0, a_b[:, d, :, csl], ALU.mult, ALU.mult,
                            r=[('kk_b', q_) for q_ in range(4)] + [('a_b', d, q_) for q_ in range(4)], w=['ka'])
                        for cq in range(4):
                            ts(k2[:, cq, :], a_b[:, d, cq, csl], -1.0, ALU.add, r=[('a_b', d, cq), 'pp'], w=['k2'],
                               s2=ppc('k_a')[:, cq:cq + 1], op1=ALU.mult)
                        stt(k2[:, :, :], k2[:, :, :], 1.0, k_b[:, :, csl], ALU.add, ALU.mult,
                            r=['k2'] + [('k_b', q_) for q_ in range(4)], w=['k2'])
                        tt(rt[:, :, :], r_b[:, :, csl], e1[:, :, :], ALU.mult, r=[('r_b', q_) for q_ in range(4)] + ['e1'], w=['rt'])
                        tt(bt[:, :, :], kk_b[:, :, csl], e2[:, :, :], ALU.mult, r=[('kk_b', q_) for q_ in range(4)] + ['e2'], w=['bt'])
                        tt(at[:, :, :], ka[:, :, :], e3[:, :, :], ALU.mult, r=['ka', 'e3'], w=['at'])
                        tt(kt[:, :, :], k2[:, :, :], e3[:, :, :], ALU.mult, r=['k2', 'e3'], w=['kt'])
                        tt(aG[:, :, :], ka[:, :, :], e4[:, :, :], ALU.mult, r=['ka', 'e4'], w=['aG'])
                        tt(kG[:, :, :], k2[:, :, :], e4[:, :, :], ALU.mult, r=['k2', 'e4'], w=['kG'])
                        RP = int(os.environ.get('RP', '9'))
                        if RP < 2:
                            continue
                        bA, bAT, bBT, bRa, bRk = nextps(), nextps(), nextps(), nextps(), nextps()
                        for h in range(8):
                            cq, hp = h // 2, h % 2
                            pb_ = slice(hp * 64, (hp + 1) * 64)
                            hs = slice(h * 64, (h + 1) * 64)
                            mm(ps[bA][0:64, hs], bt[pb_, cq, :], at[pb_, cq, :], r=['bt', 'at'], w=[psk[bA]])
                            mm(ps[bAT][0:64, hs], at[pb_, cq, :], bt[pb_, cq, :], r=['bt', 'at'], w=[psk[bAT]])
                            mm(ps[bBT][0:64, hs], kt[pb_, cq, :], bt[pb_, cq, :], r=['bt', 'kt'], w=[psk[bBT]])
                            mm(ps[bRa][0:64, hs], at[pb_, cq, :], rt[pb_, cq, :], r=['rt', 'at'], w=[psk[bRa]])
                            mm(ps[bRk][0:64, hs], kt[pb_, cq, :], rt[pb_, cq, :], r=['rt', 'kt'], w=[psk[bRk]])
                        cm = lambda off: cst[0:64, off:off + 64]
                        if d == 0:
                            m16, m16T, m32, m32T, m64 = cm(C_S16L), cm(C_S16U), cm(C_O32L), cm(C_O32U), cm(C_O64L)
                        else:
                            m16, m16T, m32, m32T, m64 = cm(C_S16U), cm(C_S16L), cm(C_O32U), cm(C_O32L), cm(C_O64U)
                        pAv = v3(ps[bA][0:64, :], 8)
                        pATv = v3(ps[bAT][0:64, :], 8)
                        tt(Nn[0][:, :, :], pAv, bc8(m16), ALU.mult, r=[psk[bA], 'cst'], w=['rNn0'])
                        tt(NTn[0][:, :, :], pATv, bc8(m16T), ALU.mult, r=[psk[bAT], 'cst'], w=['rNTn0'])
                        tt(O32a[:, :, :], pAv, bc8(m32), ALU.mult, r=[psk[bA], 'cst'], w=['O32a'])
                        tt(O32Ta[:, :, :], pATv, bc8(m32T), ALU.mult, r=[psk[bAT], 'cst'], w=['O32Ta'])
                        tt(O64a[:, :, :], pAv, bc8(m64), ALU.mult, r=[psk[bA], 'cst'], w=['O64a'])
                        tt(TT[:, :, :], NTn[0][:, :, :], bc8(I64), ALU.add, r=['rNTn0', 'cst'], w=['rTT'])
                        tt(Tn[:, :, :], Nn[0][:, :, :], bc8(I64), ALU.add, r=['rNn0', 'cst'], w=['rTn'])
                        tt(BTm[:, :, :], v3(ps[bBT][0:64, :], 8), bc8(mST), ALU.mult, r=[psk[bBT], 'cst'], w=['BTm'])
                        tt(RaT[:, :, :], v3(ps[bRa][0:64, :], 8), bc8(mIT), ALU.mult, r=[psk[bRa], 'cst'], w=['RaT'])
                        tt(RkT[:, :, :], v3(ps[bRk][0:64, :], 8), bc8(mIT), ALU.mult, r=[psk[bRk], 'cst'], w=['RkT'])
                        if RP < 3:
                            continue

                        def mm8(bank, L_, R_, rk):
                            for h in range(8):
                                mm(ps[bank][0:64, h * 64:(h + 1) * 64], L_[:, h, :], R_[:, h, :], r=rk, w=[psk[bank]])
                        for lev in range(1, 4):
                            a_, bp_ = lev % 2, (lev - 1) % 2
                            Np, NTp, Nc, NTc = Nn[bp_], NTn[bp_], Nn[a_], NTn[a_]
                            kp = ['rNn%d' % bp_, 'rNTn%d' % bp_]
                            bn = nextps()
                            mm8(bn, NTp, Np, kp)
                            cp(Nc[:, :, :], v3(ps[bn][0:64, :], 8), r=[psk[bn]], w=['rNn%d' % a_], eng='act')
                            bn2 = nextps()
                            mm8(bn2, Np, NTp, kp)
                            cp(NTc[:, :, :], v3(ps[bn2][0:64, :], 8), r=[psk[bn2]], w=['rNTn%d' % a_], eng='act')
                            b1_ = nextps()
                            mm8(b1_, Nc, TT, ['rNn%d' % a_, 'rTT'])
                            b2_ = nextps()
                            mm8(b2_, NTc, Tn, ['rNTn%d' % a_, 'rTn'])
                            tt(TT[:, :, :], TT[:, :, :], v3(ps[b1_][0:64, :], 8), ALU.add, r=['rTT', psk[b1_]], w=['rTT'])
                            tt(Tn[:, :, :], Tn[:, :, :], v3(ps[b2_][0:64, :], 8), ALU.add, r=['rTn', psk[b2_]], w=['rTn'])
                        Xs, X2s = Nn[0], NTn[0]
                        bx = nextps()
                        mm8(bx, O32a, TT, ['O32a', 'rTT'])
                        cp(Xs[:, :, :], v3(ps[bx][0:64, :], 8), r=[psk[bx]], w=['rNn0'], eng='act')
                        bx2 = nextps()
                        mm8(bx2, O32Ta, Tn, ['O32Ta', 'rTn'])
                        cp(X2s[:, :, :], v3(ps[bx2][0:64, :], 8), r=[psk[bx2]], w=['rNTn0'], eng='act')
                        by1 = nextps()
                        mm8(by1, Tn, Xs, ['rTn', 'rNn0'])
                        by2 = nextps()
                        mm8(by2, TT, X2s, ['rTT', 'rNTn0'])
                        tt(TT[:, :, :], TT[:, :, :], v3(ps[by1][0:64, :], 8), ALU.add, r=['rTT', psk[by1]], w=['rTT'])
                        tt(Tn[:, :, :], Tn[:, :, :], v3(ps[by2][0:64, :], 8), ALU.add, r=['rTn', psk[by2]], w=['rTn'])
                        bx = nextps()
                        mm8(bx, O64a, TT, ['O64a', 'rTT'])
                        cp(Xs[:, :, :], v3(ps[bx][0:64, :], 8), r=[psk[bx]], w=['rNn0'], eng='act')
                        by1 = nextps()
                        mm8(by1, Tn, Xs, ['rTn', 'rNn0'])
                        tt(TT[:, :, :], TT[:, :, :], v3(ps[by1][0:64, :], 8), ALU.add, r=['rTT', psk[by1]], w=['rTT'])
                        cp(TTb[:, :, :], TT[:, :, :], r=['rTT'], w=['TTb'], eng='act')
                        if RP < 4:
                            continue
                        for (src, skey, dst, dkey) in ((bt, 'bt', btok, 'btok'), (aG, 'aG', aGtok, 'aGtok'),
                                                       (kG, 'kG', kGtok, 'kGtok'), (None, None, vtok, 'rvtok')):
                            bb = nextps()
                            pq = ps[bb][:, :].bitcast(BF16)
                            for cq in range(4):
                                if src is None:
                                    tr(pq[0:64, cq * 128:(cq + 1) * 128], v_b[:, cq, csl], ident_bf[:, :],
                                       r=[('v_b', cq), 'ident_bf'], w=[psk[bb]])
                                else:
                                    tr(pq[0:64, cq * 128:(cq + 1) * 128], src[:, cq, :], ident_bf[:, :],
                                       r=[skey, 'ident_bf'], w=[psk[bb]])
                            cp(dst[:, :, :], v3(pq[0:64, 0:512], 4), r=[psk[bb]], w=[dkey], eng='act')
                        if RP < 5:
                            continue
                        bb = nextps()
                        for h in range(8):
                            cq, hp = h // 2, h % 2
                            mm(ps[bb][0:64, h * 64:(h + 1) * 64], BTm[:, h, :], vtok[:, cq, hp * 64:(hp + 1) * 64],
                               r=['BTm', 'rvtok'], w=[psk[bb]])
                        cp(BVb[:, :, :], v3(ps[bb][0:64, :], 8), r=[psk[bb]], w=['BVb'], eng='act')
                        bb = nextps()
                        bw_ = nextps()
                        for h in range(8):
                            cq, hp = h // 2, h % 2
                            mm(ps[bb][0:64, h * 64:(h + 1) * 64], TTb[:, h, :], BVb[:, h, :], r=['TTb', 'BVb'], w=[psk[bb]])
                            mm(ps[bw_][hp * 64:(hp + 1) * 64, cq * 64:(cq + 1) * 64], btok[:, cq, hp * 64:(hp + 1) * 64], TTb[:, h, :],
                               r=['btok', 'TTb'], w=[psk[bw_]])
                        cp(U0[:, :, :], v3(ps[bb][0:64, :], 8), r=[psk[bb]], w=['U0'], eng='act')
                        cp(WmTb[:, :, :], v3(ps[bw_][:, 0:256], 4), r=[psk[bw_]], w=['WmTb'], eng='act')
                        if RP < 6:
                            continue
                        bu_ = nextps()
                        for h in range(8):
                            cq, hp = h // 2, h % 2
                            pb_ = slice(hp * 64, (hp + 1) * 64)
                            if hp == 1 and os.environ.get('HP0'):
                                continue
                            mm(ps[bu_][0:64, h * 64:(h + 1) * 64], WmTb[pb_, cq, :], Pbf[pb_, cq, :], r=['WmTb', 'Pbf'], w=[psk[bu_]])
                        tt(Ub[:, :, :], U0[:, :, :], v3(ps[bu_][0:64, :], 8), ALU.add, r=['U0', psk[bu_]], w=['Ub'])
                        RQ = int(os.environ.get('RQ', '9'))
                        if RQ < 2:
                            continue
                        by_ = nextps()
                        bp2 = nextps()
                        for h in range(8):
                            cq, hp = h // 2, h % 2
                            pb_ = slice(hp * 64, (hp + 1) * 64)
                            yo_ = ps[by_][pb_, cq * 64:(cq + 1) * 64]
                            mm(yo_, Pbf[pb_, cq, :], rt[pb_, cq, :], start=True, stop=False, r=['Pbf', 'rt'], w=[psk[by_]])
                            if RQ >= 3:
                                mm(yo_, Ub[:, h, :], RaT[:, h, :], start=False, stop=False, r=['Ub', 'RaT'], w=[psk[by_]])
                                mm(yo_, vtok[:, cq, pb_], RkT[:, h, :], start=False, stop=True, r=['rvtok', 'RkT'], w=[psk[by_]])
                            po_ = ps[bp2][pb_, cq * 64:(cq + 1) * 64]
                            mm(po_, aGtok[:, cq, pb_], Ub[:, h, :], start=True, stop=False, r=['aGtok', 'Ub'], w=[psk[bp2]])
                            mm(po_, kGtok[:, cq, pb_], vtok[:, cq, pb_], start=False, stop=True, r=['kGtok', 'rvtok'], w=[psk[bp2]])
                        py = v3(ps[by_][:, 0:256], 4)
                        if d == 0:
                            cp(yacc[:, :, csl], py, r=[psk[by_]], w=[('yacc', c)], eng='act')
                        else:
                            tt(yacc[:, :, csl], yacc[:, :, csl], py, ALU.add, r=[psk[by_], ('yacc', c)], w=[('yacc', c)])
                        for cq in range(4):
                            stt(Pf[:, cq, :], Pf[:, cq, :], eTot[:, cq:cq + 1], ps[bp2][:, cq * 64:(cq + 1) * 64],
                                ALU.mult, ALU.add, r=['Pf', 'eTot', psk[bp2]], w=['Pf'])
                        seg_end = (c % 4 == 3) if d == 0 else (c % 4 == 0)
                        if seg_end:
                            dma('sp', rso_d[d, c // 4], Pf[:, :, :], r=['Pf'])
                            ts(Pf[:, :, :], Pf[:, :, :], flag[:, 0:1], ALU.mult, r=['Pf', 'flag'], w=['Pf'])
                        cp(Pbf[:, :, :], Pf[:, :, :], r=['Pf'], w=['Pbf'], eng='act')
                yk = [('yacc', c) for c in range(16)]
                for cq in range(4):
                    ysrc = yacc[:, cq, :]
                    for t2_ in range(2):
                        tsl = slice(t2_ * 512, (t2_ + 1) * 512)
                        bb = nextps()
                        cp(lw[:, 0, tsl].bitcast(BF16)[:, 0:512], ysrc[:, tsl], r=yk, w=['lw'], eng='act')
                        mm(ps[bb][:, :], bones[:, :], lw[:, 0, tsl].bitcast(BF16)[:, 0:512], r=['bones', 'lw'], w=[psk[bb]])
                        stt(lw[:, 1, tsl], ps[bb][:, :], -1.0 / 64, ysrc[:, tsl], ALU.mult, ALU.add, r=[psk[bb]] + yk, w=['lw'])
                        act(lw[:, 0, tsl].bitcast(BF16)[:, 0:512], lw[:, 1, tsl], AF.Square, r=['lw'], w=['lw'])
                        bb2 = nextps()
                        mm(ps[bb2][:, :], bones[:, :], lw[:, 0, tsl].bitcast(BF16)[:, 0:512], r=['bones', 'lw'], w=[psk[bb2]])
                        act(lw[:, 2, tsl], ps[bb2][:, :], AF.Sqrt, r=[psk[bb2], 'epsb'], w=['lw'], bias=epsb[:, 2:3], scale=1.0 / 64)
                        rcp(lw[:, 2, tsl], lw[:, 2, tsl], r=['lw'], w=['lw'])
                        stt(lw[:, 1, tsl], lw[:, 1, tsl], ppc('ln_w')[:, cq:cq + 1], lw[:, 2, tsl], ALU.mult, ALU.mult,
                            r=['lw', 'lw', 'pp'], w=['lw'])
                        stt(lw[:, 1, tsl], lw[:, 1, tsl], ppc('ln_b')[:, cq:cq + 1], bon[:, cq, tsl], ALU.add, ALU.add,
                            r=['lw', 'pp', ('bon', cq)], w=['lw'])
                        tt(mix1T[:, 4 + cq, tsl], lw[:, 1, tsl], gate[:, cq, tsl], ALU.mult, r=['lw', ('gate', cq)],
                           w=[('mix1T', 4 + cq)])
                S.flush()

        if STAGE >= 1:
            with contextlib.ExitStack() as sc:
                alloc_work(sc, 'p1')
                layer0_mixer(sc)
                S.flush()
        if STAGE >= 4:
            with contextlib.ExitStack() as sc:
                alloc_work(sc, 'p2')
                uT = sc.enter_context(nc.sbuf_tensor("uT", [128, 32, NT], BF16))
                mlp(0, uT)
                S.flush()
        if STAGE >= 6:
            layer1_mixer()
        with contextlib.ExitStack() as sc:
            alloc_work(sc, 'p7')
            if STAGE >= 5:
                uT = sc.enter_context(nc.sbuf_tensor("uT2", [128, 32, NT], BF16))
                mlp(1, uT)
            yo = sc.enter_context(nc.sbuf_tensor("yo", [128, 2, 512], F32))
            norm_mod(ppc('norm_final'), None, yo, 'yo', final=True)
            S.flush()
    return nc


def make_l1_consts():
    rm = np.zeros((128, 128), np.float32)
    for hb in range(2):
        for d in range(64):
            if (d % 32) < 16:
                rm[hb * 64 + d + 16, hb * 64 + d] = -1.0
            else:
                rm[hb * 64 + d - 16, hb * 64 + d] = 1.0
    bones = np.zeros((128, 128), np.float32)
    bones[:64, :64] = 1.0
    bones[64:, 64:] = 1.0
    t = np.arange(NT)
    row = (t // 64).astype(np.float32)
    col = (t % 64).astype(np.float32)
    inv = (10000.0 ** (-np.arange(0, 32, 2, dtype=np.float32) / 32)).astype(np.float32)
    cs = np.zeros((128, NT), np.float32)
    sn = np.zeros((128, NT), np.float32)
    for p in range(128):
        d = p % 64
        pos = row if d < 32 else col
        ang = (pos * inv[(d % 32) % 16]).astype(np.float32)
        cs[p] = np.cos(ang)
        sn[p] = np.sin(ang)
    NEG = -30000.0
    mk = np.full((2, 8, 128, 896), NEG, np.float32)
    for n in range(8):
        tq = n * 128 + np.arange(128)[:, None]
        tk = (n - 1) * 128 + np.arange(384)[None, :]
        inr = (tk >= 0) & (tk < NT)
        mk[0, n, :, :384] = np.where(inr & ((tk // 256) == (tq // 256)), 0.0, NEG)
        mk[1, n, :, :384] = np.where(inr & (np.abs(tk - tq) <= 128), 0.0, NEG)
        mk[1, n, :, 384:] = 0.0
    return rm, bones, cs, sn, mk


def kernel(**inp):
    inp = {k: np.asarray(v) for k, v in inp.items()}
    nc = build_nc()
    cst = make_consts()
    pp = make_pp(inp)
    xp = inp['x_prompt'].astype(np.float32)
    xs = inp['x_sample'].astype(np.float32)
    shared = {
        'pp': pp, 'cst': cst,
        'mod_w': np.ascontiguousarray(inp['mod_w'], np.float32),
        'mlp_w1': np.ascontiguousarray(inp['mlp_w1'], np.float32),
        'mlp_w2': np.ascontiguousarray(inp['mlp_w2'], np.float32),
        'ev_w_in': np.ascontiguousarray(inp['ev_w_in'][0], np.float32),
        'ev_w_out': np.ascontiguousarray(inp['ev_w_out'][0], np.float32),
    }
    rm, bones, rcs, rsn, mk = make_l1_consts()
    wi = np.asarray(inp['od_w_in'][0], np.float32)
    shared['od_w_in_x'] = np.ascontiguousarray(np.concatenate(
        [wi[:, 0:512], wi[:, 512:576], wi[:, 512:576], wi[:, 576:640], wi[:, 576:640], wi[:, 640:768], wi[:, 768:]], 1))
    shared['od_w_out'] = np.ascontiguousarray(inp['od_w_out'][0], np.float32)
    shared['w_up'] = np.ascontiguousarray(np.asarray(inp['rwkv_w_up'][0], np.float32).reshape(128, 512))
    shared['a_up'] = np.ascontiguousarray(np.asarray(inp['rwkv_a_up'][0], np.float32).reshape(128, 512))
    shared['g_up'] = np.ascontiguousarray(inp['rwkv_g_up'][0], np.float32)
    shared['bones'] = bones
    shared['rm'] = rm

    def st_in(sv):
        a = np.asarray(sv, np.float32).transpose(0, 2, 1).reshape(4, 2, 64, 64)
        return np.ascontiguousarray(a.transpose(1, 2, 0, 3).reshape(128, 4, 64))

    def st_out(a):
        b = a.reshape(2, 64, 4, 64).transpose(2, 0, 1, 3).reshape(8, 64, 64)
        return b.transpose(0, 2, 1)
    in_maps = []
    for core in range(8):
        m = dict(shared)
        if core < 4:
            xt = xp[core * 4:(core + 1) * 4].reshape(NT, 1024)
            m['cv'] = fm(inp['c_ctx'], 8)
            m['flag'] = np.zeros((128, 1), np.float32)
            m['gs0'] = np.zeros((2, 4, 128, 128), np.float32)
            m['ropec'] = np.ones((128, NT), np.float32)
            m['ropes'] = np.zeros((128, NT), np.float32)
            m['maskb'] = mk[0]
            m['ckT'] = np.zeros((2, 128, 512), np.float32)
            m['cvt'] = np.zeros((128, 4, 128), np.float32)
            m['rs0'] = np.zeros((2, 128, 4, 64), np.float32)
        else:
            b = core - 4
            xt = xs[b]
            m['cv'] = fm(inp['c'][b], 8)
            m['flag'] = np.ones((128, 1), np.float32)
            m['gs0'] = np.ascontiguousarray(
                np.stack([inp['state_gdn_fwd'][b, 0], inp['state_gdn_bwd'][b, 0]]), np.float32)
            m['ropec'] = rcs
            m['ropes'] = rsn
            m['maskb'] = mk[1]
            ck = np.asarray(inp['cache_attn_k'][b, 0], np.float32)
            m['ckT'] = np.ascontiguousarray(np.stack([np.concatenate([ck[k].T, ck[k].T], 0) for k in range(2)]))
            cvv = np.asarray(inp['cache_attn_v'][b, 0], np.float32)
            m['cvt'] = np.ascontiguousarray(cvv.transpose(1, 0, 2).reshape(4, 128, 128).transpose(1, 0, 2))
            m['rs0'] = np.stack([st_in(inp['state_rwkv_fwd'][b, 0]), st_in(inp['state_rwkv_bwd'][b, 0])])
        m['xT'] = np.ascontiguousarray(xt.T)
        in_maps.append(m)
    res = run_bass_kernel_spmd(nc, in_maps, core_ids=list(range(8)))
    R = res.results
    y_prompt = np.stack([R[c]['yT'].T.reshape(4, 256, 1024) for c in range(4)]).reshape(16, 256, 1024)
    y_sample = np.stack([R[4 + b]['yT'].T for b in range(4)])
    gf = np.stack([R[c]['gso'][0] for c in range(4)]).reshape(16, 1, 4, 128, 128)
    gb = np.stack([R[c]['gso'][1] for c in range(4)]).reshape(16, 1, 4, 128, 128)
    def kvout(i):
        o = np.stack([R[c]['kvo'][i] for c in range(4)])
        o = o.reshape(4, 2, 64, 4, 256).transpose(0, 3, 1, 4, 2)
        return np.ascontiguousarray(o.reshape(16, 1, 2, 256, 64)).astype(np.float32)

    def rsout(d):
        o = np.stack([np.stack([st_out(R[c]['rso'][d, sg]) for sg in range(4)]) for c in range(4)])
        return np.ascontiguousarray(o.reshape(16, 1, 8, 64, 64)).astype(np.float32)
    return (y_prompt.astype(np.float32), y_sample.astype(np.float32), gf.astype(np.float32), gb.astype(np.float32),
            kvout(0), kvout(1), rsout(0), rsout(1))
```

```python
import contextlib
import os
import numpy as np
import concourse.bass as bass
import concourse.mybir as mybir
from concourse.bass_utils import run_bass_kernel_spmd

F32 = mybir.dt.float32
BF16 = mybir.dt.bfloat16
AF = mybir.ActivationFunctionType
ALU = mybir.AluOpType

ENGS = ['pe', 'act', 'dve', 'pool', 'sp']
DMAQ = ('sp', 'pool')
KRING = 6
NT = 1024
EPS = 1e-6


class Sched:
    def __init__(self, nc):
        self.nc = nc
        self.prog = {e: [] for e in ENGS}
        self.cnt = {e: 0 for e in ENGS}
        self.known = {e: {} for e in ENGS}
        self.res = {}
        self.dma_i = {q: 0 for q in DMAQ}
        self.sems = {}
        self.eng_obj = {'pe': nc.tensor, 'act': nc.scalar, 'dve': nc.vector,
                        'pool': nc.gpsimd, 'sp': nc.sync}

    def alloc_sems(self, stack):
        for e in ['pe', 'act', 'dve', 'pool']:
            self.sems[e] = stack.enter_context(self.nc.semaphore('c_' + e))
        for q in DMAQ:
            for j in range(KRING):
                self.sems[(q, j)] = stack.enter_context(self.nc.semaphore('d_%s%d' % (q, j)))

    def _r(self, k):
        if k not in self.res:
            self.res[k] = {'w': None, 'r': {}}
        return self.res[k]

    def _need(self, eng, waits, dep, own):
        if dep is None:
            return
        sk, val = dep
        if sk == own:
            return
        if self.known[eng].get(sk, 0) >= val:
            return
        if waits.get(sk, 0) < val:
            waits[sk] = val

    def op(self, eng, fn, reads=(), writes=()):
        own = eng
        skip = eng if eng == 'pe' else None
        waits = {}
        for k in reads:
            self._need(eng, waits, self._r(k)['w'], skip)
        for k in writes:
            r = self._r(k)
            self._need(eng, waits, r['w'], skip)
            for sk, v in r['r'].items():
                self._need(eng, waits, (sk, v), skip)
        self.cnt[eng] += 1
        v = self.cnt[eng]
        for sk, val in waits.items():
            self.known[eng][sk] = val
        self.prog[eng].append((fn, list(waits.items()), (own, 1)))
        for k in reads:
            r = self._r(k)
            if r['r'].get(own, 0) < v:
                r['r'][own] = v
        for k in writes:
            r = self._r(k)
            r['w'] = (own, v)
            r['r'] = {}

    def dma(self, q, fn, reads=(), writes=()):
        i = self.dma_i[q]
        self.dma_i[q] += 1
        own = (q, i % KRING)
        val = 16 * (i // KRING + 1)
        waits = {}
        if i >= KRING:
            self._need(q, waits, (own, val - 16), None)
        for k in reads:
            self._need(q, waits, self._r(k)['w'], None)
        for k in writes:
            r = self._r(k)
            self._need(q, waits, r['w'], None)
            for sk, v in r['r'].items():
                self._need(q, waits, (sk, v), None)
        for sk, v in waits.items():
            self.known[q][sk] = v
        self.prog[q].append((fn, list(waits.items()), (own, 16)))
        for k in reads:
            r = self._r(k)
            if r['r'].get(own, 0) < val:
                r['r'][own] = val
        for k in writes:
            r = self._r(k)
            r['w'] = (own, val)
            r['r'] = {}

    def _all_done(self):
        waits = []
        for q in DMAQ:
            n = self.dma_i[q]
            for j in range(KRING):
                cntj = len(range(j, n, KRING))
                if cntj:
                    waits.append(((q, j), 16 * cntj))
        for e in ['pe', 'act', 'dve', 'pool']:
            if self.cnt[e]:
                waits.append((e, self.cnt[e]))
        return waits

    def barrier(self):
        waits = self._all_done()
        for e in ENGS:
            w2 = [(sk, v) for sk, v in waits if sk != e and self.known[e].get(sk, 0) < v]
            for sk, v in w2:
                self.known[e][sk] = v
            self.prog[e].append((None, w2, None))
        self.res = {}

    def finish(self, eng='sp'):
        self.prog[eng].append((None, self._all_done(), None))

    def emit(self, block):
        S = self

        def replay(e):
            def body(_eng):
                eo = S.eng_obj[e]
                for fn, waits, inc in S.prog[e]:
                    for sk, v in waits:
                        eo.wait_ge(S.sems[sk], v)
                    if fn is not None:
                        ins = fn(eo)
                        ins.then_inc(S.sems[inc[0]], inc[1])
            return body
        block.tensor(replay('pe'))
        block.scalar(replay('act'))
        block.vector(replay('dve'))
        block.gpsimd(replay('pool'))
        block.sync(replay('sp'))

    def flush(self):
        self.barrier()
        with self.nc.Block() as block:
            self.emit(block)
        self.prog = {e: [] for e in ENGS}


def _pp_layout():
    ent = [('mod_b', 96), ('norm_mix', 16), ('norm_mlp', 16), ('norm_final', 8),
           ('gdn_conv', 36), ('sc_conv', 12), ('gdn_norm', 1), ('a_log', 1), ('dt_bias', 1),
           ('mu', 30), ('w0', 8), ('a0', 8), ('k_k', 4), ('k_a', 4), ('ln_w', 4), ('ln_b', 4), ('r_k', 4), ('sink', 8)]
    off = {}
    o = 0
    for n, w in ent:
        off[n] = (o, w)
        o += w
    return off, o


PP_OFF, PP_N = _pp_layout()
C_ID, C_U, C_L, C_SU, C_SL, C_N = 0, 128, 192, 256, 320, 768
C_S16L, C_O32L, C_O64L, C_S16U, C_O32U, C_O64U = 384, 448, 512, 576, 640, 704


def make_consts():
    c = np.zeros((128, C_N), np.float32)
    c[:, C_ID:C_ID + 128] = np.eye(128, dtype=np.float32)
    p = np.arange(64)[:, None]
    f = np.arange(64)[None, :]
    c[:64, C_U:C_U + 64] = (p <= f)
    c[:64, C_L:C_L + 64] = (p >= f)
    c[:64, C_SU:C_SU + 64] = (p < f)
    c[:64, C_SL:C_SL + 64] = (p > f)
    c[:64, C_S16L:C_S16L + 64] = (p > f) & (p // 16 == f // 16)
    c[:64, C_O32L:C_O32L + 64] = (p > f) & (p // 32 == f // 32) & (p // 16 != f // 16)
    c[:64, C_O64L:C_O64L + 64] = (p > f) & (p // 32 != f // 32)
    c[:64, C_S16U:C_S16U + 64] = (p < f) & (p // 16 == f // 16)
    c[:64, C_O32U:C_O32U + 64] = (p < f) & (p // 32 == f // 32) & (p // 16 != f // 16)
    c[:64, C_O64U:C_O64U + 64] = (p < f) & (p // 32 != f // 32)
    return c


def fm(v, nch):
    return np.ascontiguousarray(np.asarray(v, np.float32).reshape(nch, 128).T)


def make_pp(inp):
    pp = np.zeros((128, PP_N), np.float32)

    def put(name, arr):
        o, w = PP_OFF[name]
        assert arr.shape == (128, w), (name, arr.shape, w)
        pp[:, o:o + w] = arr
    put('mod_b', np.concatenate([fm(inp['mod_b'][l], 48) for l in range(2)], 1))
    put('norm_mix', np.concatenate([fm(inp['norm_mix'][l], 8) for l in range(2)], 1))
    put('norm_mlp', np.concatenate([fm(inp['norm_mlp'][l], 8) for l in range(2)], 1))
    put('norm_final', fm(inp['norm_final'], 8))
    put('gdn_conv', np.concatenate([fm(inp['gdn_conv'][0][i], 12) for i in range(3)], 1))
    put('sc_conv', np.concatenate([fm(inp['sc_conv'][0][i], 4) for i in range(3)], 1))
    put('gdn_norm', np.asarray(inp['gdn_norm'][0], np.float32).reshape(128, 1))
    a = np.zeros((128, 1), np.float32)
    a[:8, 0] = np.asarray(inp['gdn_a_log'][0], np.float32).reshape(8)
    put('a_log', a)
    a = np.zeros((128, 1), np.float32)
    a[:8, 0] = np.asarray(inp['gdn_dt_bias'][0], np.float32).reshape(8)
    put('dt_bias', a)
    put('mu', np.concatenate([fm(inp['rwkv_mu'][0][d], 15) for d in range(2)], 1))
    put('w0', np.concatenate([fm(inp['rwkv_w0'][0][d], 4) for d in range(2)], 1))
    put('a0', np.concatenate([fm(inp['rwkv_a0'][0][d], 4) for d in range(2)], 1))
    put('k_k', fm(inp['rwkv_k_k'][0], 4))
    put('k_a', fm(inp['rwkv_k_a'][0], 4))
    put('ln_w', fm(inp['rwkv_ln_w'][0], 4))
    put('ln_b', fm(inp['rwkv_ln_b'][0], 4))
    put('r_k', fm(np.asarray(inp['rwkv_r_k'][0]).reshape(512), 4))
    put('sink', np.tile(np.asarray(inp['attn_sink'][0], np.float32).reshape(1, 8), (128, 1)))
    return pp


STAGE = 9
L1PART = int(os.environ.get('L1PART', '9'))
SUB = 9


def build_nc(do_l1=False):
    nc = bass.Bass("TRN2", target_bir_lowering=False)

    def din(name, shape):
        return nc.dram_tensor(name, list(shape), F32, kind="ExternalInput").ap()

    def dout(name, shape):
        return nc.dram_tensor(name, list(shape), F32, kind="ExternalOutput").ap()

    xT_d = din("xT", [1024, NT])
    cv_d = din("cv", [128, 8])
    flag_d = din("flag", [128, 1])
    s0_d = din("gs0", [2, 4, 128, 128])
    pp_d = din("pp", [128, PP_N])
    cst_d = din("cst", [128, C_N])
    modw_d = din("mod_w", [2, 1024, 6144])
    w1_d = din("mlp_w1", [2, 1024, 4096])
    w2_d = din("mlp_w2", [2, 4096, 1024])
    evin_d = din("ev_w_in", [1024, 3600])
    evout_d = din("ev_w_out", [1024, 1024])
    odin_d = din("od_w_in_x", [1024, 2816])
    odout_d = din("od_w_out", [1024, 1024])
    wup_d = din("w_up", [128, 512])
    aup_d = din("a_up", [128, 512])
    gup_d = din("g_up", [128, 512])
    bones_d = din("bones", [128, 128])
    rm_d = din("rm", [128, 128])
    ropec_d = din("ropec", [128, NT])
    ropes_d = din("ropes", [128, NT])
    maskb_d = din("maskb", [8, 128, 896])
    ckT_d = din("ckT", [2, 128, 512])
    cvt_d = din("cvt", [128, 4, 128])
    rs0_d = din("rs0", [2, 128, 4, 64])
    kvo_d = dout("kvo", [2, 128, NT])
    rso_d = dout("rso", [2, 4, 128, 4, 64])
    yT_d = dout("yT", [1024, NT])
    gso_d = dout("gso", [2, 4, 4, 128, 128])

    with contextlib.ExitStack() as st:
        S = Sched(nc)
        S.alloc_sems(st)
        with nc.Block() as blk0:
            def _clr(_e):
                for sm in S.sems.values():
                    nc.sync.sem_clear(sm)
            blk0.sync(_clr)

        def tile(name, shape, dt=F32):
            return st.enter_context(nc.sbuf_tensor('sb_' + name, list(shape), dt))

        ps = [st.enter_context(nc.psum_tensor("ps%d" % i, [128, 512], F32)) for i in range(8)]
        psk = ["ps%d" % i for i in range(8)]
        state = {'ps': 0, 'wb': 0}

        def nextps():
            b = state['ps']
            state['ps'] = (b + 1) % 8
            return b

        def pe_mode(mode):
            if not state.get('drain'):
                return
            if state.get('pemode') != mode:
                state['pemode'] = mode
                if S.cnt['pe'] > 0:
                    S.prog['pe'].append((None, [('pe', S.cnt['pe'])], None))

        def rnd(n):
            return 32 if n <= 32 else (64 if n <= 64 else 128)

        def mm(out, lhsT, rhs, start=True, stop=True, r=(), w=()):
            pe_mode(('mm', rnd(lhsT.shape[0]), rnd(lhsT.shape[-1]), lhsT.start_partition(), out.start_partition()))
            S.op('pe', lambda e: e.matmul(out, lhsT, rhs, start=start, stop=stop), r, w)

        def tr(out, in_, ident, r=(), w=()):
            pe_mode(('tr', rnd(in_.shape[0]), rnd(in_.shape[-1]), in_.start_partition(), out.start_partition()))
            S.op('pe', lambda e: e.transpose(out, in_, ident), r, w)

        def act(out, in_, func, r=(), w=(), bias=None, scale=None):
            kw = {}
            if bias is not None:
                kw['bias'] = bias
            if scale is not None:
                kw['scale'] = scale
            S.op('act', lambda e: e.activation(out, in_, func, **kw), r, w)

        def tt(out, a, b, op, r=(), w=(), eng='dve'):
            S.op(eng, lambda e: e.tensor_tensor(out, a, b, op), r, w)

        def ts(out, a, s1, op0, r=(), w=(), s2=None, op1=None, eng='dve'):
            if op1 is None:
                S.op(eng, lambda e: e.tensor_scalar(out, a, s1, None, op0), r, w)
            else:
                S.op(eng, lambda e: e.tensor_scalar(out, a, s1, s2, op0, op1), r, w)

        def stt(out, a, s, b, op0, op1, r=(), w=()):
            S.op('dve', lambda e: e.scalar_tensor_tensor(out, a, s, b, op0, op1), r, w)

        def cp(out, in_, r=(), w=(), eng='dve'):
            if eng == 'act':
                S.op('act', lambda e: e.copy(out, in_), r, w)
            else:
                S.op(eng, lambda e: e.tensor_scalar(out, in_, 1.0, None, ALU.mult), r, w)

        def rcp(out, in_, r=(), w=()):
            S.op('dve', lambda e: e.reciprocal(out, in_), r, w)

        def scan(out, d0, d1, r=(), w=()):
            S.op('dve', lambda e: e.tensor_tensor_scan(out, d0, d1, 0.0, ALU.mult, ALU.add), r, w)

        def mset(ap, val, r=(), w=(), eng='dve'):
            S.op(eng, lambda e: e.memset(ap, val), r, w)

        def dma(q, out, in_, r=(), w=()):
            S.dma(q, lambda e: e.dma_start(out=out, in_=in_), r, w)

        x_sb = tile("x_sb", [128, 8, NT])
        hT = None
        wb = None
        cst = tile("cst", [128, C_N])
        pp = tile("pp", [128, PP_N])
        cv = tile("cv", [128, 8])
        cvs = tile("cvs", [128, 8], BF16)
        flag = tile("flag", [128, 1])
        ident_bf = tile("ident_bf", [128, 128], BF16)
        ones_bf = tile("ones_bf", [128, 128], BF16)
        ones_f = tile("ones_f", [128, 128])
        modv = tile("modv", [128, 2, 48])
        gsA = tile("gsA", [128, 2, 8])
        gsB = tile("gsB", [128, 2, 8])
        sqb = rstd = ntmp = None

        def alloc_work(sc, tag, with_h=True, nwb=3):
            nonlocal hT, wb, sqb, rstd, ntmp
            A = lambda n, s_, d=F32: sc.enter_context(nc.sbuf_tensor('sb_%s_%s' % (n, tag), list(s_), d))
            if with_h:
                hT = A("hT", [128, 8, NT], BF16)
            wb = [A("wb%d" % i, [128, 4096], BF16) for i in range(nwb)]
            state['nwb'] = nwb
            state['wb'] = 0
            sqb = A("sqb", [128, 2, 512], BF16)
            rstd = A("rstd", [128, 512])
            ntmp = [A("ntmp%d" % i, [128, 512]) for i in range(2)]

        sc0 = contextlib.ExitStack()
        alloc_work(sc0, 'p0', with_h=False)

        ident_f = cst[:, C_ID:C_ID + 128]

        def ppc(name, j0=0, n=None):
            o, wd = PP_OFF[name]
            if n is None:
                n = wd - j0
            return pp[:, o + j0:o + j0 + n]

        dma('sp', cst[:, :], cst_d[:, :], w=['cst'])
        dma('sp', pp[:, :], pp_d[:, :], w=['pp'])
        dma('sp', cv[:, :], cv_d[:, :], w=['cv'])
        dma('sp', flag[:, :], flag_d[:, :], w=['flag'])
        for fc in range(8):
            dma('sp', x_sb[:, fc, :], xT_d[fc * 128:(fc + 1) * 128, :], w=[('x', fc)])
        act(cvs[:, :], cv[:, :], AF.Silu, r=['cv'], w=['cvs'])
        cp(ident_bf[:, :], ident_f, r=['cst'], w=['ident_bf'])
        mset(ones_bf[:, :], 1.0, w=['ones_bf'])
        mset(ones_f[:, :], 1.0, w=['ones_f'])

        def load_piece(wd_ap, kcn, wdth):
            slot = state['wb']
            state['wb'] = (slot + 1) % state['nwb']
            view = wb[slot][:, 0:kcn * wdth].rearrange("p (k n) -> p k n", k=kcn)
            dma('pool', view, wd_ap.rearrange("(k p) n -> p k n", p=128), w=[('wb', slot)])
            return slot, view

        for l in range(2):
            bm = nextps()
            for oc in range(12):
                slot, wv = load_piece(modw_d[l, :, oc * 512:(oc + 1) * 512], 8, 512)
                for c4 in range(4):
                    ocn = oc * 4 + c4
                    for kc in range(8):
                        mm(ps[bm][:, ocn:ocn + 1], wv[:, kc, c4 * 128:(c4 + 1) * 128], cvs[:, kc:kc + 1],
                           start=(kc == 0), stop=(kc == 7), r=[('wb', slot), 'cvs'], w=[psk[bm]])
            tt(modv[:, l, :], ps[bm][:, 0:48], ppc('mod_b', l * 48, 48), ALU.add,
               r=[psk[bm], 'pp'], w=['modv'])
            stt(gsA[:, l, :], modv[:, l, 8:16], 1.0, ppc('norm_mix', l * 8, 8), ALU.add, ALU.mult,
                r=['modv', 'pp'], w=['gsA'])
            stt(gsB[:, l, :], modv[:, l, 32:40], 1.0, ppc('norm_mlp', l * 8, 8), ALU.add, ALU.mult,
                r=['modv', 'pp'], w=['gsB'])

        S.flush()
        sc0.close()

        def norm_mod(gs_ap, shift_ap, dst, dst_key, final=False):
            for th in range(2):
                tsl = slice(th * 512, (th + 1) * 512)
                b = nextps()
                for fc in range(8):
                    act(sqb[:, fc % 2, :], x_sb[:, fc, tsl], AF.Square, r=[('x', fc)], w=[('sqb', fc % 2)])
                    mm(ps[b][:, :], ones_bf[:, :], sqb[:, fc % 2, :], start=(fc == 0), stop=(fc == 7),
                       r=['ones_bf', ('sqb', fc % 2)], w=[psk[b]])
                act(rstd[:, :], ps[b][:, :], AF.Sqrt, r=[psk[b], 'epsb'], w=['rstd'], bias=epsb[:, 0:1], scale=1.0 / 1024)
                rcp(rstd[:, :], rstd[:, :], r=['rstd'], w=['rstd'])
                for fc in range(8):
                    k = fc % 2
                    tt(ntmp[k][:, :], x_sb[:, fc, tsl], rstd[:, :], ALU.mult,
                       r=[('x', fc), 'rstd'], w=[('ntmp', k)])
                    if final:
                        act(dst[:, k, :], ntmp[k][:, :], AF.Identity, r=[('ntmp', k), 'pp'],
                            w=[(dst_key, k)], scale=gs_ap[:, fc:fc + 1])
                        dma('sp', yT_d[fc * 128:(fc + 1) * 128, tsl], dst[:, k, :], r=[(dst_key, k)])
                    else:
                        act(dst[:, fc, tsl], ntmp[k][:, :], AF.Identity, r=[('ntmp', k), 'modv', 'gsA', 'gsB'],
                            w=[(dst_key, fc)], scale=gs_ap[:, fc:fc + 1], bias=shift_ap[:, fc:fc + 1])

        epsb = tile("epsb", [128, 4])
        mset(epsb[:, 0:1], EPS, w=['epsb'])
        mset(epsb[:, 1:2], 1e-6, w=['epsb'])
        mset(epsb[:, 2:3], 64e-5, w=['epsb'])

        def linear(wd, kcn, pieces, src, src_key, consumer):
            for (c0, wdth) in pieces:
                slot, wv = load_piece(wd[:, c0:c0 + wdth], kcn, wdth)
                for cs in range(0, wdth, 128):
                    cw = min(128, wdth - cs)
                    for th in range(2):
                        b = nextps()
                        for kc in range(kcn):
                            mm(ps[b][0:cw, :], wv[:, kc, cs:cs + cw], src[:, kc, th * 512:(th + 1) * 512],
                               start=(kc == 0), stop=(kc == kcn - 1),
                               r=[('wb', slot), (src_key, kc)], w=[psk[b]])
                        consumer(c0 + cs, cw, th, b)

        def mlp(l, uT):
            norm_mod(gsB[:, l, :], modv[:, l, 24:32], hT, 'hT')

            def c1(c0, cw, th, b):
                oc = c0 // 128
                act(ntmp[th][:, :], ps[b][:, :], AF.Relu, r=[psk[b]], w=[('ntmp', th)])
                tt(uT[:, oc, th * 512:(th + 1) * 512], ntmp[th][:, :], ntmp[th][:, :], ALU.mult,
                   r=[('ntmp', th)], w=[('uT', oc)])
            linear(w1_d[l], 8, [(i * 512, 512) for i in range(8)], hT, 'hT', c1)

            def c2(c0, cw, th, b):
                fc = c0 // 128
                tsl = slice(th * 512, (th + 1) * 512)
                stt(x_sb[:, fc, tsl], ps[b][:, :], modv[:, l, 40 + fc:41 + fc], x_sb[:, fc, tsl],
                    ALU.mult, ALU.add, r=[psk[b], 'modv', ('x', fc)], w=[('x', fc)])
            linear(w2_d[l], 32, [(i * 128, 128) for i in range(8)], uT, 'uT', c2)

        def layer0_mixer(sc):
            T = lambda n, s, d=F32: sc.enter_context(nc.sbuf_tensor('sb_' + n, list(s), d))
            qT = T("qT", [128, 4, NT], BF16)
            kT = T("kT", [128, 4, NT], BF16)
            vT = T("vT", [128, 4, NT], BF16)
            mixT = T("mixT", [128, 8, NT], BF16)
            oacc = T("oacc", [128, 4, NT])
            pad = [T("pad%d" % i, [128, 4, 258]) for i in range(2)]
            cacc1 = T("cacc", [128, 4, 256])
            cacc = [cacc1, cacc1]
            qsq = T("qsq", [128, NT], BF16)
            rinv = T("rinv", [128, NT])
            sctmp = T("sctmp", [128, NT])
            betaT = T("betaT", [8, NT])
            gT = T("gT", [8, NT])
            negA = T("negA", [8, 1])
            Sf = T("Sf", [128, 4, 128])
            Sb = T("Sb", [128, 4, 128], BF16)

            for i in range(2):
                mset(pad[i][:, :, :], 0.0, w=[('pad', i)])
            act(negA[:, :], ppc('a_log')[0:8, :], AF.Exp, r=['pp'], w=['negA'])
            ts(negA[:, :], negA[:, :], -1.0, ALU.mult, r=['negA'], w=['negA'])

            norm_mod(gsA[:, 0, :], modv[:, 0, 0:8], hT, 'hT')

            cstate = {'i': 0}
            if SUB < 1:
                return

            def conv3(pi, wname, nch, ch):
                p = pad[pi]
                ts(p[:, 1:4, 0:1], p[:, 0:3, 256:257], flag[:, 0:1], ALU.mult,
                   r=[('pad', pi), 'flag'], w=[('pad', pi)])
                ts(p[:, 0:3, 257:258], p[:, 1:4, 1:2], flag[:, 0:1], ALU.mult,
                   r=[('pad', pi), 'flag'], w=[('pad', pi)])
                o, _ = PP_OFF[wname]
                w0 = pp[:, o + 0 * nch + ch:o + 0 * nch + ch + 1]
                w1 = pp[:, o + 1 * nch + ch:o + 1 * nch + ch + 1]
                w2 = pp[:, o + 2 * nch + ch:o + 2 * nch + ch + 1]
                ts(cacc[pi][:, :, :], p[:, :, 0:256], w0, ALU.mult, r=[('pad', pi), 'pp'], w=['cacc'])
                stt(cacc[pi][:, :, :], p[:, :, 1:257], w1, cacc[pi][:, :, :], ALU.mult, ALU.add,
                    r=[('pad', pi), 'pp', 'cacc'], w=['cacc'])
                stt(cacc[pi][:, :, :], p[:, :, 2:258], w2, cacc[pi][:, :, :], ALU.mult, ALU.add,
                    r=[('pad', pi), 'pp', 'cacc'], w=['cacc'])

            def pad_in(pi, th):
                return pad[pi][:, 2 * th:2 * th + 2, 1:257]

            def ps3(b):
                return ps[b][:, :].rearrange("p (s t) -> p s t", s=2)

            def c_qkv(c0, cw, th, b):
                ch = c0 // 128
                pi = ch % 2
                cp(pad_in(pi, th), ps3(b), r=[psk[b]], w=[('pad', pi)], eng='act')
                if th == 0:
                    return
                conv3(pi, 'gdn_conv', 12, ch)
                flat = cacc[pi][:, :, :].rearrange("p s t -> p (s t)")
                h = ch % 4
                if ch >= 8:
                    act(vT[:, h, :], flat, AF.Silu, r=['cacc'], w=[('vT', h)])
                    return
                act(sctmp[:, :], flat, AF.Silu, r=['cacc'], w=['sctmp'])
                act(qsq[:, :], sctmp[:, :], AF.Square, r=['sctmp'], w=['qsq'])
                for t2 in range(2):
                    bb = nextps()
                    mm(ps[bb][:, :], ones_bf[:, :], qsq[:, t2 * 512:(t2 + 1) * 512], r=['ones_bf', 'qsq'], w=[psk[bb]])
                    act(rinv[:, t2 * 512:(t2 + 1) * 512], ps[bb][:, :], AF.Sqrt, r=[psk[bb], 'epsb'],
                        w=['rinv'], bias=epsb[:, 1:2])
                rcp(rinv[:, :], rinv[:, :], r=['rinv'], w=['rinv'])
                dst, key = (qT, 'qT') if ch < 4 else (kT, 'kT')
                scl = (128.0 ** -0.5) if ch < 4 else 1.0
                stt(dst[:, h, :], sctmp[:, :], scl, rinv[:, :], ALU.mult, ALU.mult,
                    r=['sctmp', 'rinv'], w=[(key, h)])
            linear(evin_d, 8, [(i * 512, 512) for i in range(3)], hT, 'hT', c_qkv)

            if SUB < 2:
                return
            def c_z(c0, cw, th, b):
                h = (c0 - 1536) // 128
                act(mixT[:, h, th * 512:(th + 1) * 512], ps[b][:, :], AF.Silu, r=[psk[b]], w=[('mixT', h)])
            linear(evin_d, 8, [(1536, 512)], hT, 'hT', c_z)

            if SUB < 3:
                return
            def c_beta(c0, cw, th, b):
                act(betaT[:, th * 512:(th + 1) * 512], ps[b][0:8, :], AF.Sigmoid, r=[psk[b]], w=['betaT'])

            def c_a(c0, cw, th, b):
                tsl = slice(th * 512, (th + 1) * 512)
                act(rinv[0:8, tsl], ps[b][0:8, :], AF.Exp, r=[psk[b], 'pp'], w=['rinv'], bias=ppc('dt_bias')[0:8, :])
                act(rinv[0:8, tsl], rinv[0:8, tsl], AF.Ln, r=['rinv'], w=['rinv'], bias=1.0)
                ts(gT[:, tsl], rinv[0:8, tsl], negA[:, 0:1], ALU.mult, r=['rinv', 'negA'], w=['gT'])
            linear(evin_d, 8, [(2048, 8)], hT, 'hT', c_beta)
            linear(evin_d, 8, [(2056, 8)], hT, 'hT', c_a)

            if SUB < 4:
                return
            def mk_sc(j):
                def c_c(c0, cw, th, b):
                    cp(sctmp[:, th * 512:(th + 1) * 512], ps[b][:, :], r=[psk[b]], w=['sctmp'], eng='act')

                def c_h(c0, cw, th, b):
                    pi = j % 2
                    tt(pad_in(pi, th), ps3(b), sctmp[:, th * 512:(th + 1) * 512].rearrange("p (s t) -> p s t", s=2),
                       ALU.mult, r=[psk[b], 'sctmp'], w=[('pad', pi)])
                    if th == 1:
                        conv3(pi, 'sc_conv', 4, j)

                def c_b(c0, cw, th, b):
                    pi = j % 2
                    tt(mixT[:, 4 + j, th * 512:(th + 1) * 512].rearrange("p (s t) -> p s t", s=2), ps3(b),
                       cacc[pi][:, 2 * th:2 * th + 2, :], ALU.mult, r=[psk[b], 'cacc'], w=[('mixT', 4 + j)])
                return c_c, c_h, c_b
            import os
            SCJ = int(os.environ.get('SCJ', '4'))
            SCP = int(os.environ.get('SCP', '3'))
            for j in range(SCJ):
                c_c, c_h, c_b = mk_sc(j)
                linear(evin_d, 8, [(2064 + 512 + j * 128, 128)], hT, 'hT', c_c)
                if SCP >= 2:
                    linear(evin_d, 8, [(2064 + 1024 + j * 128, 128)], hT, 'hT', c_h)
                if SCP >= 3:
                    linear(evin_d, 8, [(2064 + j * 128, 128)], hT, 'hT', c_b)

            if STAGE < 2:
                return
            def T2(n, s, d=F32):
                t_ = T(n, s, d)
                return [t_, t_]
            gbtok = T2("gbtok", [64, 16])
            gcc = T2("gcc", [64, 4])
            Dgb = T2("Dgb", [64, 2, 4, 64])
            Dm = T2("Dm", [64, 4, 64])
            t1 = T2("t1", [64, 4, 64])
            t2_ = T2("t2", [64, 4, 64])
            E1 = T2("E1", [64, 4, 64])
            E2 = T2("E2", [64, 4, 64])
            decIT = T2("decIT", [64, 4, 64])
            Nn = [T2("Nn%d" % k, [64, 4, 64]) for k in range(2)]
            NTn = [T2("NTn%d" % k, [64, 4, 64]) for k in range(2)]
            TT = T2("TT", [64, 4, 64])
            TTb = T2("TTb", [64, 4, 64], BF16)
            intraT = T2("intraT", [64, 4, 64], BF16)
            ktok = T2("ktok", [64, 4, 128], BF16)
            vtok = T2("vtok", [64, 4, 128], BF16)
            vb = T2("vb", [64, 4, 128], BF16)
            kbg = T2("kbg", [64, 4, 128], BF16)
            kd = T2("kd", [64, 4, 128], BF16)
            bg = T2("bg", [64, 4])
            ekd = T2("ekd", [64, 4])
            egr = T2("egr", [128, 4, 64])
            qdT = T2("qdT", [128, 4, 64], BF16)
            u_sb = T2("u_sb", [64, 4, 128])
            wTb = T2("wTb", [128, 4, 64], BF16)
            e_b = T2("e_b", [64, 4, 128], BF16)

            def v3(ap, a):
                return ap.rearrange("p (a b) -> p a b", a=a)

            for d in range(2):
                mS = cst[0:64, C_SL:C_SL + 64] if d == 0 else cst[0:64, C_SU:C_SU + 64]
                mST = cst[0:64, C_SU:C_SU + 64] if d == 0 else cst[0:64, C_SL:C_SL + 64]
                mIT = cst[0:64, C_U:C_U + 64] if d == 0 else cst[0:64, C_L:C_L + 64]
                cum = cst[0:64, C_U:C_U + 64] if d == 0 else cst[0:64, C_L:C_L + 64]
                last = 63 if d == 0 else 0
                I64 = cst[0:64, C_ID:C_ID + 64]

                def bc1(ap2):
                    return ap2.unsqueeze(1).to_broadcast([64, 4, 64])

                def bc2(ap2, n=64):
                    return ap2.unsqueeze(2).to_broadcast([64, 4, n])
                dma('sp', Sf[:, :, :], s0_d[d].rearrange("h k v -> k h v"), w=['Sf'])
                cp(Sb[:, :, :], Sf[:, :, :], r=['Sf'], w=['Sb'], eng='act')
                order = list(range(16)) if d == 0 else list(range(15, -1, -1))
                GCH = int(os.environ.get('GCH', '16'))
                GP = int(os.environ.get('GP', '99'))
                for step, c in enumerate(order[:GCH]):
                    p = 0
                    K = lambda n: (n, p)
                    csl = slice(c * 64, (c + 1) * 64)
                    b0 = nextps()
                    tr(ps[b0][0:64, 0:8], gT[0:8, csl], ident_f[0:8, 0:8], r=['gT', 'cst'], w=[psk[b0]])
                    tr(ps[b0][0:64, 8:16], betaT[0:8, csl], ident_f[0:8, 0:8], r=['betaT', 'cst'], w=[psk[b0]])
                    cp(gbtok[p][:, :], ps[b0][0:64, 0:16], r=[psk[b0]], w=[K('gbtok')], eng='act')
                    gtok = gbtok[p][:, d * 4:d * 4 + 4]
                    btok = gbtok[p][:, 8 + d * 4:8 + d * 4 + 4]
                    b1 = nextps()
                    mm(ps[b1][0:64, 0:4], cum, gtok, r=['cst', K('gbtok')], w=[psk[b1]])
                    cp(gcc[p][:, :], ps[b1][0:64, 0:4], r=[psk[b1]], w=[K('gcc')], eng='act')
                    if GP < 1:
                        continue
                    tt(Dgb[p][:, 0, :, :], bc1(I64), bc2(gcc[p][:, :]), ALU.mult, r=['cst', K('gcc')], w=[K('Dgb')])
                    tt(Dgb[p][:, 1, :, :], bc1(I64), bc2(btok), ALU.mult, r=['cst', K('gbtok')], w=[K('Dgb')])
                    bR = nextps()
                    mm(ps[bR][:, 0:256], ones_f[0:64, 0:128], Dgb[p][:, 0, :, :].rearrange('p a b -> p (a b)'), r=['ones_f', K('Dgb')], w=[psk[bR]])
                    mm(ps[bR][0:64, 256:512], ones_f[0:64, 0:64], Dgb[p][:, 1, :, :].rearrange('p a b -> p (a b)'), r=['ones_f', K('Dgb')], w=[psk[bR]])
                    grow = v3(ps[bR][:, 0:256], 4)
                    brow = v3(ps[bR][0:64, 256:512], 4)
                    tt(Dm[p][:, :, :], bc2(gcc[p][:, :]), grow[0:64], ALU.subtract, r=[K('gcc'), psk[bR]], w=[K('Dm')])
                    ts(t1[p][:, :, :], Dm[p][:, :, :], 0.0, ALU.min, r=[K('Dm')], w=[K('t1')])
                    ts(t2_[p][:, :, :], Dm[p][:, :, :], -1.0, ALU.mult, r=[K('Dm')], w=[K('t2')], s2=0.0, op1=ALU.min)
                    act(E1[p][:, :, :], t1[p][:, :, :], AF.Exp, r=[K('t1')], w=[K('E1')])
                    act(E2[p][:, :, :], t2_[p][:, :, :], AF.Exp, r=[K('t2')], w=[K('E2')])
                    act(egr[p][:, :, :], grow, AF.Exp, r=[psk[bR]], w=[K('egr')])
                    tt(ekd[p][:, :], grow[0:64, :, last], gcc[p][:, :], ALU.subtract, r=[psk[bR], K('gcc')], w=[K('ekd')])
                    act(ekd[p][:, :], ekd[p][:, :], AF.Exp, r=[K('ekd')], w=[K('ekd')])
                    act(bg[p][:, :], gcc[p][:, :], AF.Exp, r=[K('gcc')], w=[K('bg')])
                    tt(bg[p][:, :], bg[p][:, :], btok, ALU.mult, r=[K('bg'), K('gbtok')], w=[K('bg')])
                    tt(decIT[p][:, :, :], E2[p][:, :, :], bc1(mIT), ALU.mult, r=[K('E2'), 'cst'], w=[K('decIT')])
                    tt(E1[p][:, :, :], E1[p][:, :, :], bc1(mS), ALU.mult, r=[K('E1'), 'cst'], w=[K('E1')])
                    tt(E2[p][:, :, :], E2[p][:, :, :], bc1(mST), ALU.mult, r=[K('E2'), 'cst'], w=[K('E2')])
                    if GP < 2:
                        continue
                    bK = nextps()
                    for h in range(4):
                        mm(ps[bK][0:64, h * 64:(h + 1) * 64], kT[:, h, csl], kT[:, h, csl], r=[('kT', h)], w=[psk[bK]])
                        mm(ps[bK][0:64, 256 + h * 64:256 + (h + 1) * 64], kT[:, h, csl], qT[:, h, csl],
                           r=[('kT', h), ('qT', h)], w=[psk[bK]])
                    pKK = v3(ps[bK][0:64, 0:256], 4)
                    pQK = v3(ps[bK][0:64, 256:512], 4)
                    N0, NT0 = Nn[0][p], NTn[0][p]
                    tt(t1[p][:, :, :], pKK, E1[p][:, :, :], ALU.mult, r=[psk[bK], K('E1')], w=[K('t1')])
                    stt(N0[:, :, :], t1[p][:, :, :], -1.0, bc2(btok), ALU.mult, ALU.mult,
                        r=[K('t1'), K('gbtok')], w=[K('Nn0')])
                    tt(t2_[p][:, :, :], pKK, E2[p][:, :, :], ALU.mult, r=[psk[bK], K('E2')], w=[K('t2')])
                    stt(NT0[:, :, :], t2_[p][:, :, :], -1.0, brow, ALU.mult, ALU.mult,
                        r=[K('t2'), psk[bR]], w=[K('NTn0')])
                    tt(TT[p][:, :, :], NT0[:, :, :], bc1(I64), ALU.add, r=[K('NTn0'), 'cst'], w=[K('TT')])
                    tt(intraT[p][:, :, :], pQK, decIT[p][:, :, :], ALU.mult, r=[psk[bK], K('decIT')], w=[K('intraT')])
                    if GP < 3:
                        continue
                    NLEV = int(os.environ.get('NLEV', '5'))
                    NPART = int(os.environ.get('NPART', '3'))
                    for lev in range(1, NLEV + 1):
                        a, bprev = lev % 2, (lev - 1) % 2
                        Np, NTp = Nn[bprev][p], NTn[bprev][p]
                        Nc, NTc = Nn[a][p], NTn[a][p]
                        bn = nextps()
                        for h in range(4):
                            mm(ps[bn][0:64, h * 64:(h + 1) * 64], NTp[:, h, :], Np[:, h, :],
                               r=[K('Nn%d' % bprev), K('NTn%d' % bprev)], w=[psk[bn]])
                        if lev < 5 and NPART >= 2:
                            for h in range(4):
                                mm(ps[bn][0:64, 256 + h * 64:256 + (h + 1) * 64], Np[:, h, :], NTp[:, h, :],
                                   r=[K('Nn%d' % bprev), K('NTn%d' % bprev)], w=[psk[bn]])
                        cp(Nc[:, :, :], v3(ps[bn][0:64, 0:256], 4), r=[psk[bn]], w=[K('Nn%d' % a)], eng='act')
                        if lev < 5 and NPART >= 2:
                            cp(NTc[:, :, :], v3(ps[bn][0:64, 256:512], 4), r=[psk[bn]], w=[K('NTn%d' % a)], eng='act')
                        if NPART < 3:
                            continue
                        bt = nextps()
                        for h in range(4):
                            mm(ps[bt][0:64, h * 64:(h + 1) * 64], Nc[:, h, :], TT[p][:, h, :],
                               r=[K('Nn%d' % a), K('TT')], w=[psk[bt]])
                        tt(TT[p][:, :, :], TT[p][:, :, :], v3(ps[bt][0:64, 0:256], 4), ALU.add,
                           r=[K('TT'), psk[bt]], w=[K('TT')])
                    cp(TTb[p][:, :, :], TT[p][:, :, :], r=[K('TT')], w=[K('TTb')], eng='act')
                    if GP < 4:
                        continue
                    bkv = nextps()
                    pkv = ps[bkv][:, :].bitcast(BF16)
                    for h in range(4):
                        tr(pkv[0:64, h * 128:(h + 1) * 128], kT[:, h, csl], ident_bf[:, :], r=[('kT', h), 'ident_bf'], w=[psk[bkv]])
                        tr(pkv[0:64, 512 + h * 128:512 + (h + 1) * 128], vT[:, h, csl], ident_bf[:, :],
                           r=[('vT', h), 'ident_bf'], w=[psk[bkv]])
                    cp(ktok[p][:, :, :], v3(pkv[0:64, 0:512], 4), r=[psk[bkv]], w=[K('ktok')], eng='act')
                    cp(vtok[p][:, :, :], v3(pkv[0:64, 512:1024], 4), r=[psk[bkv]], w=[K('vtok')], eng='act')
                    tt(vb[p][:, :, :], vtok[p][:, :, :], bc2(btok, 128), ALU.mult, r=[K('vtok'), K('gbtok')], w=[K('vb')])
                    tt(kbg[p][:, :, :], ktok[p][:, :, :], bc2(bg[p][:, :], 128), ALU.mult, r=[K('ktok'), K('bg')], w=[K('kbg')])
                    tt(kd[p][:, :, :], ktok[p][:, :, :], bc2(ekd[p][:, :], 128), ALU.mult, r=[K('ktok'), K('ekd')], w=[K('kd')])
                    tt(qdT[p][:, :, :], qT[:, :, csl], egr[p][:, :, :], ALU.mult,
                       r=[('qT', 0), ('qT', 1), ('qT', 2), ('qT', 3), K('egr')], w=[K('qdT')])
                    if GP < 5:
                        continue
                    bu = nextps()
                    bw = nextps()
                    for h in range(4):
                        mm(ps[bu][0:64, h * 128:(h + 1) * 128], TTb[p][:, h, :], vb[p][:, h, :],
                           r=[K('TTb'), K('vb')], w=[psk[bu]])
                        mm(ps[bw][:, h * 64:(h + 1) * 64], kbg[p][:, h, :], TTb[p][:, h, :],
                           r=[K('kbg'), K('TTb')], w=[psk[bw]])
                    cp(u_sb[p][:, :, :], v3(ps[bu][0:64, :], 4), r=[psk[bu]], w=[K('u_sb')], eng='act')
                    cp(wTb[p][:, :, :], v3(ps[bw][:, 0:256], 4), r=[psk[bw]], w=[K('wTb')], eng='act')
                    if GP < 6:
                        continue
                    be = nextps()
                    for h in range(4):
                        mm(ps[be][0:64, h * 128:(h + 1) * 128], wTb[p][:, h, :], Sb[:, h, :], r=[K('wTb'), 'Sb'], w=[psk[be]])
                    tt(e_b[p][:, :, :], u_sb[p][:, :, :], v3(ps[be][0:64, :], 4), ALU.subtract,
                       r=[K('u_sb'), psk[be]], w=[K('e_b')])
                    bo = nextps()
                    for h in range(4):
                        mm(ps[bo][:, h * 64:(h + 1) * 64], Sb[:, h, :], qdT[p][:, h, :], start=True, stop=False,
                           r=['Sb', K('qdT')], w=[psk[bo]])
                        mm(ps[bo][:, h * 64:(h + 1) * 64], e_b[p][:, h, :], intraT[p][:, h, :], start=False, stop=True,
                           r=[K('e_b'), K('intraT')], w=[psk[bo]])
                    po = v3(ps[bo][:, 0:256], 4)
                    if d == 0:
                        cp(oacc[:, :, csl], po, r=[psk[bo]], w=[('oacc', c)], eng='act')
                    else:
                        tt(oacc[:, :, csl], oacc[:, :, csl], po, ALU.add, r=[psk[bo], ('oacc', c)], w=[('oacc', c)])
                    bs = nextps()
                    for h in range(4):
                        mm(ps[bs][:, h * 128:(h + 1) * 128], kd[p][:, h, :], e_b[p][:, h, :], r=[K('kd'), K('e_b')], w=[psk[bs]])
                    for h in range(4):
                        stt(Sf[:, h, :], Sf[:, h, :], egr[p][:, h, last:last + 1], ps[bs][:, h * 128:(h + 1) * 128],
                            ALU.mult, ALU.add, r=['Sf', K('egr'), psk[bs]], w=['Sf'])
                    seg_end = (c % 4 == 3) if d == 0 else (c % 4 == 0)
                    if seg_end:
                        seg = c // 4
                        dma('sp', gso_d[d, seg].rearrange("h k v -> k h v"), Sf[:, :, :], r=['Sf'])
                        ts(Sf[:, :, :], Sf[:, :, :], flag[:, 0:1], ALU.mult, r=['Sf', 'flag'], w=['Sf'])
                    cp(Sb[:, :, :], Sf[:, :, :], r=['Sf'], w=['Sb'], eng='act')

            if STAGE < 3:
                return
            for h in range(4):
                act(qsq[:, :], oacc[:, h, :], AF.Square, r=[('oacc', c) for c in range(16)], w=['qsq'])
                for t2 in range(2):
                    bb = nextps()
                    tsl = slice(t2 * 512, (t2 + 1) * 512)
                    mm(ps[bb][:, :], ones_bf[:, :], qsq[:, tsl], r=['ones_bf', 'qsq'], w=[psk[bb]])
                    act(rinv[:, tsl], ps[bb][:, :], AF.Sqrt, r=[psk[bb], 'epsb'], w=['rinv'],
                        bias=epsb[:, 0:1], scale=1.0 / 128)
                rcp(rinv[:, :], rinv[:, :], r=['rinv'], w=['rinv'])
                stt(sctmp[:, :], oacc[:, h, :], ppc('gdn_norm')[:, 0:1], rinv[:, :], ALU.mult, ALU.mult,
                    r=[('oacc', c) for c in range(16)] + ['pp', 'rinv'], w=['sctmp'])
                tt(mixT[:, h, :], sctmp[:, :], mixT[:, h, :], ALU.mult, r=['sctmp', ('mixT', h)], w=[('mixT', h)])

            def c_out(c0, cw, th, b):
                fc = c0 // 128
                tsl = slice(th * 512, (th + 1) * 512)
                stt(x_sb[:, fc, tsl], ps[b][:, :], modv[:, 0, 16 + fc:17 + fc], x_sb[:, fc, tsl],
                    ALU.mult, ALU.add, r=[psk[b], 'modv', ('x', fc)], w=[('x', fc)])
            linear(evout_d, 8, [(i * 512, 512) for i in range(2)], mixT, 'mixT', c_out)

        def layer1_mixer():
            with contextlib.ExitStack() as so:
                mix1T = so.enter_context(nc.sbuf_tensor("sb_mix1T", [128, 8, NT], BF16))
                with contextlib.ExitStack() as sc:
                    alloc_work(sc, 'p3')
                    attention_phase(sc, mix1T)
                    S.flush()
                if L1PART >= 2:
                    with contextlib.ExitStack() as sr:
                        rwkv_phases(sr, mix1T)
                with contextlib.ExitStack() as sc:
                    alloc_work(sc, 'p6')

                    def c_out(c0, cw, th, b):
                        fc = c0 // 128
                        tsl = slice(th * 512, (th + 1) * 512)
                        stt(x_sb[:, fc, tsl], ps[b][:, :], modv[:, 1, 16 + fc:17 + fc], x_sb[:, fc, tsl],
                            ALU.mult, ALU.add, r=[psk[b], 'modv', ('x', fc)], w=[('x', fc)])
                    linear(odout_d, 8, [(i * 512, 512) for i in range(2)], mix1T, 'mix1T', c_out)
                    S.flush()

        def attention_phase(sc, mix1T):
            T = lambda n, s, d=F32: sc.enter_context(nc.sbuf_tensor('sb_a_' + n, list(s), d))
            qTr = T("qTr", [128, 4, NT], BF16)
            kpad = T("kpad", [128, 2, NT + 256], BF16)
            vTb = T("vTb", [128, NT], BF16)
            vtokp = T("vtokp", [128, 10, 128], BF16)
            ckT = T("ckT", [128, 2, 512], BF16)
            cvt = T("cvt", [128, 4, 128], BF16)
            cosT = T("cosT", [128, NT])
            sinT = T("sinT", [128, NT])
            Rm = T("Rm", [128, 128])
            xf = T("xf", [128, NT])
            xr = T("xr", [128, NT])
            mb = [T("mb%d" % i, [128, 896]) for i in range(2)]
            scs = T("scs", [128, 896])
            Pb = T("Pb", [128, 896], BF16)
            PT = T("PT", [128, 7, 128], BF16)
            atok = T("atok", [128, 512], BF16)
            sm = T("sm", [128, 8])

            dma('sp', cosT[:, :], ropec_d[:, :], w=['cosT'])
            dma('sp', sinT[:, :], ropes_d[:, :], w=['sinT'])
            dma('sp', Rm[:, :], rm_d[:, :], w=['Rm'])
            for kvh in range(2):
                dma('pool', ckT[:, kvh, :], ckT_d[kvh], w=['ckT'])
            dma('pool', cvt[:, :, :], cvt_d[:, :, :], w=['cvt'])
            mset(kpad[:, :, 0:128], 0.0, w=['kpad'])
            mset(kpad[:, :, 128 + NT:256 + NT], 0.0, w=['kpad'])
            mset(vtokp[:, 0, :], 0.0, w=['vtokp'])
            mset(vtokp[:, 9, :], 0.0, w=['vtokp'])

            norm_mod(gsA[:, 1, :], modv[:, 1, 0:8], hT, 'hT')

            def rope(dst_ap_fn, dkey):
                for t2 in range(2):
                    tsl = slice(t2 * 512, (t2 + 1) * 512)
                    bb = nextps()
                    mm(ps[bb][:, :], Rm[:, :], xf[:, tsl], r=['Rm', 'xf'], w=[psk[bb]])
                    tt(xr[:, tsl], ps[bb][:, :], sinT[:, tsl], ALU.mult, r=[psk[bb], 'sinT'], w=['xr'])
                    tt(xf[:, tsl], xf[:, tsl], cosT[:, tsl], ALU.mult, r=['xf', 'cosT'], w=['xf'])
                    tt(dst_ap_fn(tsl), xf[:, tsl], xr[:, tsl], ALU.add, r=['xf', 'xr'], w=[dkey])

            def c_q(c0, cw, th, b):
                a = c0 // 128
                cp(xf[:, th * 512:(th + 1) * 512], ps[b][:, :], r=[psk[b]], w=['xf'], eng='act')
                if th == 1:
                    rope(lambda tsl: qTr[:, a, tsl], 'qTr')
            linear(odin_d, 8, [(0, 512)], hT, 'hT', c_q)

            def c_k(c0, cw, th, b):
                kvh = (c0 - 512) // 128
                cp(xf[:, th * 512:(th + 1) * 512], ps[b][:, :], r=[psk[b]], w=['xf'], eng='act')
                if th == 1:
                    dma('sp', kvo_d[0, kvh * 64:(kvh + 1) * 64, :], xf[0:64, :], r=['xf'])
                    rope(lambda tsl: kpad[:, kvh, 128 + tsl.start:128 + tsl.stop], 'kpad')
            linear(odin_d, 8, [(512, 256)], hT, 'hT', c_k)

            def c_v(c0, cw, th, b):
                cp(xf[:, th * 512:(th + 1) * 512], ps[b][:, :], r=[psk[b]], w=['xf'], eng='act')
                if th == 1:
                    dma('sp', kvo_d[1, :, :], xf[:, :], r=['xf'])
                    cp(vTb[:, :], xf[:, :], r=['xf'], w=['vTb'], eng='act')
                    for blk in range(8):
                        bb = nextps()
                        pv_ = ps[bb][:, :].bitcast(BF16)
                        tr(pv_[:, 0:128], vTb[:, blk * 128:(blk + 1) * 128], ident_bf[:, :], r=['vTb', 'ident_bf'], w=[psk[bb]])
                        cp(vtokp[:, 1 + blk, :], pv_[:, 0:128], r=[psk[bb]], w=['vtokp'], eng='act')
            linear(odin_d, 8, [(768, 128)], hT, 'hT', c_v)

            sink = ppc('sink')
            for n in range(8):
                mbi = n % 2
                dma('sp', mb[mbi][:, :], maskb_d[n], w=[('mb', mbi)])
                for h in range(8):
                    kvh = h // 4
                    hp = h % 2
                    a = h // 2
                    pb_ = slice(hp * 64, (hp + 1) * 64)
                    qap = qTr[pb_, a, n * 128:(n + 1) * 128]
                    b1 = nextps()
                    b2 = nextps()
                    mm(ps[b1][:, 0:384], qap, kpad[pb_, kvh, n * 128:n * 128 + 384], r=['qTr', 'kpad'], w=[psk[b1]])
                    mm(ps[b2][:, :], qap, ckT[pb_, kvh, :], r=['qTr', 'ckT'], w=[psk[b2]])
                    stt(scs[:, 0:384], ps[b1][:, 0:384], 0.125, mb[mbi][:, 0:384], ALU.mult, ALU.add,
                        r=[psk[b1], ('mb', mbi)], w=['scs'])
                    stt(scs[:, 384:896], ps[b2][:, :], 0.125, mb[mbi][:, 384:896], ALU.mult, ALU.add,
                        r=[psk[b2], ('mb', mbi)], w=['scs'])
                    S.op('dve', lambda e: e.reduce_max(sm[:, 0:1], scs[:, :], mybir.AxisListType.X), ['scs'], ['sm'])
                    tt(sm[:, 1:2], sm[:, 0:1], sink[:, h:h + 1], ALU.max, r=['sm', 'pp'], w=['sm'])
                    ts(sm[:, 2:3], sm[:, 1:2], -1.0, ALU.mult, r=['sm'], w=['sm'])
                    S.op('act', lambda e: e.activation(Pb[:, :], scs[:, :], AF.Exp, bias=sm[:, 2:3], accum_out=sm[:, 3:4]),
                         ['scs', 'sm'], ['Pb', 'sm'])
                    act(sm[:, 4:5], sink[:, h:h + 1], AF.Exp, r=['pp', 'sm'], w=['sm'], bias=sm[:, 2:3])
                    tt(sm[:, 5:6], sm[:, 3:4], sm[:, 4:5], ALU.add, r=['sm'], w=['sm'])
                    rcp(sm[:, 6:7], sm[:, 5:6], r=['sm'], w=['sm'])
                    bt_ = nextps()
                    ptp = ps[bt_][:, :].bitcast(BF16)
                    for kb in range(7):
                        tr(ptp[:, kb * 128:(kb + 1) * 128], Pb[:, kb * 128:(kb + 1) * 128], ident_bf[:, :],
                           r=['Pb', 'ident_bf'], w=[psk[bt_]])
                    cp(PT[:, :, :], ptp[:, 0:896].rearrange("p (a b) -> p a b", a=7), r=[psk[bt_]], w=['PT'], eng='act')
                    bo_ = nextps()
                    for kb in range(7):
                        if kb < 3:
                            vap = vtokp[:, n + kb, kvh * 64:(kvh + 1) * 64]
                            rk = 'vtokp'
                        else:
                            vap = cvt[:, kb - 3, kvh * 64:(kvh + 1) * 64]
                            rk = 'cvt'
                        mm(ps[bo_][:, 0:64], PT[:, kb, :], vap, start=(kb == 0), stop=(kb == 6), r=['PT', rk], w=[psk[bo_]])
                    act(atok[:, h * 64:(h + 1) * 64], ps[bo_][:, 0:64], AF.Identity, r=[psk[bo_], 'sm'], w=['atok'],
                        scale=sm[:, 6:7])
                ba_ = nextps()
                pa_ = ps[ba_][:, :].bitcast(BF16)
                for a in range(4):
                    tr(pa_[:, a * 128:(a + 1) * 128], atok[:, a * 128:(a + 1) * 128], ident_bf[:, :],
                       r=['atok', 'ident_bf'], w=[psk[ba_]])
                cp(mix1T[:, 0:4, n * 128:(n + 1) * 128], pa_[:, 0:512].rearrange("p (a b) -> p a b", a=4),
                   r=[psk[ba_]], w=[('mix1T', 0), ('mix1T', 1), ('mix1T', 2), ('mix1T', 3)], eng='act')

        def rwkv_phases(sr, mix1T):
            TR = lambda n, s, d=F32: sr.enter_context(nc.sbuf_tensor('sb_r_' + n, list(s), d))
            r_b = TR("r_b", [128, 4, NT], BF16)
            k_b = TR("k_b", [128, 4, NT], BF16)
            kk_b = TR("kk_b", [128, 4, NT], BF16)
            v_b = TR("v_b", [128, 4, NT], BF16)
            a_b = TR("a_b", [128, 2, 4, NT], BF16)
            wlt = TR("wlt", [128, NT], BF16)
            bon = TR("bon", [128, 4, NT], BF16)
            gate = TR("gate", [128, 4, NT], BF16)
            wup = TR("wup", [128, 512], BF16)
            bones = TR("bones", [128, 128], BF16)
            wmid = TR("wmid", [128, 15])
            dma('pool', wup[:, :], wup_d[:, :], w=['wup'])
            dma('pool', bones[:, :], bones_d[:, :], w=['bones'])
            mu0 = ppc('mu', 0, 15)
            mu1 = ppc('mu', 15, 15)
            tt(wmid[:, :], mu0, mu1, ALU.add, r=['pp'], w=['wmid'])
            ts(wmid[:, :], wmid[:, :], -1.0, ALU.mult, r=['wmid'], w=['wmid'], s2=1.0, op1=ALU.add)

            with contextlib.ExitStack() as sc:
                alloc_work(sc, 'p4')
                T = lambda n, s, d=F32: sc.enter_context(nc.sbuf_tensor('sb_rp_' + n, list(s), d))
                pad = [T("pad%d" % i, [128, 4, 258]) for i in range(2)]
                xs_ = T("xs", [128, 4, 256])
                t1 = T("t1", [128, NT])
                t2 = T("t2", [128, NT])
                sqh = T("sqh", [128, NT], BF16)
                alb = T("alb", [128, NT], BF16)
                glb = T("glb", [128, NT], BF16)
                aup = T("aup", [128, 512], BF16)
                gup = T("gup", [128, 512], BF16)
                dma('pool', aup[:, :], aup_d[:, :], w=['aup'])
                dma('pool', gup[:, :], gup_d[:, :], w=['gup'])
                for i in range(2):
                    mset(pad[i][:, :, :], 0.0, w=[('rpad', i)])
                norm_mod(gsA[:, 1, :], modv[:, 1, 0:8], hT, 'hT')
                xsf = xs_[:, :, :].rearrange("p s t -> p (s t)")

                def hsum(dst_ps_fn, src_bf):
                    for t2_ in range(2):
                        bb = nextps()
                        mm(ps[bb][:, :], bones[:, :], src_bf[:, t2_ * 512:(t2_ + 1) * 512], r=['bones', 'sqh'], w=[psk[bb]])
                        dst_ps_fn(t2_, bb)

                def c_rw(c0, cw, th, b):
                    j = (c0 - 896) // 128
                    pi = j % 2
                    p = pad[pi]
                    cp(p[:, 2 * th:2 * th + 2, 1:257], ps[b][:, :].rearrange("p (s t) -> p s t", s=2),
                       r=[psk[b]], w=[('rpad', pi)], eng='act')
                    if th == 0:
                        return
                    ts(p[:, 1:4, 0:1], p[:, 0:3, 256:257], flag[:, 0:1], ALU.mult, r=[('rpad', pi), 'flag'], w=[('rpad', pi)])
                    ts(p[:, 0:3, 257:258], p[:, 1:4, 1:2], flag[:, 0:1], ALU.mult, r=[('rpad', pi), 'flag'], w=[('rpad', pi)])
                    ts(xs_[:, :, :], p[:, :, 0:256], mu0[:, j:j + 1], ALU.mult, r=[('rpad', pi), 'pp'], w=['xs'])
                    stt(xs_[:, :, :], p[:, :, 1:257], wmid[:, j:j + 1], xs_[:, :, :], ALU.mult, ALU.add,
                        r=[('rpad', pi), 'wmid', 'xs'], w=['xs'])
                    stt(xs_[:, :, :], p[:, :, 2:258], mu1[:, j:j + 1], xs_[:, :, :], ALU.mult, ALU.add,
                        r=[('rpad', pi), 'pp', 'xs'], w=['xs'])
                    cq = j % 4
                    if j < 4:
                        cp(r_b[:, cq, :], xsf, r=['xs'], w=[('r_b', cq)], eng='act')
                    elif j < 8:
                        cp(k_b[:, cq, :], xsf, r=['xs'], w=[('k_b', cq)], eng='act')
                        ts(t1[:, :], xsf, ppc('k_k')[:, cq:cq + 1], ALU.mult, r=['xs', 'pp'], w=['t1'])
                        act(sqh[:, :], t1[:, :], AF.Square, r=['t1'], w=['sqh'])

                        def d1(t2_, bb):
                            act(t2[:, t2_ * 512:(t2_ + 1) * 512], ps[bb][:, :], AF.Sqrt, r=[psk[bb], 'epsb'], w=['t2'],
                                bias=epsb[:, 1:2])
                        hsum(d1, sqh)
                        rcp(t2[:, :], t2[:, :], r=['t2'], w=['t2'])
                        tt(kk_b[:, cq, :], t1[:, :], t2[:, :], ALU.mult, r=['t1', 't2'], w=[('kk_b', cq)])
                    elif j < 12:
                        cp(v_b[:, cq, :], xsf, r=['xs'], w=[('v_b', cq)], eng='act')
                    elif j == 12:
                        act(wlt[:, :], xsf, AF.Tanh, r=['xs'], w=['wlt'])
                    elif j == 13:
                        cp(alb[:, :], xsf, r=['xs'], w=['alb'], eng='act')
                        for d in range(2):
                            dsl = slice(d * 64, (d + 1) * 64)
                            for cq2 in range(4):
                                for t2_ in range(2):
                                    bb = nextps()
                                    mm(ps[bb][:, :], aup[dsl, cq2 * 128:(cq2 + 1) * 128], alb[dsl, t2_ * 512:(t2_ + 1) * 512],
                                       r=['aup', 'alb'], w=[psk[bb]])
                                    act(a_b[:, d, cq2, t2_ * 512:(t2_ + 1) * 512], ps[bb][:, :], AF.Sigmoid,
                                        r=[psk[bb], 'pp'], w=[('a_b', d, cq2)], bias=ppc('a0')[:, d * 4 + cq2:d * 4 + cq2 + 1])
                    else:
                        act(glb[:, :], xsf, AF.Sigmoid, r=['xs'], w=['glb'])
                        for cq2 in range(4):
                            for t2_ in range(2):
                                bb = nextps()
                                mm(ps[bb][:, :], gup[:, cq2 * 128:(cq2 + 1) * 128], glb[:, t2_ * 512:(t2_ + 1) * 512],
                                   r=['gup', 'glb'], w=[psk[bb]])
                                cp(gate[:, cq2, t2_ * 512:(t2_ + 1) * 512], ps[bb][:, :], r=[psk[bb]], w=[('gate', cq2)], eng='act')
                linear(odin_d, 8, [(896 + j * 128, 128) for j in range(15)], hT, 'hT', c_rw)
                for cq in range(4):
                    for d in range(2):
                        ts(t1[:, :], a_b[:, d, cq, :], -1.0, ALU.add, r=[('a_b', d, cq), 'pp'], w=['t1'],
                           s2=ppc('k_a')[:, cq:cq + 1], op1=ALU.mult)
                        stt(t2[:, :] if d == 0 else t1[:, :], t1[:, :], 1.0, k_b[:, cq, :], ALU.add, ALU.mult,
                            r=['t1', ('k_b', cq)], w=['t2' if d == 0 else 't1'])
                    tt(t2[:, :], t2[:, :], t1[:, :], ALU.add, r=['t1', 't2'], w=['t2'])
                    stt(sqh[:, :], t2[:, :], ppc('r_k')[:, cq:cq + 1], r_b[:, cq, :], ALU.mult, ALU.mult,
                        r=['t2', 'pp', ('r_b', cq)], w=['sqh'])

                    def d2(t2_, bb):
                        tsl = slice(t2_ * 512, (t2_ + 1) * 512)
                        tt(bon[:, cq, tsl], ps[bb][:, :], v_b[:, cq, tsl], ALU.mult, r=[psk[bb], ('v_b', cq)], w=[('bon', cq)])
                    hsum(d2, sqh)
                S.flush()

            if L1PART < 3:
                return
            with contextlib.ExitStack() as sc:
                state['drain'] = True
                state['pemode'] = None
                T = lambda n, s, d=F32: sc.enter_context(nc.sbuf_tensor('sb_rs_' + n, list(s), d))
                lw = T("lw", [128, 4, NT])
                yacc = T("yacc", [128, 4, NT])
                Pf = T("Pf", [128, 4, 64])
                Pbf = T("Pbf", [128, 4, 64], BF16)
                Lf = T("Lf", [128, 4, 64])
                Lam = T("Lam", [128, 4, 64])
                e1 = T("e1", [128, 4, 64])
                e2 = T("e2", [128, 4, 64])
                e3 = T("e3", [128, 4, 64])
                e4 = T("e4", [128, 4, 64])
                eTot = T("eTot", [128, 4])
                ka = T("ka", [128, 4, 64])
                k2 = T("k2", [128, 4, 64])
                rt = T("rt", [128, 4, 64], BF16)
                bt = T("bt", [128, 4, 64], BF16)
                at = T("at", [128, 4, 64], BF16)
                kt = T("kt", [128, 4, 64], BF16)
                aG = T("aG", [128, 4, 64], BF16)
                kG = T("kG", [128, 4, 64], BF16)
                Nn = [T("Nn%d" % i, [64, 8, 64]) for i in range(2)]
                NTn = [T("NTn%d" % i, [64, 8, 64]) for i in range(2)]
                TT = T("TT", [64, 8, 64])
                TTb = T("TTb", [64, 8, 64], BF16)
                Tn = T("Tn", [64, 8, 64])
                O32a = T("O32a", [64, 8, 64])
                O32Ta = T("O32Ta", [64, 8, 64])
                O64a = T("O64a", [64, 8, 64])
                BTm = T("BTm", [64, 8, 64], BF16)
                RaT = T("RaT", [64, 8, 64], BF16)
                RkT = T("RkT", [64, 8, 64], BF16)
                btok = T("btok", [64, 4, 128], BF16)
                aGtok = T("aGtok", [64, 4, 128], BF16)
                kGtok = T("kGtok", [64, 4, 128], BF16)
                vtok = T("vtok", [64, 4, 128], BF16)
                BVb = T("BVb", [64, 8, 64], BF16)
                U0 = T("U0", [64, 8, 64])
                Ub = T("Ub", [64, 8, 64], BF16)
                WmTb = T("WmTb", [128, 4, 64], BF16)
                onesc = T("onesc", [128, 64])
                mset(onesc[:, :], 1.0, w=['onesc'])
                I64 = cst[0:64, C_ID:C_ID + 64]
                HORD = [0, 2, 4, 6, 1, 3, 5, 7]

                def v3(ap, a):
                    return ap.rearrange("p (a b) -> p a b", a=a)

                def bc8(ap2):
                    return ap2.unsqueeze(1).to_broadcast([64, 8, 64])

                for d in range(2):
                    dsl = slice(d * 64, (d + 1) * 64)
                    mS = cst[0:64, C_SL:C_SL + 64] if d == 0 else cst[0:64, C_SU:C_SU + 64]
                    mST = cst[0:64, C_SU:C_SU + 64] if d == 0 else cst[0:64, C_SL:C_SL + 64]
                    mIT = cst[0:64, C_U:C_U + 64] if d == 0 else cst[0:64, C_L:C_L + 64]
                    for cq in range(4):
                        for t2_ in range(2):
                            bb = nextps()
                            mm(ps[bb][:, :], wup[dsl, cq * 128:(cq + 1) * 128], wlt[dsl, t2_ * 512:(t2_ + 1) * 512],
                               r=['wup', 'wlt'], w=[psk[bb]])
                            act(lw[:, cq, t2_ * 512:(t2_ + 1) * 512], ps[bb][:, :], AF.Sigmoid, r=[psk[bb], 'pp'], w=['lw'],
                                bias=ppc('w0')[:, d * 4 + cq:d * 4 + cq + 1])
                    ts(lw[:, :, :], lw[:, :, :], -0.6065306597126334, ALU.mult, r=['lw'], w=['lw'])
                    dma('sp', Pf[:, :, :], rs0_d[d], w=['Pf'])
                    cp(Pbf[:, :, :], Pf[:, :, :], r=['Pf'], w=['Pbf'], eng='act')
                    order = list(range(16)) if d == 0 else list(range(15, -1, -1))
                    RCH = int(os.environ.get('RCH', '16'))
                    for c in order[:RCH]:
                        csl = slice(c * 64, (c + 1) * 64)
                        lwc = lw[:, :, csl]
                        for cq in range(4):
                            scan(Lf[:, cq, :], onesc[:, :], lw[:, cq, csl], r=['lw', 'onesc'], w=['Lf'])
                        tot = Lf[:, :, 63:64]
                        if d == 0:
                            LamT, lk = Lf, 'Lf'
                        else:
                            tt(Lam[:, :, :], tot.to_broadcast([128, 4, 64]), Lf[:, :, :], ALU.subtract, r=['Lf'], w=['Lam'])
                            tt(Lam[:, :, :], Lam[:, :, :], lwc, ALU.add, r=['Lam', 'lw'], w=['Lam'])
                            LamT, lk = Lam, 'Lam'
                        act(e1[:, :, :], LamT[:, :, :], AF.Exp, r=[lk], w=['e1'])
                        tt(e2[:, :, :], LamT[:, :, :], lwc, ALU.subtract, r=[lk, 'lw'], w=['e2'])
                        act(e2[:, :, :], e2[:, :, :], AF.Exp, r=['e2'], w=['e2'])
                        act(e3[:, :, :], LamT[:, :, :], AF.Exp, r=[lk], w=['e3'], scale=-1.0)
                        tt(e4[:, :, :], tot.to_broadcast([128, 4, 64]), LamT[:, :, :], ALU.subtract, r=['Lf', lk], w=['e4'])
                        act(e4[:, :, :], e4[:, :, :], AF.Exp, r=['e4'], w=['e4'])
                        act(eTot[:, :], Lf[:, :, 63], AF.Exp, r=['Lf'], w=['eTot'])
                        stt(ka[:, :, :], kk_b[:, :, csl], -1.0, a_b[:, d, :, csl], ALU.mult, ALU.mult,
                            r=[('kk_b', q_) for q_ in range(4)] + [('a_b', d, q_) for q_ in range(4)], w=['ka'])
                        for cq in range(4):
                            ts(k2[:, cq, :], a_b[:, d, cq, csl], -1.0, ALU.add, r=[('a_b', d, cq), 'pp'], w=['k2'],
                               s2=ppc('k_a')[:, cq:cq + 1], op1=ALU.mult)
                        stt(k2[:, :, :], k2[:, :, :], 1.0, k_b[:, :, csl], ALU.add, ALU.mult,
                            r=['k2'] + [('k_b', q_) for q_ in range(4)], w=['k2'])
                        tt(rt[:, :, :], r_b[:, :, csl], e1[:, :, :], ALU.mult, r=[('r_b', q_) for q_ in range(4)] + ['e1'], w=['rt'])
                        tt(bt[:, :, :], kk_b[:, :, csl], e2[:, :, :], ALU.mult, r=[('kk_b', q_) for q_ in range(4)] + ['e2'], w=['bt'])
                        tt(at[:, :, :], ka[:, :, :], e3[:, :, :], ALU.mult, r=['ka', 'e3'], w=['at'])
                        tt(kt[:, :, :], k2[:, :, :], e3[:, :, :], ALU.mult, r=['k2', 'e3'], w=['kt'])
                        tt(aG[:, :, :], ka[:, :, :], e4[:, :, :], ALU.mult, r=['ka', 'e4'], w=['aG'])
                        tt(kG[:, :, :], k2[:, :, :], e4[:, :, :], ALU.mult, r=['k2', 'e4'], w=['kG'])
                        RP = int(os.environ.get('RP', '9'))
                        if RP < 2:
                            continue
                        bA, bAT, bBT, bRa, bRk = nextps(), nextps(), nextps(), nextps(), nextps()
                        for h in HORD:
                            cq, hp = h // 2, h % 2
                            pb_ = slice(hp * 64, (hp + 1) * 64)
                            hs = slice(h * 64, (h + 1) * 64)
                            mm(ps[bA][0:64, hs], bt[pb_, cq, :], at[pb_, cq, :], r=['bt', 'at'], w=[psk[bA]])
                            mm(ps[bAT][0:64, hs], at[pb_, cq, :], bt[pb_, cq, :], r=['bt', 'at'], w=[psk[bAT]])
                            mm(ps[bBT][0:64, hs], kt[pb_, cq, :], bt[pb_, cq, :], r=['bt', 'kt'], w=[psk[bBT]])
                            mm(ps[bRa][0:64, hs], at[pb_, cq, :], rt[pb_, cq, :], r=['rt', 'at'], w=[psk[bRa]])
                            mm(ps[bRk][0:64, hs], kt[pb_, cq, :], rt[pb_, cq, :], r=['rt', 'kt'], w=[psk[bRk]])
                        cm = lambda off: cst[0:64, off:off + 64]
                        if d == 0:
                            m16, m16T, m32, m32T, m64 = cm(C_S16L), cm(C_S16U), cm(C_O32L), cm(C_O32U), cm(C_O64L)
                        else:
                            m16, m16T, m32, m32T, m64 = cm(C_S16U), cm(C_S16L), cm(C_O32U), cm(C_O32L), cm(C_O64U)
                        pAv = v3(ps[bA][0:64, :], 8)
                        pATv = v3(ps[bAT][0:64, :], 8)
                        tt(Nn[0][:, :, :], pAv, bc8(m16), ALU.mult, r=[psk[bA], 'cst'], w=['rNn0'])
                        tt(NTn[0][:, :, :], pATv, bc8(m16T), ALU.mult, r=[psk[bAT], 'cst'], w=['rNTn0'])
                        tt(O32a[:, :, :], pAv, bc8(m32), ALU.mult, r=[psk[bA], 'cst'], w=['O32a'])
                        tt(O32Ta[:, :, :], pATv, bc8(m32T), ALU.mult, r=[psk[bAT], 'cst'], w=['O32Ta'])
                        tt(O64a[:, :, :], pAv, bc8(m64), ALU.mult, r=[psk[bA], 'cst'], w=['O64a'])
                        tt(TT[:, :, :], NTn[0][:, :, :], bc8(I64), ALU.add, r=['rNTn0', 'cst'], w=['rTT'])
                        tt(Tn[:, :, :], Nn[0][:, :, :], bc8(I64), ALU.add, r=['rNn0', 'cst'], w=['rTn'])
                        tt(BTm[:, :, :], v3(ps[bBT][0:64, :], 8), bc8(mST), ALU.mult, r=[psk[bBT], 'cst'], w=['BTm'])
                        tt(RaT[:, :, :], v3(ps[bRa][0:64, :], 8), bc8(mIT), ALU.mult, r=[psk[bRa], 'cst'], w=['RaT'])
                        tt(RkT[:, :, :], v3(ps[bRk][0:64, :], 8), bc8(mIT), ALU.mult, r=[psk[bRk], 'cst'], w=['RkT'])
                        if RP < 3:
                            continue

                        def mm8(bank, L_, R_, rk):
                            for h in range(8):
                                mm(ps[bank][0:64, h * 64:(h + 1) * 64], L_[:, h, :], R_[:, h, :], r=rk, w=[psk[bank]])
                        for lev in range(1, 4):
                            a_, bp_ = lev % 2, (lev - 1) % 2
                            Np, NTp, Nc, NTc = Nn[bp_], NTn[bp_], Nn[a_], NTn[a_]
                            kp = ['rNn%d' % bp_, 'rNTn%d' % bp_]
                            bn = nextps()
                            mm8(bn, NTp, Np, kp)
                            cp(Nc[:, :, :], v3(ps[bn][0:64, :], 8), r=[psk[bn]], w=['rNn%d' % a_], eng='act')
                            bn2 = nextps()
                            mm8(bn2, Np, NTp, kp)
                            cp(NTc[:, :, :], v3(ps[bn2][0:64, :], 8), r=[psk[bn2]], w=['rNTn%d' % a_], eng='act')
                            b1_ = nextps()
                            mm8(b1_, Nc, TT, ['rNn%d' % a_, 'rTT'])
                            b2_ = nextps()
                            mm8(b2_, NTc, Tn, ['rNTn%d' % a_, 'rTn'])
                            tt(TT[:, :, :], TT[:, :, :], v3(ps[b1_][0:64, :], 8), ALU.add, r=['rTT', psk[b1_]], w=['rTT'])
                            tt(Tn[:, :, :], Tn[:, :, :], v3(ps[b2_][0:64, :], 8), ALU.add, r=['rTn', psk[b2_]], w=['rTn'])
                        Xs, X2s = Nn[0], NTn[0]
                        bx = nextps()
                        mm8(bx, O32a, TT, ['O32a', 'rTT'])
                        cp(Xs[:, :, :], v3(ps[bx][0:64, :], 8), r=[psk[bx]], w=['rNn0'], eng='act')
                        bx2 = nextps()
                        mm8(bx2, O32Ta, Tn, ['O32Ta', 'rTn'])
                        cp(X2s[:, :, :], v3(ps[bx2][0:64, :], 8), r=[psk[bx2]], w=['rNTn0'], eng='act')
                        by1 = nextps()
                        mm8(by1, Tn, Xs, ['rTn', 'rNn0'])
                        by2 = nextps()
                        mm8(by2, TT, X2s, ['rTT', 'rNTn0'])
                        tt(TT[:, :, :], TT[:, :, :], v3(ps[by1][0:64, :], 8), ALU.add, r=['rTT', psk[by1]], w=['rTT'])
                        tt(Tn[:, :, :], Tn[:, :, :], v3(ps[by2][0:64, :], 8), ALU.add, r=['rTn', psk[by2]], w=['rTn'])
                        bx = nextps()
                        mm8(bx, O64a, TT, ['O64a', 'rTT'])
                        cp(Xs[:, :, :], v3(ps[bx][0:64, :], 8), r=[psk[bx]], w=['rNn0'], eng='act')
                        by1 = nextps()
                        mm8(by1, Tn, Xs, ['rTn', 'rNn0'])
                        tt(TT[:, :, :], TT[:, :, :], v3(ps[by1][0:64, :], 8), ALU.add, r=['rTT', psk[by1]], w=['rTT'])
                        cp(TTb[:, :, :], TT[:, :, :], r=['rTT'], w=['TTb'], eng='act')
                        if RP < 4:
                            continue
                        for (src, skey, dst, dkey) in ((bt, 'bt', btok, 'btok'), (aG, 'aG', aGtok, 'aGtok'),
                                                       (kG, 'kG', kGtok, 'kGtok'), (None, None, vtok, 'rvtok')):
                            bb = nextps()
                            pq = ps[bb][:, :].bitcast(BF16)
                            for cq in range(4):
                                if src is None:
                                    tr(pq[0:64, cq * 128:(cq + 1) * 128], v_b[:, cq, csl], ident_bf[:, :],
                                       r=[('v_b', cq), 'ident_bf'], w=[psk[bb]])
                                else:
                                    tr(pq[0:64, cq * 128:(cq + 1) * 128], src[:, cq, :], ident_bf[:, :],
                                       r=[skey, 'ident_bf'], w=[psk[bb]])
                            cp(dst[:, :, :], v3(pq[0:64, 0:512], 4), r=[psk[bb]], w=[dkey], eng='act')
                        if RP < 5:
                            continue
                        bb = nextps()
                        for h in range(8):
                            cq, hp = h // 2, h % 2
                            mm(ps[bb][0:64, h * 64:(h + 1) * 64], BTm[:, h, :], vtok[:, cq, hp * 64:(hp + 1) * 64],
                               r=['BTm', 'rvtok'], w=[psk[bb]])
                        cp(BVb[:, :, :], v3(ps[bb][0:64, :], 8), r=[psk[bb]], w=['BVb'], eng='act')
                        bb = nextps()
                        bw_ = nextps()
                        for h in range(8):
                            mm(ps[bb][0:64, h * 64:(h + 1) * 64], TTb[:, h, :], BVb[:, h, :], r=['TTb', 'BVb'], w=[psk[bb]])
                        for h in HORD:
                            cq, hp = h // 2, h % 2
                            mm(ps[bw_][hp * 64:(hp + 1) * 64, cq * 64:(cq + 1) * 64], btok[:, cq, hp * 64:(hp + 1) * 64], TTb[:, h, :],
                               r=['btok', 'TTb'], w=[psk[bw_]])
                        cp(U0[:, :, :], v3(ps[bb][0:64, :], 8), r=[psk[bb]], w=['U0'], eng='act')
                        cp(WmTb[:, :, :], v3(ps[bw_][:, 0:256], 4), r=[psk[bw_]], w=['WmTb'], eng='act')
                        if RP < 6:
                            continue
                        bu_ = nextps()
                        for h in HORD:
                            cq, hp = h // 2, h % 2
                            pb_ = slice(hp * 64, (hp + 1) * 64)
                            if hp == 1 and os.environ.get('HP0'):
                                continue
                            mm(ps[bu_][0:64, h * 64:(h + 1) * 64], WmTb[pb_, cq, :], Pbf[pb_, cq, :], r=['WmTb', 'Pbf'], w=[psk[bu_]])
                        tt(Ub[:, :, :], U0[:, :, :], v3(ps[bu_][0:64, :], 8), ALU.add, r=['U0', psk[bu_]], w=['Ub'])
                        RQ = int(os.environ.get('RQ', '9'))
                        if RQ < 2:
                            continue
                        by_ = nextps()
                        bp2 = nextps()
                        for h in HORD:
                            cq, hp = h // 2, h % 2
                            pb_ = slice(hp * 64, (hp + 1) * 64)
                            yo_ = ps[by_][pb_, cq * 64:(cq + 1) * 64]
                            mm(yo_, Pbf[pb_, cq, :], rt[pb_, cq, :], start=True, stop=False, r=['Pbf', 'rt'], w=[psk[by_]])
                            if RQ >= 3:
                                mm(yo_, Ub[:, h, :], RaT[:, h, :], start=False, stop=False, r=['Ub', 'RaT'], w=[psk[by_]])
                                mm(yo_, vtok[:, cq, pb_], RkT[:, h, :], start=False, stop=True, r=['rvtok', 'RkT'], w=[psk[by_]])
                            po_ = ps[bp2][pb_, cq * 64:(cq + 1) * 64]
                            mm(po_, aGtok[:, cq, pb_], Ub[:, h, :], start=True, stop=False, r=['aGtok', 'Ub'], w=[psk[bp2]])
                            mm(po_, kGtok[:, cq, pb_], vtok[:, cq, pb_], start=False, stop=True, r=['kGtok', 'rvtok'], w=[psk[bp2]])
                        py = v3(ps[by_][:, 0:256], 4)
                        if d == 0:
                            cp(yacc[:, :, csl], py, r=[psk[by_]], w=[('yacc', c)], eng='act')
                        else:
                            tt(yacc[:, :, csl], yacc[:, :, csl], py, ALU.add, r=[psk[by_], ('yacc', c)], w=[('yacc', c)])
                        for cq in range(4):
                            stt(Pf[:, cq, :], Pf[:, cq, :], eTot[:, cq:cq + 1], ps[bp2][:, cq * 64:(cq + 1) * 64],
                                ALU.mult, ALU.add, r=['Pf', 'eTot', psk[bp2]], w=['Pf'])
                        seg_end = (c % 4 == 3) if d == 0 else (c % 4 == 0)
                        if seg_end:
                            dma('sp', rso_d[d, c // 4], Pf[:, :, :], r=['Pf'])
                            ts(Pf[:, :, :], Pf[:, :, :], flag[:, 0:1], ALU.mult, r=['Pf', 'flag'], w=['Pf'])
                        cp(Pbf[:, :, :], Pf[:, :, :], r=['Pf'], w=['Pbf'], eng='act')
                yk = [('yacc', c) for c in range(16)]
                for cq in range(4):
                    ysrc = yacc[:, cq, :]
                    for t2_ in range(2):
                        tsl = slice(t2_ * 512, (t2_ + 1) * 512)
                        bb = nextps()
                        cp(lw[:, 0, tsl].bitcast(BF16)[:, 0:512], ysrc[:, tsl], r=yk, w=['lw'], eng='act')
                        mm(ps[bb][:, :], bones[:, :], lw[:, 0, tsl].bitcast(BF16)[:, 0:512], r=['bones', 'lw'], w=[psk[bb]])
                        stt(lw[:, 1, tsl], ps[bb][:, :], -1.0 / 64, ysrc[:, tsl], ALU.mult, ALU.add, r=[psk[bb]] + yk, w=['lw'])
                        act(lw[:, 0, tsl].bitcast(BF16)[:, 0:512], lw[:, 1, tsl], AF.Square, r=['lw'], w=['lw'])
                        bb2 = nextps()
                        mm(ps[bb2][:, :], bones[:, :], lw[:, 0, tsl].bitcast(BF16)[:, 0:512], r=['bones', 'lw'], w=[psk[bb2]])
                        act(lw[:, 2, tsl], ps[bb2][:, :], AF.Sqrt, r=[psk[bb2], 'epsb'], w=['lw'], bias=epsb[:, 2:3], scale=1.0 / 64)
                        rcp(lw[:, 2, tsl], lw[:, 2, tsl], r=['lw'], w=['lw'])
                        stt(lw[:, 1, tsl], lw[:, 1, tsl], ppc('ln_w')[:, cq:cq + 1], lw[:, 2, tsl], ALU.mult, ALU.mult,
                            r=['lw', 'lw', 'pp'], w=['lw'])
                        stt(lw[:, 1, tsl], lw[:, 1, tsl], ppc('ln_b')[:, cq:cq + 1], bon[:, cq, tsl], ALU.add, ALU.add,
                            r=['lw', 'pp', ('bon', cq)], w=['lw'])
                        tt(mix1T[:, 4 + cq, tsl], lw[:, 1, tsl], gate[:, cq, tsl], ALU.mult, r=['lw', ('gate', cq)],
                           w=[('mix1T', 4 + cq)])
                pe_mode('end')
                state['drain'] = False
                S.flush()

        if STAGE >= 1:
            with contextlib.ExitStack() as sc:
                alloc_work(sc, 'p1')
                layer0_mixer(sc)
                S.flush()
        if STAGE >= 4:
            with contextlib.ExitStack() as sc:
                alloc_work(sc, 'p2', nwb=3)
                uT = sc.enter_context(nc.sbuf_tensor("uT", [128, 32, NT], BF16))
                mlp(0, uT)
                S.flush()
        if STAGE >= 6:
            layer1_mixer()
        with contextlib.ExitStack() as sc:
            alloc_work(sc, 'p7', nwb=3)
            if STAGE >= 5:
                uT = sc.enter_context(nc.sbuf_tensor("uT2", [128, 32, NT], BF16))
                mlp(1, uT)
            yo = sc.enter_context(nc.sbuf_tensor("yo", [128, 2, 512], F32))
            norm_mod(ppc('norm_final'), None, yo, 'yo', final=True)
            S.flush()
    return nc


def make_l1_consts():
    rm = np.zeros((128, 128), np.float32)
    for hb in range(2):
        for d in range(64):
            if (d % 32) < 16:
                rm[hb * 64 + d + 16, hb * 64 + d] = -1.0
            else:
                rm[hb * 64 + d - 16, hb * 64 + d] = 1.0
    bones = np.zeros((128, 128), np.float32)
    bones[:64, :64] = 1.0
    bones[64:, 64:] = 1.0
    t = np.arange(NT)
    row = (t // 64).astype(np.float32)
    col = (t % 64).astype(np.float32)
    inv = (10000.0 ** (-np.arange(0, 32, 2, dtype=np.float32) / 32)).astype(np.float32)
    cs = np.zeros((128, NT), np.float32)
    sn = np.zeros((128, NT), np.float32)
    for p in range(128):
        d = p % 64
        pos = row if d < 32 else col
        ang = (pos * inv[(d % 32) % 16]).astype(np.float32)
        cs[p] = np.cos(ang)
        sn[p] = np.sin(ang)
    NEG = -30000.0
    mk = np.full((2, 8, 128, 896), NEG, np.float32)
    for n in range(8):
        tq = n * 128 + np.arange(128)[:, None]
        tk = (n - 1) * 128 + np.arange(384)[None, :]
        inr = (tk >= 0) & (tk < NT)
        mk[0, n, :, :384] = np.where(inr & ((tk // 256) == (tq // 256)), 0.0, NEG)
        mk[1, n, :, :384] = np.where(inr & (np.abs(tk - tq) <= 128), 0.0, NEG)
        mk[1, n, :, 384:] = 0.0
    return rm, bones, cs, sn, mk


def kernel(**inp):
    inp = {k: np.asarray(v) for k, v in inp.items()}
    nc = build_nc()
    cst = make_consts()
    pp = make_pp(inp)
    xp = inp['x_prompt'].astype(np.float32)
    xs = inp['x_sample'].astype(np.float32)
    shared = {
        'pp': pp, 'cst': cst,
        'mod_w': np.ascontiguousarray(inp['mod_w'], np.float32),
        'mlp_w1': np.ascontiguousarray(inp['mlp_w1'], np.float32),
        'mlp_w2': np.ascontiguousarray(inp['mlp_w2'], np.float32),
        'ev_w_in': np.ascontiguousarray(inp['ev_w_in'][0], np.float32),
        'ev_w_out': np.ascontiguousarray(inp['ev_w_out'][0], np.float32),
    }
    rm, bones, rcs, rsn, mk = make_l1_consts()
    wi = np.asarray(inp['od_w_in'][0], np.float32)
    shared['od_w_in_x'] = np.ascontiguousarray(np.concatenate(
        [wi[:, 0:512], wi[:, 512:576], wi[:, 512:576], wi[:, 576:640], wi[:, 576:640], wi[:, 640:768], wi[:, 768:]], 1))
    shared['od_w_out'] = np.ascontiguousarray(inp['od_w_out'][0], np.float32)
    shared['w_up'] = np.ascontiguousarray(np.asarray(inp['rwkv_w_up'][0], np.float32).reshape(128, 512))
    shared['a_up'] = np.ascontiguousarray(np.asarray(inp['rwkv_a_up'][0], np.float32).reshape(128, 512))
    shared['g_up'] = np.ascontiguousarray(inp['rwkv_g_up'][0], np.float32)
    shared['bones'] = bones
    shared['rm'] = rm

    def st_in(sv):
        a = np.asarray(sv, np.float32).transpose(0, 2, 1).reshape(4, 2, 64, 64)
        return np.ascontiguousarray(a.transpose(1, 2, 0, 3).reshape(128, 4, 64))

    def st_out(a):
        b = a.reshape(2, 64, 4, 64).transpose(2, 0, 1, 3).reshape(8, 64, 64)
        return b.transpose(0, 2, 1)
    in_maps = []
    for core in range(8):
        m = dict(shared)
        if core < 4:
            xt = xp[core * 4:(core + 1) * 4].reshape(NT, 1024)
            m['cv'] = fm(inp['c_ctx'], 8)
            m['flag'] = np.zeros((128, 1), np.float32)
            m['gs0'] = np.zeros((2, 4, 128, 128), np.float32)
            m['ropec'] = np.ones((128, NT), np.float32)
            m['ropes'] = np.zeros((128, NT), np.float32)
            m['maskb'] = mk[0]
            m['ckT'] = np.zeros((2, 128, 512), np.float32)
            m['cvt'] = np.zeros((128, 4, 128), np.float32)
            m['rs0'] = np.zeros((2, 128, 4, 64), np.float32)
        else:
            b = core - 4
            xt = xs[b]
            m['cv'] = fm(inp['c'][b], 8)
            m['flag'] = np.ones((128, 1), np.float32)
            m['gs0'] = np.ascontiguousarray(
                np.stack([inp['state_gdn_fwd'][b, 0], inp['state_gdn_bwd'][b, 0]]), np.float32)
            m['ropec'] = rcs
            m['ropes'] = rsn
            m['maskb'] = mk[1]
            ck = np.asarray(inp['cache_attn_k'][b, 0], np.float32)
            m['ckT'] = np.ascontiguousarray(np.stack([np.concatenate([ck[k].T, ck[k].T], 0) for k in range(2)]))
            cvv = np.asarray(inp['cache_attn_v'][b, 0], np.float32)
            m['cvt'] = np.ascontiguousarray(cvv.transpose(1, 0, 2).reshape(4, 128, 128).transpose(1, 0, 2))
            m['rs0'] = np.stack([st_in(inp['state_rwkv_fwd'][b, 0]), st_in(inp['state_rwkv_bwd'][b, 0])])
        m['xT'] = np.ascontiguousarray(xt.T)
        in_maps.append(m)
    res = run_bass_kernel_spmd(nc, in_maps, core_ids=list(range(8)))
    R = res.results
    y_prompt = np.stack([R[c]['yT'].T.reshape(4, 256, 1024) for c in range(4)]).reshape(16, 256, 1024)
    y_sample = np.stack([R[4 + b]['yT'].T for b in range(4)])
    gf = np.stack([R[c]['gso'][0] for c in range(4)]).reshape(16, 1, 4, 128, 128)
    gb = np.stack([R[c]['gso'][1] for c in range(4)]).reshape(16, 1, 4, 128, 128)
    def kvout(i):
        o = np.stack([R[c]['kvo'][i] for c in range(4)])
        o = o.reshape(4, 2, 64, 4, 256).transpose(0, 3, 1, 4, 2)
        return np.ascontiguousarray(o.reshape(16, 1, 2, 256, 64)).astype(np.float32)

    def rsout(d):
        o = np.stack([np.stack([st_out(R[c]['rso'][d, sg]) for sg in range(4)]) for c in range(4)])
        return np.ascontiguousarray(o.reshape(16, 1, 8, 64, 64)).astype(np.float32)
    return (y_prompt.astype(np.float32), y_sample.astype(np.float32), gf.astype(np.float32), gb.astype(np.float32),
            kvout(0), kvout(1), rsout(0), rsout(1))
```

```python
import contextlib
import os
import numpy as np
import concourse.bass as bass
import concourse.mybir as mybir
from concourse.bass_utils import run_bass_kernel_spmd

F32 = mybir.dt.float32
BF16 = mybir.dt.bfloat16
AF = mybir.ActivationFunctionType
ALU = mybir.AluOpType

ENGS = ['pe', 'act', 'dve', 'pool', 'sp']
DMAQ = ('sp', 'pool')
KRING = 6
NT = 1024
EPS = 1e-6


class Sched:
    def __init__(self, nc):
        self.nc = nc
        self.prog = {e: [] for e in ENGS}
        self.cnt = {e: 0 for e in ENGS}
        self.known = {e: {} for e in ENGS}
        self.res = {}
        self.dma_i = {q: 0 for q in DMAQ}
        self.sems = {}
        self.eng_obj = {'pe': nc.tensor, 'act': nc.scalar, 'dve': nc.vector,
                        'pool': nc.gpsimd, 'sp': nc.sync}

    def alloc_sems(self, stack):
        for e in ['pe', 'act', 'dve', 'pool']:
            self.sems[e] = stack.enter_context(self.nc.semaphore('c_' + e))
        for q in DMAQ:
            for j in range(KRING):
                self.sems[(q, j)] = stack.enter_context(self.nc.semaphore('d_%s%d' % (q, j)))

    def _r(self, k):
        if k not in self.res:
            self.res[k] = {'w': None, 'r': {}}
        return self.res[k]

    def _need(self, eng, waits, dep, own):
        if dep is None:
            return
        sk, val = dep
        if sk == own:
            return
        if self.known[eng].get(sk, 0) >= val:
            return
        if waits.get(sk, 0) < val:
            waits[sk] = val

    def op(self, eng, fn, reads=(), writes=()):
        own = eng
        skip = eng if eng == 'pe' else None
        waits = {}
        for k in reads:
            self._need(eng, waits, self._r(k)['w'], skip)
        for k in writes:
            r = self._r(k)
            self._need(eng, waits, r['w'], skip)
            for sk, v in r['r'].items():
                self._need(eng, waits, (sk, v), skip)
        self.cnt[eng] += 1
        v = self.cnt[eng]
        for sk, val in waits.items():
            self.known[eng][sk] = val
        self.prog[eng].append((fn, list(waits.items()), (own, 1)))
        for k in reads:
            r = self._r(k)
            if r['r'].get(own, 0) < v:
                r['r'][own] = v
        for k in writes:
            r = self._r(k)
            r['w'] = (own, v)
            r['r'] = {}

    def dma(self, q, fn, reads=(), writes=()):
        i = self.dma_i[q]
        self.dma_i[q] += 1
        own = (q, i % KRING)
        val = 16 * (i // KRING + 1)
        waits = {}
        if i >= KRING:
            self._need(q, waits, (own, val - 16), None)
        for k in reads:
            self._need(q, waits, self._r(k)['w'], None)
        for k in writes:
            r = self._r(k)
            self._need(q, waits, r['w'], None)
            for sk, v in r['r'].items():
                self._need(q, waits, (sk, v), None)
        for sk, v in waits.items():
            self.known[q][sk] = v
        self.prog[q].append((fn, list(waits.items()), (own, 16)))
        for k in reads:
            r = self._r(k)
            if r['r'].get(own, 0) < val:
                r['r'][own] = val
        for k in writes:
            r = self._r(k)
            r['w'] = (own, val)
            r['r'] = {}

    def _all_done(self):
        waits = []
        for q in DMAQ:
            n = self.dma_i[q]
            for j in range(KRING):
                cntj = len(range(j, n, KRING))
                if cntj:
                    waits.append(((q, j), 16 * cntj))
        for e in ['pe', 'act', 'dve', 'pool']:
            if self.cnt[e]:
                waits.append((e, self.cnt[e]))
        return waits

    def barrier(self):
        waits = self._all_done()
        for e in ENGS:
            w2 = [(sk, v) for sk, v in waits if sk != e and self.known[e].get(sk, 0) < v]
            for sk, v in w2:
                self.known[e][sk] = v
            self.prog[e].append((None, w2, None))
        self.res = {}

    def finish(self, eng='sp'):
        self.prog[eng].append((None, self._all_done(), None))

    def emit(self, block):
        S = self

        def replay(e):
            def body(_eng):
                eo = S.eng_obj[e]
                for fn, waits, inc in S.prog[e]:
                    for sk, v in waits:
                        eo.wait_ge(S.sems[sk], v)
                    if fn is not None:
                        ins = fn(eo)
                        ins.then_inc(S.sems[inc[0]], inc[1])
            return body
        block.tensor(replay('pe'))
        block.scalar(replay('act'))
        block.vector(replay('dve'))
        block.gpsimd(replay('pool'))
        block.sync(replay('sp'))

    def flush(self):
        self.barrier()
        with self.nc.Block() as block:
            self.emit(block)
        self.prog = {e: [] for e in ENGS}


def _pp_layout():
    ent = [('mod_b', 96), ('norm_mix', 16), ('norm_mlp', 16), ('norm_final', 8),
           ('gdn_conv', 36), ('sc_conv', 12), ('gdn_norm', 1), ('a_log', 1), ('dt_bias', 1),
           ('mu', 30), ('w0', 8), ('a0', 8), ('k_k', 4), ('k_a', 4), ('ln_w', 4), ('ln_b', 4), ('r_k', 4), ('sink', 8)]
    off = {}
    o = 0
    for n, w in ent:
        off[n] = (o, w)
        o += w
    return off, o


PP_OFF, PP_N = _pp_layout()
C_ID, C_U, C_L, C_SU, C_SL, C_N = 0, 128, 192, 256, 320, 768
C_S16L, C_O32L, C_O64L, C_S16U, C_O32U, C_O64U = 384, 448, 512, 576, 640, 704


def make_consts():
    c = np.zeros((128, C_N), np.float32)
    c[:, C_ID:C_ID + 128] = np.eye(128, dtype=np.float32)
    p = np.arange(64)[:, None]
    f = np.arange(64)[None, :]
    c[:64, C_U:C_U + 64] = (p <= f)
    c[:64, C_L:C_L + 64] = (p >= f)
    c[:64, C_SU:C_SU + 64] = (p < f)
    c[:64, C_SL:C_SL + 64] = (p > f)
    c[:64, C_S16L:C_S16L + 64] = (p > f) & (p // 16 == f // 16)
    c[:64, C_O32L:C_O32L + 64] = (p > f) & (p // 32 == f // 32) & (p // 16 != f // 16)
    c[:64, C_O64L:C_O64L + 64] = (p > f) & (p // 32 != f // 32)
    c[:64, C_S16U:C_S16U + 64] = (p < f) & (p // 16 == f // 16)
    c[:64, C_O32U:C_O32U + 64] = (p < f) & (p // 32 == f // 32) & (p // 16 != f // 16)
    c[:64, C_O64U:C_O64U + 64] = (p < f) & (p // 32 != f // 32)
    return c


def fm(v, nch):
    return np.ascontiguousarray(np.asarray(v, np.float32).reshape(nch, 128).T)


def make_pp(inp):
    pp = np.zeros((128, PP_N), np.float32)

    def put(name, arr):
        o, w = PP_OFF[name]
        assert arr.shape == (128, w), (name, arr.shape, w)
        pp[:, o:o + w] = arr
    put('mod_b', np.concatenate([fm(inp['mod_b'][l], 48) for l in range(2)], 1))
    put('norm_mix', np.concatenate([fm(inp['norm_mix'][l], 8) for l in range(2)], 1))
    put('norm_mlp', np.concatenate([fm(inp['norm_mlp'][l], 8) for l in range(2)], 1))
    put('norm_final', fm(inp['norm_final'], 8))
    put('gdn_conv', np.concatenate([fm(inp['gdn_conv'][0][i], 12) for i in range(3)], 1))
    put('sc_conv', np.concatenate([fm(inp['sc_conv'][0][i], 4) for i in range(3)], 1))
    put('gdn_norm', np.asarray(inp['gdn_norm'][0], np.float32).reshape(128, 1))
    a = np.zeros((128, 1), np.float32)
    a[:8, 0] = np.asarray(inp['gdn_a_log'][0], np.float32).reshape(8)
    put('a_log', a)
    a = np.zeros((128, 1), np.float32)
    a[:8, 0] = np.asarray(inp['gdn_dt_bias'][0], np.float32).reshape(8)
    put('dt_bias', a)
    put('mu', np.concatenate([fm(inp['rwkv_mu'][0][d], 15) for d in range(2)], 1))
    put('w0', np.concatenate([fm(inp['rwkv_w0'][0][d], 4) for d in range(2)], 1))
    put('a0', np.concatenate([fm(inp['rwkv_a0'][0][d], 4) for d in range(2)], 1))
    put('k_k', fm(inp['rwkv_k_k'][0], 4))
    put('k_a', fm(inp['rwkv_k_a'][0], 4))
    put('ln_w', fm(inp['rwkv_ln_w'][0], 4))
    put('ln_b', fm(inp['rwkv_ln_b'][0], 4))
    put('r_k', fm(np.asarray(inp['rwkv_r_k'][0]).reshape(512), 4))
    put('sink', np.tile(np.asarray(inp['attn_sink'][0], np.float32).reshape(1, 8), (128, 1)))
    return pp


STAGE = int(os.environ.get('STAGE', '9'))
L1PART = int(os.environ.get('L1PART', '9'))
SUB = 9


def build_nc(do_l1=False):
    nc = bass.Bass("TRN2", target_bir_lowering=False)

    def din(name, shape):
        return nc.dram_tensor(name, list(shape), F32, kind="ExternalInput").ap()

    def dout(name, shape):
        return nc.dram_tensor(name, list(shape), F32, kind="ExternalOutput").ap()

    xT_d = din("xT", [1024, NT])
    cv_d = din("cv", [128, 8])
    flag_d = din("flag", [128, 1])
    s0_d = din("gs0", [2, 4, 128, 128])
    pp_d = din("pp", [128, PP_N])
    cst_d = din("cst", [128, C_N])
    modw_d = din("mod_w", [2, 1024, 6144])
    w1_d = din("mlp_w1", [2, 1024, 4096])
    w2_d = din("mlp_w2", [2, 4096, 1024])
    evin_d = din("ev_w_in", [1024, 3600])
    evout_d = din("ev_w_out", [1024, 1024])
    odin_d = din("od_w_in_x", [1024, 2816])
    odout_d = din("od_w_out", [1024, 1024])
    wup_d = din("w_up", [128, 512])
    aup_d = din("a_up", [128, 512])
    gup_d = din("g_up", [128, 512])
    bones_d = din("bones", [128, 128])
    rm_d = din("rm", [128, 128])
    ropec_d = din("ropec", [128, NT])
    ropes_d = din("ropes", [128, NT])
    maskb_d = din("maskb", [8, 128, 896])
    ckT_d = din("ckT", [2, 128, 512])
    cvt_d = din("cvt", [128, 4, 128])
    rs0_d = din("rs0", [2, 128, 4, 64])
    kvo_d = dout("kvo", [2, 128, NT])
    rso_d = dout("rso", [2, 4, 128, 4, 64])
    yT_d = dout("yT", [1024, NT])
    gso_d = dout("gso", [2, 4, 4, 128, 128])

    with contextlib.ExitStack() as st:
        S = Sched(nc)
        S.alloc_sems(st)
        with nc.Block() as blk0:
            def _clr(_e):
                for sm in S.sems.values():
                    nc.sync.sem_clear(sm)
            blk0.sync(_clr)

        def tile(name, shape, dt=F32):
            return st.enter_context(nc.sbuf_tensor('sb_' + name, list(shape), dt))

        ps = [st.enter_context(nc.psum_tensor("ps%d" % i, [128, 512], F32)) for i in range(8)]
        psk = ["ps%d" % i for i in range(8)]
        state = {'ps': 0, 'wb': 0}

        def nextps():
            b = state['ps']
            state['ps'] = (b + 1) % 8
            return b

        def pe_mode(mode):
            if not state.get('drain'):
                return
            if state.get('pemode') != mode:
                state['pemode'] = mode
                if S.cnt['pe'] > 0:
                    S.prog['pe'].append((None, [('pe', S.cnt['pe'])], None))

        def rnd(n):
            return 32 if n <= 32 else (64 if n <= 64 else 128)

        def mm(out, lhsT, rhs, start=True, stop=True, r=(), w=()):
            pe_mode(('mm', rnd(lhsT.shape[0]), rnd(lhsT.shape[-1]), lhsT.start_partition(), out.start_partition()))
            S.op('pe', lambda e: e.matmul(out, lhsT, rhs, start=start, stop=stop), r, w)

        def tr(out, in_, ident, r=(), w=()):
            pe_mode(('tr', rnd(in_.shape[0]), rnd(in_.shape[-1]), in_.start_partition(), out.start_partition()))
            S.op('pe', lambda e: e.transpose(out, in_, ident), r, w)

        def act(out, in_, func, r=(), w=(), bias=None, scale=None):
            kw = {}
            if bias is not None:
                kw['bias'] = bias
            if scale is not None:
                kw['scale'] = scale
            S.op('act', lambda e: e.activation(out, in_, func, **kw), r, w)

        def tt(out, a, b, op, r=(), w=(), eng='dve'):
            S.op(eng, lambda e: e.tensor_tensor(out, a, b, op), r, w)

        def ts(out, a, s1, op0, r=(), w=(), s2=None, op1=None, eng='dve'):
            if op1 is None:
                S.op(eng, lambda e: e.tensor_scalar(out, a, s1, None, op0), r, w)
            else:
                S.op(eng, lambda e: e.tensor_scalar(out, a, s1, s2, op0, op1), r, w)

        def stt(out, a, s, b, op0, op1, r=(), w=()):
            S.op('dve', lambda e: e.scalar_tensor_tensor(out, a, s, b, op0, op1), r, w)

        def cp(out, in_, r=(), w=(), eng='dve'):
            if eng == 'act':
                S.op('act', lambda e: e.copy(out, in_), r, w)
            else:
                S.op(eng, lambda e: e.tensor_scalar(out, in_, 1.0, None, ALU.mult), r, w)

        def rcp(out, in_, r=(), w=()):
            S.op('dve', lambda e: e.reciprocal(out, in_), r, w)

        def scan(out, d0, d1, r=(), w=()):
            S.op('dve', lambda e: e.tensor_tensor_scan(out, d0, d1, 0.0, ALU.mult, ALU.add), r, w)

        def rmax(out, in_, r=(), w=()):
            S.op('dve', lambda e: e.reduce_max(out, in_, mybir.AxisListType.X), r, w)

        def expacc(out, in_, bias, acc, r=(), w=()):
            S.op('act', lambda e: e.activation(out, in_, AF.Exp, bias=bias, accum_out=acc), r, w)

        def mset(ap, val, r=(), w=(), eng='dve'):
            S.op(eng, lambda e: e.memset(ap, val), r, w)

        def dma(q, out, in_, r=(), w=()):
            S.dma(q, lambda e: e.dma_start(out=out, in_=in_), r, w)

        x_sb = tile("x_sb", [128, 8, NT])
        hT = None
        wb = None
        cst = tile("cst", [128, C_N])
        pp = tile("pp", [128, PP_N])
        cv = tile("cv", [128, 8])
        cvs = tile("cvs", [128, 8], BF16)
        flag = tile("flag", [128, 1])
        ident_bf = tile("ident_bf", [128, 128], BF16)
        ones_bf = tile("ones_bf", [128, 128], BF16)
        ones_f = tile("ones_f", [128, 128])
        modv = tile("modv", [128, 2, 48])
        gsA = tile("gsA", [128, 2, 8])
        gsB = tile("gsB", [128, 2, 8])
        sqb = rstd = ntmp = None

        def alloc_work(sc, tag, with_h=True, nwb=3):
            nonlocal hT, wb, sqb, rstd, ntmp
            A = lambda n, s_, d=F32: sc.enter_context(nc.sbuf_tensor('sb_%s_%s' % (n, tag), list(s_), d))
            if with_h:
                hT = A("hT", [128, 8, NT], BF16)
            wb = [A("wb%d" % i, [128, 4096], BF16) for i in range(nwb)]
            state['nwb'] = nwb
            state['wb'] = 0
            sqb = A("sqb", [128, 2, 512], BF16)
            rstd = A("rstd", [128, 512])
            ntmp = [A("ntmp%d" % i, [128, 512]) for i in range(2)]

        sc0 = contextlib.ExitStack()
        alloc_work(sc0, 'p0', with_h=False)

        ident_f = cst[:, C_ID:C_ID + 128]

        def ppc(name, j0=0, n=None):
            o, wd = PP_OFF[name]
            if n is None:
                n = wd - j0
            return pp[:, o + j0:o + j0 + n]

        dma('sp', cst[:, :], cst_d[:, :], w=['cst'])
        dma('sp', pp[:, :], pp_d[:, :], w=['pp'])
        dma('sp', cv[:, :], cv_d[:, :], w=['cv'])
        dma('sp', flag[:, :], flag_d[:, :], w=['flag'])
        for fc in range(8):
            dma('sp', x_sb[:, fc, :], xT_d[fc * 128:(fc + 1) * 128, :], w=[('x', fc)])
        act(cvs[:, :], cv[:, :], AF.Silu, r=['cv'], w=['cvs'])
        cp(ident_bf[:, :], ident_f, r=['cst'], w=['ident_bf'])
        mset(ones_bf[:, :], 1.0, w=['ones_bf'])
        mset(ones_f[:, :], 1.0, w=['ones_f'])

        def load_piece(wd_ap, kcn, wdth):
            slot = state['wb']
            state['wb'] = (slot + 1) % state['nwb']
            view = wb[slot][:, 0:kcn * wdth].rearrange("p (k n) -> p k n", k=kcn)
            dma('pool', view, wd_ap.rearrange("(k p) n -> p k n", p=128), w=[('wb', slot)])
            return slot, view

        for l in range(2):
            bm = nextps()
            for oc in range(12):
                slot, wv = load_piece(modw_d[l, :, oc * 512:(oc + 1) * 512], 8, 512)
                for c4 in range(4):
                    ocn = oc * 4 + c4
                    for kc in range(8):
                        mm(ps[bm][:, ocn:ocn + 1], wv[:, kc, c4 * 128:(c4 + 1) * 128], cvs[:, kc:kc + 1],
                           start=(kc == 0), stop=(kc == 7), r=[('wb', slot), 'cvs'], w=[psk[bm]])
            tt(modv[:, l, :], ps[bm][:, 0:48], ppc('mod_b', l * 48, 48), ALU.add,
               r=[psk[bm], 'pp'], w=['modv'])
            stt(gsA[:, l, :], modv[:, l, 8:16], 1.0, ppc('norm_mix', l * 8, 8), ALU.add, ALU.mult,
                r=['modv', 'pp'], w=['gsA'])
            stt(gsB[:, l, :], modv[:, l, 32:40], 1.0, ppc('norm_mlp', l * 8, 8), ALU.add, ALU.mult,
                r=['modv', 'pp'], w=['gsB'])

        S.flush()
        sc0.close()

        def norm_mod(gs_ap, shift_ap, dst, dst_key, final=False):
            for th in range(2):
                tsl = slice(th * 512, (th + 1) * 512)
                b = nextps()
                for fc in range(8):
                    act(sqb[:, fc % 2, :], x_sb[:, fc, tsl], AF.Square, r=[('x', fc)], w=[('sqb', fc % 2)])
                    mm(ps[b][:, :], ones_bf[:, :], sqb[:, fc % 2, :], start=(fc == 0), stop=(fc == 7),
                       r=['ones_bf', ('sqb', fc % 2)], w=[psk[b]])
                act(rstd[:, :], ps[b][:, :], AF.Sqrt, r=[psk[b], 'epsb'], w=['rstd'], bias=epsb[:, 0:1], scale=1.0 / 1024)
                rcp(rstd[:, :], rstd[:, :], r=['rstd'], w=['rstd'])
                for fc in range(8):
                    k = fc % 2
                    tt(ntmp[k][:, :], x_sb[:, fc, tsl], rstd[:, :], ALU.mult,
                       r=[('x', fc), 'rstd'], w=[('ntmp', k)])
                    if final:
                        act(dst[:, k, :], ntmp[k][:, :], AF.Identity, r=[('ntmp', k), 'pp'],
                            w=[(dst_key, k)], scale=gs_ap[:, fc:fc + 1])
                        dma('sp', yT_d[fc * 128:(fc + 1) * 128, tsl], dst[:, k, :], r=[(dst_key, k)])
                    else:
                        act(dst[:, fc, tsl], ntmp[k][:, :], AF.Identity, r=[('ntmp', k), 'modv', 'gsA', 'gsB'],
                            w=[(dst_key, fc)], scale=gs_ap[:, fc:fc + 1], bias=shift_ap[:, fc:fc + 1])

        epsb = tile("epsb", [128, 4])
        mset(epsb[:, 0:1], EPS, w=['epsb'])
        mset(epsb[:, 1:2], 1e-6, w=['epsb'])
        mset(epsb[:, 2:3], 64e-5, w=['epsb'])

        def linear(wd, kcn, pieces, src, src_key, consumer):
            for (c0, wdth) in pieces:
                slot, wv = load_piece(wd[:, c0:c0 + wdth], kcn, wdth)
                for cs in range(0, wdth, 128):
                    cw = min(128, wdth - cs)
                    for th in range(2):
                        b = nextps()
                        for kc in range(kcn):
                            mm(ps[b][0:cw, :], wv[:, kc, cs:cs + cw], src[:, kc, th * 512:(th + 1) * 512],
                               start=(kc == 0), stop=(kc == kcn - 1),
                               r=[('wb', slot), (src_key, kc)], w=[psk[b]])
                        consumer(c0 + cs, cw, th, b)

        def mlp(l, uT):
            norm_mod(gsB[:, l, :], modv[:, l, 24:32], hT, 'hT')

            def c1(c0, cw, th, b):
                oc = c0 // 128
                act(ntmp[th][:, :], ps[b][:, :], AF.Relu, r=[psk[b]], w=[('ntmp', th)])
                tt(uT[:, oc, th * 512:(th + 1) * 512], ntmp[th][:, :], ntmp[th][:, :], ALU.mult,
                   r=[('ntmp', th)], w=[('uT', oc)])
            linear(w1_d[l], 8, [(i * 512, 512) for i in range(8)], hT, 'hT', c1)

            def c2(c0, cw, th, b):
                fc = c0 // 128
                tsl = slice(th * 512, (th + 1) * 512)
                stt(x_sb[:, fc, tsl], ps[b][:, :], modv[:, l, 40 + fc:41 + fc], x_sb[:, fc, tsl],
                    ALU.mult, ALU.add, r=[psk[b], 'modv', ('x', fc)], w=[('x', fc)])
            linear(w2_d[l], 32, [(i * 128, 128) for i in range(8)], uT, 'uT', c2)

        def layer0_mixer(sc):
            T = lambda n, s, d=F32: sc.enter_context(nc.sbuf_tensor('sb_' + n, list(s), d))
            qT = T("qT", [128, 4, NT], BF16)
            kT = T("kT", [128, 4, NT], BF16)
            vT = T("vT", [128, 4, NT], BF16)
            mixT = T("mixT", [128, 8, NT], BF16)
            oacc = T("oacc", [128, 4, NT])
            pad = [T("pad%d" % i, [128, 4, 258]) for i in range(2)]
            cacc1 = T("cacc", [128, 4, 256])
            cacc = [cacc1, cacc1]
            qsq = T("qsq", [128, NT], BF16)
            rinv = T("rinv", [128, NT])
            sctmp = T("sctmp", [128, NT])
            betaT = T("betaT", [8, NT])
            gT = T("gT", [8, NT])
            negA = T("negA", [8, 1])
            Sf = T("Sf", [128, 4, 128])
            Sb = T("Sb", [128, 4, 128], BF16)

            for i in range(2):
                mset(pad[i][:, :, :], 0.0, w=[('pad', i)])
            act(negA[:, :], ppc('a_log')[0:8, :], AF.Exp, r=['pp'], w=['negA'])
            ts(negA[:, :], negA[:, :], -1.0, ALU.mult, r=['negA'], w=['negA'])

            norm_mod(gsA[:, 0, :], modv[:, 0, 0:8], hT, 'hT')

            cstate = {'i': 0}
            if SUB < 1:
                return

            def conv3(pi, wname, nch, ch):
                p = pad[pi]
                ts(p[:, 1:4, 0:1], p[:, 0:3, 256:257], flag[:, 0:1], ALU.mult,
                   r=[('pad', pi), 'flag'], w=[('pad', pi)])
                ts(p[:, 0:3, 257:258], p[:, 1:4, 1:2], flag[:, 0:1], ALU.mult,
                   r=[('pad', pi), 'flag'], w=[('pad', pi)])
                o, _ = PP_OFF[wname]
                w0 = pp[:, o + 0 * nch + ch:o + 0 * nch + ch + 1]
                w1 = pp[:, o + 1 * nch + ch:o + 1 * nch + ch + 1]
                w2 = pp[:, o + 2 * nch + ch:o + 2 * nch + ch + 1]
                ts(cacc[pi][:, :, :], p[:, :, 0:256], w0, ALU.mult, r=[('pad', pi), 'pp'], w=['cacc'])
                stt(cacc[pi][:, :, :], p[:, :, 1:257], w1, cacc[pi][:, :, :], ALU.mult, ALU.add,
                    r=[('pad', pi), 'pp', 'cacc'], w=['cacc'])
                stt(cacc[pi][:, :, :], p[:, :, 2:258], w2, cacc[pi][:, :, :], ALU.mult, ALU.add,
                    r=[('pad', pi), 'pp', 'cacc'], w=['cacc'])

            def pad_in(pi, th):
                return pad[pi][:, 2 * th:2 * th + 2, 1:257]

            def ps3(b):
                return ps[b][:, :].rearrange("p (s t) -> p s t", s=2)

            def c_qkv(c0, cw, th, b):
                ch = c0 // 128
                pi = ch % 2
                cp(pad_in(pi, th), ps3(b), r=[psk[b]], w=[('pad', pi)], eng='act')
                if th == 0:
                    return
                conv3(pi, 'gdn_conv', 12, ch)
                flat = cacc[pi][:, :, :].rearrange("p s t -> p (s t)")
                h = ch % 4
                if ch >= 8:
                    act(vT[:, h, :], flat, AF.Silu, r=['cacc'], w=[('vT', h)])
                    return
                act(sctmp[:, :], flat, AF.Silu, r=['cacc'], w=['sctmp'])
                act(qsq[:, :], sctmp[:, :], AF.Square, r=['sctmp'], w=['qsq'])
                for t2 in range(2):
                    bb = nextps()
                    mm(ps[bb][:, :], ones_bf[:, :], qsq[:, t2 * 512:(t2 + 1) * 512], r=['ones_bf', 'qsq'], w=[psk[bb]])
                    act(rinv[:, t2 * 512:(t2 + 1) * 512], ps[bb][:, :], AF.Sqrt, r=[psk[bb], 'epsb'],
                        w=['rinv'], bias=epsb[:, 1:2])
                rcp(rinv[:, :], rinv[:, :], r=['rinv'], w=['rinv'])
                dst, key = (qT, 'qT') if ch < 4 else (kT, 'kT')
                scl = (128.0 ** -0.5) if ch < 4 else 1.0
                stt(dst[:, h, :], sctmp[:, :], scl, rinv[:, :], ALU.mult, ALU.mult,
                    r=['sctmp', 'rinv'], w=[(key, h)])
            linear(evin_d, 8, [(i * 512, 512) for i in range(3)], hT, 'hT', c_qkv)

            if SUB < 2:
                return
            def c_z(c0, cw, th, b):
                h = (c0 - 1536) // 128
                act(mixT[:, h, th * 512:(th + 1) * 512], ps[b][:, :], AF.Silu, r=[psk[b]], w=[('mixT', h)])
            linear(evin_d, 8, [(1536, 512)], hT, 'hT', c_z)

            if SUB < 3:
                return
            def c_beta(c0, cw, th, b):
                act(betaT[:, th * 512:(th + 1) * 512], ps[b][0:8, :], AF.Sigmoid, r=[psk[b]], w=['betaT'])

            def c_a(c0, cw, th, b):
                tsl = slice(th * 512, (th + 1) * 512)
                act(rinv[0:8, tsl], ps[b][0:8, :], AF.Exp, r=[psk[b], 'pp'], w=['rinv'], bias=ppc('dt_bias')[0:8, :])
                act(rinv[0:8, tsl], rinv[0:8, tsl], AF.Ln, r=['rinv'], w=['rinv'], bias=1.0)
                ts(gT[:, tsl], rinv[0:8, tsl], negA[:, 0:1], ALU.mult, r=['rinv', 'negA'], w=['gT'])
            linear(evin_d, 8, [(2048, 8)], hT, 'hT', c_beta)
            linear(evin_d, 8, [(2056, 8)], hT, 'hT', c_a)

            if SUB < 4:
                return
            def mk_sc(j):
                def c_c(c0, cw, th, b):
                    cp(sctmp[:, th * 512:(th + 1) * 512], ps[b][:, :], r=[psk[b]], w=['sctmp'], eng='act')

                def c_h(c0, cw, th, b):
                    pi = j % 2
                    tt(pad_in(pi, th), ps3(b), sctmp[:, th * 512:(th + 1) * 512].rearrange("p (s t) -> p s t", s=2),
                       ALU.mult, r=[psk[b], 'sctmp'], w=[('pad', pi)])
                    if th == 1:
                        conv3(pi, 'sc_conv', 4, j)

                def c_b(c0, cw, th, b):
                    pi = j % 2
                    tt(mixT[:, 4 + j, th * 512:(th + 1) * 512].rearrange("p (s t) -> p s t", s=2), ps3(b),
                       cacc[pi][:, 2 * th:2 * th + 2, :], ALU.mult, r=[psk[b], 'cacc'], w=[('mixT', 4 + j)])
                return c_c, c_h, c_b
            import os
            SCJ = int(os.environ.get('SCJ', '4'))
            SCP = int(os.environ.get('SCP', '3'))
            for j in range(SCJ):
                c_c, c_h, c_b = mk_sc(j)
                linear(evin_d, 8, [(2064 + 512 + j * 128, 128)], hT, 'hT', c_c)
                if SCP >= 2:
                    linear(evin_d, 8, [(2064 + 1024 + j * 128, 128)], hT, 'hT', c_h)
                if SCP >= 3:
                    linear(evin_d, 8, [(2064 + j * 128, 128)], hT, 'hT', c_b)

            if STAGE < 2:
                return
            def T2(n, s, d=F32):
                t_ = T(n, s, d)
                return [t_, t_]
            gbtok = T2("gbtok", [64, 16])
            gcc = T2("gcc", [64, 4])
            Dgb = T2("Dgb", [64, 2, 4, 64])
            Dm = T2("Dm", [64, 4, 64])
            t1 = T2("t1", [64, 4, 64])
            t2_ = T2("t2", [64, 4, 64])
            E1 = T2("E1", [64, 4, 64])
            E2 = T2("E2", [64, 4, 64])
            decIT = T2("decIT", [64, 4, 64])
            Nn = [T2("Nn%d" % k, [64, 4, 64]) for k in range(2)]
            NTn = [T2("NTn%d" % k, [64, 4, 64]) for k in range(2)]
            TT = T2("TT", [64, 4, 64])
            TTb = T2("TTb", [64, 4, 64], BF16)
            intraT = T2("intraT", [64, 4, 64], BF16)
            ktok = T2("ktok", [64, 4, 128], BF16)
            vtok = T2("vtok", [64, 4, 128], BF16)
            vb = T2("vb", [64, 4, 128], BF16)
            kbg = T2("kbg", [64, 4, 128], BF16)
            kd = T2("kd", [64, 4, 128], BF16)
            bg = T2("bg", [64, 4])
            ekd = T2("ekd", [64, 4])
            egr = T2("egr", [128, 4, 64])
            qdT = T2("qdT", [128, 4, 64], BF16)
            u_sb = T2("u_sb", [64, 4, 128])
            wTb = T2("wTb", [128, 4, 64], BF16)
            e_b = T2("e_b", [64, 4, 128], BF16)

            def v3(ap, a):
                return ap.rearrange("p (a b) -> p a b", a=a)

            for d in range(2):
                mS = cst[0:64, C_SL:C_SL + 64] if d == 0 else cst[0:64, C_SU:C_SU + 64]
                mST = cst[0:64, C_SU:C_SU + 64] if d == 0 else cst[0:64, C_SL:C_SL + 64]
                mIT = cst[0:64, C_U:C_U + 64] if d == 0 else cst[0:64, C_L:C_L + 64]
                cum = cst[0:64, C_U:C_U + 64] if d == 0 else cst[0:64, C_L:C_L + 64]
                last = 63 if d == 0 else 0
                I64 = cst[0:64, C_ID:C_ID + 64]

                def bc1(ap2):
                    return ap2.unsqueeze(1).to_broadcast([64, 4, 64])

                def bc2(ap2, n=64):
                    return ap2.unsqueeze(2).to_broadcast([64, 4, n])
                dma('sp', Sf[:, :, :], s0_d[d].rearrange("h k v -> k h v"), w=['Sf'])
                cp(Sb[:, :, :], Sf[:, :, :], r=['Sf'], w=['Sb'], eng='act')
                order = list(range(16)) if d == 0 else list(range(15, -1, -1))
                GCH = int(os.environ.get('GCH', '16'))
                GP = int(os.environ.get('GP', '99'))
                for step, c in enumerate(order[:GCH]):
                    p = 0
                    K = lambda n: (n, p)
                    csl = slice(c * 64, (c + 1) * 64)
                    b0 = nextps()
                    tr(ps[b0][0:64, 0:8], gT[0:8, csl], ident_f[0:8, 0:8], r=['gT', 'cst'], w=[psk[b0]])
                    tr(ps[b0][0:64, 8:16], betaT[0:8, csl], ident_f[0:8, 0:8], r=['betaT', 'cst'], w=[psk[b0]])
                    cp(gbtok[p][:, :], ps[b0][0:64, 0:16], r=[psk[b0]], w=[K('gbtok')], eng='act')
                    gtok = gbtok[p][:, d * 4:d * 4 + 4]
                    btok = gbtok[p][:, 8 + d * 4:8 + d * 4 + 4]
                    b1 = nextps()
                    mm(ps[b1][0:64, 0:4], cum, gtok, r=['cst', K('gbtok')], w=[psk[b1]])
                    cp(gcc[p][:, :], ps[b1][0:64, 0:4], r=[psk[b1]], w=[K('gcc')], eng='act')
                    if GP < 1:
                        continue
                    tt(Dgb[p][:, 0, :, :], bc1(I64), bc2(gcc[p][:, :]), ALU.mult, r=['cst', K('gcc')], w=[K('Dgb')])
                    tt(Dgb[p][:, 1, :, :], bc1(I64), bc2(btok), ALU.mult, r=['cst', K('gbtok')], w=[K('Dgb')])
                    bR = nextps()
                    mm(ps[bR][:, 0:256], ones_f[0:64, 0:128], Dgb[p][:, 0, :, :].rearrange('p a b -> p (a b)'), r=['ones_f', K('Dgb')], w=[psk[bR]])
                    mm(ps[bR][0:64, 256:512], ones_f[0:64, 0:64], Dgb[p][:, 1, :, :].rearrange('p a b -> p (a b)'), r=['ones_f', K('Dgb')], w=[psk[bR]])
                    grow = v3(ps[bR][:, 0:256], 4)
                    brow = v3(ps[bR][0:64, 256:512], 4)
                    tt(Dm[p][:, :, :], bc2(gcc[p][:, :]), grow[0:64], ALU.subtract, r=[K('gcc'), psk[bR]], w=[K('Dm')])
                    ts(t1[p][:, :, :], Dm[p][:, :, :], 0.0, ALU.min, r=[K('Dm')], w=[K('t1')])
                    ts(t2_[p][:, :, :], Dm[p][:, :, :], -1.0, ALU.mult, r=[K('Dm')], w=[K('t2')], s2=0.0, op1=ALU.min)
                    act(E1[p][:, :, :], t1[p][:, :, :], AF.Exp, r=[K('t1')], w=[K('E1')])
                    act(E2[p][:, :, :], t2_[p][:, :, :], AF.Exp, r=[K('t2')], w=[K('E2')])
                    act(egr[p][:, :, :], grow, AF.Exp, r=[psk[bR]], w=[K('egr')])
                    tt(ekd[p][:, :], grow[0:64, :, last], gcc[p][:, :], ALU.subtract, r=[psk[bR], K('gcc')], w=[K('ekd')])
                    act(ekd[p][:, :], ekd[p][:, :], AF.Exp, r=[K('ekd')], w=[K('ekd')])
                    act(bg[p][:, :], gcc[p][:, :], AF.Exp, r=[K('gcc')], w=[K('bg')])
                    tt(bg[p][:, :], bg[p][:, :], btok, ALU.mult, r=[K('bg'), K('gbtok')], w=[K('bg')])
                    tt(decIT[p][:, :, :], E2[p][:, :, :], bc1(mIT), ALU.mult, r=[K('E2'), 'cst'], w=[K('decIT')])
                    tt(E1[p][:, :, :], E1[p][:, :, :], bc1(mS), ALU.mult, r=[K('E1'), 'cst'], w=[K('E1')])
                    tt(E2[p][:, :, :], E2[p][:, :, :], bc1(mST), ALU.mult, r=[K('E2'), 'cst'], w=[K('E2')])
                    if GP < 2:
                        continue
                    bK = nextps()
                    for h in range(4):
                        mm(ps[bK][0:64, h * 64:(h + 1) * 64], kT[:, h, csl], kT[:, h, csl], r=[('kT', h)], w=[psk[bK]])
                        mm(ps[bK][0:64, 256 + h * 64:256 + (h + 1) * 64], kT[:, h, csl], qT[:, h, csl],
                           r=[('kT', h), ('qT', h)], w=[psk[bK]])
                    pKK = v3(ps[bK][0:64, 0:256], 4)
                    pQK = v3(ps[bK][0:64, 256:512], 4)
                    N0, NT0 = Nn[0][p], NTn[0][p]
                    tt(t1[p][:, :, :], pKK, E1[p][:, :, :], ALU.mult, r=[psk[bK], K('E1')], w=[K('t1')])
                    stt(N0[:, :, :], t1[p][:, :, :], -1.0, bc2(btok), ALU.mult, ALU.mult,
                        r=[K('t1'), K('gbtok')], w=[K('Nn0')])
                    tt(t2_[p][:, :, :], pKK, E2[p][:, :, :], ALU.mult, r=[psk[bK], K('E2')], w=[K('t2')])
                    stt(NT0[:, :, :], t2_[p][:, :, :], -1.0, brow, ALU.mult, ALU.mult,
                        r=[K('t2'), psk[bR]], w=[K('NTn0')])
                    tt(TT[p][:, :, :], NT0[:, :, :], bc1(I64), ALU.add, r=[K('NTn0'), 'cst'], w=[K('TT')])
                    tt(intraT[p][:, :, :], pQK, decIT[p][:, :, :], ALU.mult, r=[psk[bK], K('decIT')], w=[K('intraT')])
                    if GP < 3:
                        continue
                    NLEV = int(os.environ.get('NLEV', '5'))
                    NPART = int(os.environ.get('NPART', '3'))
                    for lev in range(1, NLEV + 1):
                        a, bprev = lev % 2, (lev - 1) % 2
                        Np, NTp = Nn[bprev][p], NTn[bprev][p]
                        Nc, NTc = Nn[a][p], NTn[a][p]
                        bn = nextps()
                        for h in range(4):
                            mm(ps[bn][0:64, h * 64:(h + 1) * 64], NTp[:, h, :], Np[:, h, :],
                               r=[K('Nn%d' % bprev), K('NTn%d' % bprev)], w=[psk[bn]])
                        if lev < 5 and NPART >= 2:
                            for h in range(4):
                                mm(ps[bn][0:64, 256 + h * 64:256 + (h + 1) * 64], Np[:, h, :], NTp[:, h, :],
                                   r=[K('Nn%d' % bprev), K('NTn%d' % bprev)], w=[psk[bn]])
                        cp(Nc[:, :, :], v3(ps[bn][0:64, 0:256], 4), r=[psk[bn]], w=[K('Nn%d' % a)], eng='act')
                        if lev < 5 and NPART >= 2:
                            cp(NTc[:, :, :], v3(ps[bn][0:64, 256:512], 4), r=[psk[bn]], w=[K('NTn%d' % a)], eng='act')
                        if NPART < 3:
                            continue
                        bt = nextps()
                        for h in range(4):
                            mm(ps[bt][0:64, h * 64:(h + 1) * 64], Nc[:, h, :], TT[p][:, h, :],
                               r=[K('Nn%d' % a), K('TT')], w=[psk[bt]])
                        tt(TT[p][:, :, :], TT[p][:, :, :], v3(ps[bt][0:64, 0:256], 4), ALU.add,
                           r=[K('TT'), psk[bt]], w=[K('TT')])
                    cp(TTb[p][:, :, :], TT[p][:, :, :], r=[K('TT')], w=[K('TTb')], eng='act')
                    if GP < 4:
                        continue
                    bkv = nextps()
                    pkv = ps[bkv][:, :].bitcast(BF16)
                    for h in range(4):
                        tr(pkv[0:64, h * 128:(h + 1) * 128], kT[:, h, csl], ident_bf[:, :], r=[('kT', h), 'ident_bf'], w=[psk[bkv]])
                        tr(pkv[0:64, 512 + h * 128:512 + (h + 1) * 128], vT[:, h, csl], ident_bf[:, :],
                           r=[('vT', h), 'ident_bf'], w=[psk[bkv]])
                    cp(ktok[p][:, :, :], v3(pkv[0:64, 0:512], 4), r=[psk[bkv]], w=[K('ktok')], eng='act')
                    cp(vtok[p][:, :, :], v3(pkv[0:64, 512:1024], 4), r=[psk[bkv]], w=[K('vtok')], eng='act')
                    tt(vb[p][:, :, :], vtok[p][:, :, :], bc2(btok, 128), ALU.mult, r=[K('vtok'), K('gbtok')], w=[K('vb')])
                    tt(kbg[p][:, :, :], ktok[p][:, :, :], bc2(bg[p][:, :], 128), ALU.mult, r=[K('ktok'), K('bg')], w=[K('kbg')])
                    tt(kd[p][:, :, :], ktok[p][:, :, :], bc2(ekd[p][:, :], 128), ALU.mult, r=[K('ktok'), K('ekd')], w=[K('kd')])
                    tt(qdT[p][:, :, :], qT[:, :, csl], egr[p][:, :, :], ALU.mult,
                       r=[('qT', 0), ('qT', 1), ('qT', 2), ('qT', 3), K('egr')], w=[K('qdT')])
                    if GP < 5:
                        continue
                    bu = nextps()
                    bw = nextps()
                    for h in range(4):
                        mm(ps[bu][0:64, h * 128:(h + 1) * 128], TTb[p][:, h, :], vb[p][:, h, :],
                           r=[K('TTb'), K('vb')], w=[psk[bu]])
                        mm(ps[bw][:, h * 64:(h + 1) * 64], kbg[p][:, h, :], TTb[p][:, h, :],
                           r=[K('kbg'), K('TTb')], w=[psk[bw]])
                    cp(u_sb[p][:, :, :], v3(ps[bu][0:64, :], 4), r=[psk[bu]], w=[K('u_sb')], eng='act')
                    cp(wTb[p][:, :, :], v3(ps[bw][:, 0:256], 4), r=[psk[bw]], w=[K('wTb')], eng='act')
                    if GP < 6:
                        continue
                    be = nextps()
                    for h in range(4):
                        mm(ps[be][0:64, h * 128:(h + 1) * 128], wTb[p][:, h, :], Sb[:, h, :], r=[K('wTb'), 'Sb'], w=[psk[be]])
                    tt(e_b[p][:, :, :], u_sb[p][:, :, :], v3(ps[be][0:64, :], 4), ALU.subtract,
                       r=[K('u_sb'), psk[be]], w=[K('e_b')])
                    bo = nextps()
                    for h in range(4):
                        mm(ps[bo][:, h * 64:(h + 1) * 64], Sb[:, h, :], qdT[p][:, h, :], start=True, stop=False,
                           r=['Sb', K('qdT')], w=[psk[bo]])
                        mm(ps[bo][:, h * 64:(h + 1) * 64], e_b[p][:, h, :], intraT[p][:, h, :], start=False, stop=True,
                           r=[K('e_b'), K('intraT')], w=[psk[bo]])
                    po = v3(ps[bo][:, 0:256], 4)
                    if d == 0:
                        cp(oacc[:, :, csl], po, r=[psk[bo]], w=[('oacc', c)], eng='act')
                    else:
                        tt(oacc[:, :, csl], oacc[:, :, csl], po, ALU.add, r=[psk[bo], ('oacc', c)], w=[('oacc', c)])
                    bs = nextps()
                    for h in range(4):
                        mm(ps[bs][:, h * 128:(h + 1) * 128], kd[p][:, h, :], e_b[p][:, h, :], r=[K('kd'), K('e_b')], w=[psk[bs]])
                    for h in range(4):
                        stt(Sf[:, h, :], Sf[:, h, :], egr[p][:, h, last:last + 1], ps[bs][:, h * 128:(h + 1) * 128],
                            ALU.mult, ALU.add, r=['Sf', K('egr'), psk[bs]], w=['Sf'])
                    seg_end = (c % 4 == 3) if d == 0 else (c % 4 == 0)
                    if seg_end:
                        seg = c // 4
                        dma('sp', gso_d[d, seg].rearrange("h k v -> k h v"), Sf[:, :, :], r=['Sf'])
                        ts(Sf[:, :, :], Sf[:, :, :], flag[:, 0:1], ALU.mult, r=['Sf', 'flag'], w=['Sf'])
                    cp(Sb[:, :, :], Sf[:, :, :], r=['Sf'], w=['Sb'], eng='act')

            if STAGE < 3:
                return
            for h in range(4):
                act(qsq[:, :], oacc[:, h, :], AF.Square, r=[('oacc', c) for c in range(16)], w=['qsq'])
                for t2 in range(2):
                    bb = nextps()
                    tsl = slice(t2 * 512, (t2 + 1) * 512)
                    mm(ps[bb][:, :], ones_bf[:, :], qsq[:, tsl], r=['ones_bf', 'qsq'], w=[psk[bb]])
                    act(rinv[:, tsl], ps[bb][:, :], AF.Sqrt, r=[psk[bb], 'epsb'], w=['rinv'],
                        bias=epsb[:, 0:1], scale=1.0 / 128)
                rcp(rinv[:, :], rinv[:, :], r=['rinv'], w=['rinv'])
                stt(sctmp[:, :], oacc[:, h, :], ppc('gdn_norm')[:, 0:1], rinv[:, :], ALU.mult, ALU.mult,
                    r=[('oacc', c) for c in range(16)] + ['pp', 'rinv'], w=['sctmp'])
                tt(mixT[:, h, :], sctmp[:, :], mixT[:, h, :], ALU.mult, r=['sctmp', ('mixT', h)], w=[('mixT', h)])

            def c_out(c0, cw, th, b):
                fc = c0 // 128
                tsl = slice(th * 512, (th + 1) * 512)
                stt(x_sb[:, fc, tsl], ps[b][:, :], modv[:, 0, 16 + fc:17 + fc], x_sb[:, fc, tsl],
                    ALU.mult, ALU.add, r=[psk[b], 'modv', ('x', fc)], w=[('x', fc)])
            linear(evout_d, 8, [(i * 512, 512) for i in range(2)], mixT, 'mixT', c_out)

        def layer1_mixer():
            with contextlib.ExitStack() as so:
                mix1T = so.enter_context(nc.sbuf_tensor("sb_mix1T", [128, 8, NT], BF16))
                with contextlib.ExitStack() as sc:
                    alloc_work(sc, 'p3')
                    attention_phase(sc, mix1T)
                    S.flush()
                if L1PART >= 2:
                    with contextlib.ExitStack() as sr:
                        rwkv_phases(sr, mix1T)
                with contextlib.ExitStack() as sc:
                    alloc_work(sc, 'p6')

                    def c_out(c0, cw, th, b):
                        fc = c0 // 128
                        tsl = slice(th * 512, (th + 1) * 512)
                        stt(x_sb[:, fc, tsl], ps[b][:, :], modv[:, 1, 16 + fc:17 + fc], x_sb[:, fc, tsl],
                            ALU.mult, ALU.add, r=[psk[b], 'modv', ('x', fc)], w=[('x', fc)])
                    linear(odout_d, 8, [(i * 512, 512) for i in range(2)], mix1T, 'mix1T', c_out)
                    S.flush()

        def attention_phase(sc, mix1T):
            T = lambda n, s, d=F32: sc.enter_context(nc.sbuf_tensor('sb_a_' + n, list(s), d))
            qTr = T("qTr", [128, 4, NT], BF16)
            kpad = T("kpad", [128, 2, NT + 256], BF16)
            vTb = T("vTb", [128, NT], BF16)
            vtokp = T("vtokp", [128, 10, 128], BF16)
            ckT = T("ckT", [128, 2, 512], BF16)
            cvt = T("cvt", [128, 4, 128], BF16)
            cosT = T("cosT", [128, NT])
            sinT = T("sinT", [128, NT])
            Rm = T("Rm", [128, 128])
            xf = T("xf", [128, NT])
            xr = T("xr", [128, NT])
            mb = [T("mb%d" % i, [128, 896]) for i in range(2)]
            scs2 = [T("scs%d" % i, [128, 896]) for i in range(2)]
            Pb2 = [T("Pb%d" % i, [128, 896], BF16) for i in range(2)]
            PT = T("PT", [128, 7, 128], BF16)
            atok2 = [T("atok%d" % i, [128, 512], BF16) for i in range(2)]
            sm2 = [T("sm%d" % i, [128, 8]) for i in range(2)]

            dma('sp', cosT[:, :], ropec_d[:, :], w=['cosT'])
            dma('sp', sinT[:, :], ropes_d[:, :], w=['sinT'])
            dma('sp', Rm[:, :], rm_d[:, :], w=['Rm'])
            for kvh in range(2):
                dma('pool', ckT[:, kvh, :], ckT_d[kvh], w=['ckT'])
            dma('pool', cvt[:, :, :], cvt_d[:, :, :], w=['cvt'])
            mset(kpad[:, :, 0:128], 0.0, w=['kpad'])
            mset(kpad[:, :, 128 + NT:256 + NT], 0.0, w=['kpad'])
            mset(vtokp[:, 0, :], 0.0, w=['vtokp'])
            mset(vtokp[:, 9, :], 0.0, w=['vtokp'])

            norm_mod(gsA[:, 1, :], modv[:, 1, 0:8], hT, 'hT')

            def rope(dst_ap_fn, dkey):
                for t2 in range(2):
                    tsl = slice(t2 * 512, (t2 + 1) * 512)
                    bb = nextps()
                    mm(ps[bb][:, :], Rm[:, :], xf[:, tsl], r=['Rm', 'xf'], w=[psk[bb]])
                    tt(xr[:, tsl], ps[bb][:, :], sinT[:, tsl], ALU.mult, r=[psk[bb], 'sinT'], w=['xr'])
                    tt(xf[:, tsl], xf[:, tsl], cosT[:, tsl], ALU.mult, r=['xf', 'cosT'], w=['xf'])
                    tt(dst_ap_fn(tsl), xf[:, tsl], xr[:, tsl], ALU.add, r=['xf', 'xr'], w=[dkey])

            def c_q(c0, cw, th, b):
                a = c0 // 128
                cp(xf[:, th * 512:(th + 1) * 512], ps[b][:, :], r=[psk[b]], w=['xf'], eng='act')
                if th == 1:
                    rope(lambda tsl: qTr[:, a, tsl], 'qTr')
            linear(odin_d, 8, [(0, 512)], hT, 'hT', c_q)

            def c_k(c0, cw, th, b):
                kvh = (c0 - 512) // 128
                cp(xf[:, th * 512:(th + 1) * 512], ps[b][:, :], r=[psk[b]], w=['xf'], eng='act')
                if th == 1:
                    dma('sp', kvo_d[0, kvh * 64:(kvh + 1) * 64, :], xf[0:64, :], r=['xf'])
                    rope(lambda tsl: kpad[:, kvh, 128 + tsl.start:128 + tsl.stop], 'kpad')
            linear(odin_d, 8, [(512, 256)], hT, 'hT', c_k)

            def c_v(c0, cw, th, b):
                cp(xf[:, th * 512:(th + 1) * 512], ps[b][:, :], r=[psk[b]], w=['xf'], eng='act')
                if th == 1:
                    dma('sp', kvo_d[1, :, :], xf[:, :], r=['xf'])
                    cp(vTb[:, :], xf[:, :], r=['xf'], w=['vTb'], eng='act')
                    for blk in range(8):
                        bb = nextps()
                        pv_ = ps[bb][:, :].bitcast(BF16)
                        tr(pv_[:, 0:128], vTb[:, blk * 128:(blk + 1) * 128], ident_bf[:, :], r=['vTb', 'ident_bf'], w=[psk[bb]])
                        cp(vtokp[:, 1 + blk, :], pv_[:, 0:128], r=[psk[bb]], w=['vtokp'], eng='act')
            linear(odin_d, 8, [(768, 128)], hT, 'hT', c_v)

            sink = ppc('sink')

            def stageA(u):
                n, h = u // 8, u % 8
                ub = u % 2
                mbi = n % 2
                if h == 0:
                    dma('sp', mb[mbi][:, :], maskb_d[n], w=[('mb', mbi)])
                kvh, hp, a = h // 4, h % 2, h // 2
                pb_ = slice(hp * 64, (hp + 1) * 64)
                qap = qTr[pb_, a, n * 128:(n + 1) * 128]
                b1 = nextps()
                b2 = nextps()
                mm(ps[b1][:, 0:384], qap, kpad[pb_, kvh, n * 128:n * 128 + 384], r=['qTr', 'kpad'], w=[psk[b1]])
                mm(ps[b2][:, :], qap, ckT[pb_, kvh, :], r=['qTr', 'ckT'], w=[psk[b2]])
                sc_, P_, sm_ = scs2[ub], Pb2[ub], sm2[ub]
                ks, kp, km = ('scs', ub), ('Pb', ub), ('sm', ub)
                stt(sc_[:, 0:384], ps[b1][:, 0:384], 0.125, mb[mbi][:, 0:384], ALU.mult, ALU.add,
                    r=[psk[b1], ('mb', mbi)], w=[ks])
                stt(sc_[:, 384:896], ps[b2][:, :], 0.125, mb[mbi][:, 384:896], ALU.mult, ALU.add,
                    r=[psk[b2], ('mb', mbi)], w=[ks])
                rmax(sm_[:, 0:1], sc_[:, :], r=[ks], w=[km])
                tt(sm_[:, 1:2], sm_[:, 0:1], sink[:, h:h + 1], ALU.max, r=[km, 'pp'], w=[km])
                ts(sm_[:, 2:3], sm_[:, 1:2], -1.0, ALU.mult, r=[km], w=[km])
                expacc(P_[:, :], sc_[:, :], sm_[:, 2:3], sm_[:, 3:4], r=[ks, km], w=[kp, km])
                act(sm_[:, 4:5], sink[:, h:h + 1], AF.Exp, r=['pp', km], w=[km], bias=sm_[:, 2:3])
                tt(sm_[:, 5:6], sm_[:, 3:4], sm_[:, 4:5], ALU.add, r=[km], w=[km])
                rcp(sm_[:, 6:7], sm_[:, 5:6], r=[km], w=[km])

            def stageB(u):
                n, h = u // 8, u % 8
                ub = u % 2
                kvh = h // 4
                P_, sm_ = Pb2[ub], sm2[ub]
                kp, km = ('Pb', ub), ('sm', ub)
                at_ = atok2[n % 2]
                ka_ = ('atok', n % 2)
                bt_ = nextps()
                ptp = ps[bt_][:, :].bitcast(BF16)
                for kb in range(7):
                    tr(ptp[:, kb * 128:(kb + 1) * 128], P_[:, kb * 128:(kb + 1) * 128], ident_bf[:, :],
                       r=[kp, 'ident_bf'], w=[psk[bt_]])
                cp(PT[:, :, :], ptp[:, 0:896].rearrange("p (a b) -> p a b", a=7), r=[psk[bt_]], w=['PT'], eng='act')
                bo_ = nextps()
                for kb in range(7):
                    if kb < 3:
                        vap = vtokp[:, n + kb, kvh * 64:(kvh + 1) * 64]
                        rk = 'vtokp'
                    else:
                        vap = cvt[:, kb - 3, kvh * 64:(kvh + 1) * 64]
                        rk = 'cvt'
                    mm(ps[bo_][:, 0:64], PT[:, kb, :], vap, start=(kb == 0), stop=(kb == 6), r=['PT', rk], w=[psk[bo_]])
                act(at_[:, h * 64:(h + 1) * 64], ps[bo_][:, 0:64], AF.Identity, r=[psk[bo_], km], w=[ka_],
                    scale=sm_[:, 6:7])
                if h == 7:
                    ba_ = nextps()
                    pa_ = ps[ba_][:, :].bitcast(BF16)
                    for a in range(4):
                        tr(pa_[:, a * 128:(a + 1) * 128], at_[:, a * 128:(a + 1) * 128], ident_bf[:, :],
                           r=[ka_, 'ident_bf'], w=[psk[ba_]])
                    cp(mix1T[:, 0:4, n * 128:(n + 1) * 128], pa_[:, 0:512].rearrange("p (a b) -> p a b", a=4),
                       r=[psk[ba_]], w=[('mix1T', 0), ('mix1T', 1), ('mix1T', 2), ('mix1T', 3)], eng='act')
            stageA(0)
            for u in range(1, 64):
                stageA(u)
                stageB(u - 1)
            stageB(63)

        def rwkv_phases(sr, mix1T):
            TR = lambda n, s, d=F32: sr.enter_context(nc.sbuf_tensor('sb_r_' + n, list(s), d))
            r_b = TR("r_b", [128, 4, NT], BF16)
            k_b = TR("k_b", [128, 4, NT], BF16)
            kk_b = TR("kk_b", [128, 4, NT], BF16)
            v_b = TR("v_b", [128, 4, NT], BF16)
            a_b = TR("a_b", [128, 2, 4, NT], BF16)
            wlt = TR("wlt", [128, NT], BF16)
            bon = TR("bon", [128, 4, NT], BF16)
            gate = TR("gate", [128, 4, NT], BF16)
            wup = TR("wup", [128, 512], BF16)
            bones = TR("bones", [128, 128], BF16)
            wmid = TR("wmid", [128, 15])
            dma('pool', wup[:, :], wup_d[:, :], w=['wup'])
            dma('pool', bones[:, :], bones_d[:, :], w=['bones'])
            mu0 = ppc('mu', 0, 15)
            mu1 = ppc('mu', 15, 15)
            tt(wmid[:, :], mu0, mu1, ALU.add, r=['pp'], w=['wmid'])
            ts(wmid[:, :], wmid[:, :], -1.0, ALU.mult, r=['wmid'], w=['wmid'], s2=1.0, op1=ALU.add)

            with contextlib.ExitStack() as sc:
                alloc_work(sc, 'p4')
                T = lambda n, s, d=F32: sc.enter_context(nc.sbuf_tensor('sb_rp_' + n, list(s), d))
                pad = [T("pad%d" % i, [128, 4, 258]) for i in range(2)]
                xs_ = T("xs", [128, 4, 256])
                t1 = T("t1", [128, NT])
                t2 = T("t2", [128, NT])
                sqh = T("sqh", [128, NT], BF16)
                alb = T("alb", [128, NT], BF16)
                glb = T("glb", [128, NT], BF16)
                aup = T("aup", [128, 512], BF16)
                gup = T("gup", [128, 512], BF16)
                dma('pool', aup[:, :], aup_d[:, :], w=['aup'])
                dma('pool', gup[:, :], gup_d[:, :], w=['gup'])
                for i in range(2):
                    mset(pad[i][:, :, :], 0.0, w=[('rpad', i)])
                norm_mod(gsA[:, 1, :], modv[:, 1, 0:8], hT, 'hT')
                xsf = xs_[:, :, :].rearrange("p s t -> p (s t)")

                def hsum(dst_ps_fn, src_bf):
                    for t2_ in range(2):
                        bb = nextps()
                        mm(ps[bb][:, :], bones[:, :], src_bf[:, t2_ * 512:(t2_ + 1) * 512], r=['bones', 'sqh'], w=[psk[bb]])
                        dst_ps_fn(t2_, bb)

                def c_rw(c0, cw, th, b):
                    j = (c0 - 896) // 128
                    pi = j % 2
                    p = pad[pi]
                    cp(p[:, 2 * th:2 * th + 2, 1:257], ps[b][:, :].rearrange("p (s t) -> p s t", s=2),
                       r=[psk[b]], w=[('rpad', pi)], eng='act')
                    if th == 0:
                        return
                    ts(p[:, 1:4, 0:1], p[:, 0:3, 256:257], flag[:, 0:1], ALU.mult, r=[('rpad', pi), 'flag'], w=[('rpad', pi)])
                    ts(p[:, 0:3, 257:258], p[:, 1:4, 1:2], flag[:, 0:1], ALU.mult, r=[('rpad', pi), 'flag'], w=[('rpad', pi)])
                    ts(xs_[:, :, :], p[:, :, 0:256], mu0[:, j:j + 1], ALU.mult, r=[('rpad', pi), 'pp'], w=['xs'])
                    stt(xs_[:, :, :], p[:, :, 1:257], wmid[:, j:j + 1], xs_[:, :, :], ALU.mult, ALU.add,
                        r=[('rpad', pi), 'wmid', 'xs'], w=['xs'])
                    stt(xs_[:, :, :], p[:, :, 2:258], mu1[:, j:j + 1], xs_[:, :, :], ALU.mult, ALU.add,
                        r=[('rpad', pi), 'pp', 'xs'], w=['xs'])
                    cq = j % 4
                    if j < 4:
                        cp(r_b[:, cq, :], xsf, r=['xs'], w=[('r_b', cq)], eng='act')
                    elif j < 8:
                        cp(k_b[:, cq, :], xsf, r=['xs'], w=[('k_b', cq)], eng='act')
                        ts(t1[:, :], xsf, ppc('k_k')[:, cq:cq + 1], ALU.mult, r=['xs', 'pp'], w=['t1'])
                        act(sqh[:, :], t1[:, :], AF.Square, r=['t1'], w=['sqh'])

                        def d1(t2_, bb):
                            act(t2[:, t2_ * 512:(t2_ + 1) * 512], ps[bb][:, :], AF.Sqrt, r=[psk[bb], 'epsb'], w=['t2'],
                                bias=epsb[:, 1:2])
                        hsum(d1, sqh)
                        rcp(t2[:, :], t2[:, :], r=['t2'], w=['t2'])
                        tt(kk_b[:, cq, :], t1[:, :], t2[:, :], ALU.mult, r=['t1', 't2'], w=[('kk_b', cq)])
                    elif j < 12:
                        cp(v_b[:, cq, :], xsf, r=['xs'], w=[('v_b', cq)], eng='act')
                    elif j == 12:
                        act(wlt[:, :], xsf, AF.Tanh, r=['xs'], w=['wlt'])
                    elif j == 13:
                        cp(alb[:, :], xsf, r=['xs'], w=['alb'], eng='act')
                        for d in range(2):
                            dsl = slice(d * 64, (d + 1) * 64)
                            for cq2 in range(4):
                                for t2_ in range(2):
                                    bb = nextps()
                                    mm(ps[bb][:, :], aup[dsl, cq2 * 128:(cq2 + 1) * 128], alb[dsl, t2_ * 512:(t2_ + 1) * 512],
                                       r=['aup', 'alb'], w=[psk[bb]])
                                    act(a_b[:, d, cq2, t2_ * 512:(t2_ + 1) * 512], ps[bb][:, :], AF.Sigmoid,
                                        r=[psk[bb], 'pp'], w=[('a_b', d, cq2)], bias=ppc('a0')[:, d * 4 + cq2:d * 4 + cq2 + 1])
                    else:
                        act(glb[:, :], xsf, AF.Sigmoid, r=['xs'], w=['glb'])
                        for cq2 in range(4):
                            for t2_ in range(2):
                                bb = nextps()
                                mm(ps[bb][:, :], gup[:, cq2 * 128:(cq2 + 1) * 128], glb[:, t2_ * 512:(t2_ + 1) * 512],
                                   r=['gup', 'glb'], w=[psk[bb]])
                                cp(gate[:, cq2, t2_ * 512:(t2_ + 1) * 512], ps[bb][:, :], r=[psk[bb]], w=[('gate', cq2)], eng='act')
                linear(odin_d, 8, [(896 + j * 128, 128) for j in range(15)], hT, 'hT', c_rw)
                for cq in range(4):
                    for d in range(2):
                        ts(t1[:, :], a_b[:, d, cq, :], -1.0, ALU.add, r=[('a_b', d, cq), 'pp'], w=['t1'],
                           s2=ppc('k_a')[:, cq:cq + 1], op1=ALU.mult)
                        stt(t2[:, :] if d == 0 else t1[:, :], t1[:, :], 1.0, k_b[:, cq, :], ALU.add, ALU.mult,
                            r=['t1', ('k_b', cq)], w=['t2' if d == 0 else 't1'])
                    tt(t2[:, :], t2[:, :], t1[:, :], ALU.add, r=['t1', 't2'], w=['t2'])
                    stt(sqh[:, :], t2[:, :], ppc('r_k')[:, cq:cq + 1], r_b[:, cq, :], ALU.mult, ALU.mult,
                        r=['t2', 'pp', ('r_b', cq)], w=['sqh'])

                    def d2(t2_, bb):
                        tsl = slice(t2_ * 512, (t2_ + 1) * 512)
                        tt(bon[:, cq, tsl], ps[bb][:, :], v_b[:, cq, tsl], ALU.mult, r=[psk[bb], ('v_b', cq)], w=[('bon', cq)])
                    hsum(d2, sqh)
                S.flush()

            if L1PART < 3:
                return
            with contextlib.ExitStack() as sc:
                state['drain'] = True
                state['pemode'] = None
                T = lambda n, s, d=F32: sc.enter_context(nc.sbuf_tensor('sb_rs_' + n, list(s), d))
                lw = T("lw", [128, 4, NT])
                yacc = T("yacc", [128, 4, NT])
                Pf = T("Pf", [128, 4, 64])
                Pbf = T("Pbf", [128, 4, 64], BF16)
                Lf = T("Lf", [128, 4, 64])
                Lam = T("Lam", [128, 4, 64])
                e1 = T("e1", [128, 4, 64])
                e2 = T("e2", [128, 4, 64])
                e3 = T("e3", [128, 4, 64])
                e4 = T("e4", [128, 4, 64])
                eTot = T("eTot", [128, 4])
                ka = T("ka", [128, 4, 64])
                k2 = T("k2", [128, 4, 64])
                rt = T("rt", [128, 4, 64], BF16)
                bt = T("bt", [128, 4, 64], BF16)
                at = T("at", [128, 4, 64], BF16)
                kt = T("kt", [128, 4, 64], BF16)
                aG = T("aG", [128, 4, 64], BF16)
                kG = T("kG", [128, 4, 64], BF16)
                Nn = [T("Nn%d" % i, [64, 8, 64]) for i in range(2)]
                NTn = [T("NTn%d" % i, [64, 8, 64]) for i in range(2)]
                TT = T("TT", [64, 8, 64])
                TTb = T("TTb", [64, 8, 64], BF16)
                Tn = T("Tn", [64, 8, 64])
                O32a = T("O32a", [64, 8, 64])
                O32Ta = T("O32Ta", [64, 8, 64])
                O64a = T("O64a", [64, 8, 64])
                BTm = T("BTm", [64, 8, 64], BF16)
                RaT = T("RaT", [64, 8, 64], BF16)
                RkT = T("RkT", [64, 8, 64], BF16)
                btok = T("btok", [64, 4, 128], BF16)
                aGtok = T("aGtok", [64, 4, 128], BF16)
                kGtok = T("kGtok", [64, 4, 128], BF16)
                vtok = T("vtok", [64, 4, 128], BF16)
                BVb = T("BVb", [64, 8, 64], BF16)
                U0 = T("U0", [64, 8, 64])
                Ub = T("Ub", [64, 8, 64], BF16)
                WmTb = T("WmTb", [128, 4, 64], BF16)
                onesc = T("onesc", [128, 64])
                mset(onesc[:, :], 1.0, w=['onesc'])
                I64 = cst[0:64, C_ID:C_ID + 64]
                HORD = [0, 2, 4, 6, 1, 3, 5, 7]

                def v3(ap, a):
                    return ap.rearrange("p (a b) -> p a b", a=a)

                def bc8(ap2):
                    return ap2.unsqueeze(1).to_broadcast([64, 8, 64])

                for d in range(2):
                    dsl = slice(d * 64, (d + 1) * 64)
                    mS = cst[0:64, C_SL:C_SL + 64] if d == 0 else cst[0:64, C_SU:C_SU + 64]
                    mST = cst[0:64, C_SU:C_SU + 64] if d == 0 else cst[0:64, C_SL:C_SL + 64]
                    mIT = cst[0:64, C_U:C_U + 64] if d == 0 else cst[0:64, C_L:C_L + 64]
                    for cq in range(4):
                        for t2_ in range(2):
                            bb = nextps()
                            mm(ps[bb][:, :], wup[dsl, cq * 128:(cq + 1) * 128], wlt[dsl, t2_ * 512:(t2_ + 1) * 512],
                               r=['wup', 'wlt'], w=[psk[bb]])
                            act(lw[:, cq, t2_ * 512:(t2_ + 1) * 512], ps[bb][:, :], AF.Sigmoid, r=[psk[bb], 'pp'], w=['lw'],
                                bias=ppc('w0')[:, d * 4 + cq:d * 4 + cq + 1])
                    ts(lw[:, :, :], lw[:, :, :], -0.6065306597126334, ALU.mult, r=['lw'], w=['lw'])
                    dma('sp', Pf[:, :, :], rs0_d[d], w=['Pf'])
                    cp(Pbf[:, :, :], Pf[:, :, :], r=['Pf'], w=['Pbf'], eng='act')
                    order = list(range(16)) if d == 0 else list(range(15, -1, -1))
                    RCH = int(os.environ.get('RCH', '16'))
                    for c in order[:RCH]:
                        csl = slice(c * 64, (c + 1) * 64)
                        lwc = lw[:, :, csl]
                        for cq in range(4):
                            scan(Lf[:, cq, :], onesc[:, :], lw[:, cq, csl], r=['lw', 'onesc'], w=['Lf'])
                        tot = Lf[:, :, 63:64]
                        if d == 0:
                            LamT, lk = Lf, 'Lf'
                        else:
                            tt(Lam[:, :, :], tot.to_broadcast([128, 4, 64]), Lf[:, :, :], ALU.subtract, r=['Lf'], w=['Lam'])
                            tt(Lam[:, :, :], Lam[:, :, :], lwc, ALU.add, r=['Lam', 'lw'], w=['Lam'])
                            LamT, lk = Lam, 'Lam'
                        act(e1[:, :, :], LamT[:, :, :], AF.Exp, r=[lk], w=['e1'])
                        tt(e2[:, :, :], LamT[:, :, :], lwc, ALU.subtract, r=[lk, 'lw'], w=['e2'])
                        act(e2[:, :, :], e2[:, :, :], AF.Exp, r=['e2'], w=['e2'])
                        act(e3[:, :, :], LamT[:, :, :], AF.Exp, r=[lk], w=['e3'], scale=-1.0)
                        tt(e4[:, :, :], tot.to_broadcast([128, 4, 64]), LamT[:, :, :], ALU.subtract, r=['Lf', lk], w=['e4'])
                        act(e4[:, :, :], e4[:, :, :], AF.Exp, r=['e4'], w=['e4'])
                        act(eTot[:, :], Lf[:, :, 63], AF.Exp, r=['Lf'], w=['eTot'])
                        stt(ka[:, :, :], kk_b[:, :, csl], -1.0, a_b[:, d, :, csl], ALU.mult, ALU.mult,
                            r=[('kk_b', q_) for q_ in range(4)] + [('a_b', d, q_) for q_ in range(4)], w=['ka'])
                        for cq in range(4):
                            ts(k2[:, cq, :], a_b[:, d, cq, csl], -1.0, ALU.add, r=[('a_b', d, cq), 'pp'], w=['k2'],
                               s2=ppc('k_a')[:, cq:cq + 1], op1=ALU.mult)
                        stt(k2[:, :, :], k2[:, :, :], 1.0, k_b[:, :, csl], ALU.add, ALU.mult,
                            r=['k2'] + [('k_b', q_) for q_ in range(4)], w=['k2'])
                        tt(rt[:, :, :], r_b[:, :, csl], e1[:, :, :], ALU.mult, r=[('r_b', q_) for q_ in range(4)] + ['e1'], w=['rt'])
                        tt(bt[:, :, :], kk_b[:, :, csl], e2[:, :, :], ALU.mult, r=[('kk_b', q_) for q_ in range(4)] + ['e2'], w=['bt'])
                        tt(at[:, :, :], ka[:, :, :], e3[:, :, :], ALU.mult, r=['ka', 'e3'], w=['at'])
                        tt(kt[:, :, :], k2[:, :, :], e3[:, :, :], ALU.mult, r=['k2', 'e3'], w=['kt'])
                        tt(aG[:, :, :], ka[:, :, :], e4[:, :, :], ALU.mult, r=['ka', 'e4'], w=['aG'])
                        tt(kG[:, :, :], k2[:, :, :], e4[:, :, :], ALU.mult, r=['k2', 'e4'], w=['kG'])
                        RP = int(os.environ.get('RP', '9'))
                        if RP < 2:
                            continue
                        bA, bAT, bBT, bRa, bRk = nextps(), nextps(), nextps(), nextps(), nextps()
                        for h in HORD:
                            cq, hp = h // 2, h % 2
                            pb_ = slice(hp * 64, (hp + 1) * 64)
                            hs = slice(h * 64, (h + 1) * 64)
                            mm(ps[bA][0:64, hs], bt[pb_, cq, :], at[pb_, cq, :], r=['bt', 'at'], w=[psk[bA]])
                            mm(ps[bAT][0:64, hs], at[pb_, cq, :], bt[pb_, cq, :], r=['bt', 'at'], w=[psk[bAT]])
                            mm(ps[bBT][0:64, hs], kt[pb_, cq, :], bt[pb_, cq, :], r=['bt', 'kt'], w=[psk[bBT]])
                            mm(ps[bRa][0:64, hs], at[pb_, cq, :], rt[pb_, cq, :], r=['rt', 'at'], w=[psk[bRa]])
                            mm(ps[bRk][0:64, hs], kt[pb_, cq, :], rt[pb_, cq, :], r=['rt', 'kt'], w=[psk[bRk]])
                        cm = lambda off: cst[0:64, off:off + 64]
                        if d == 0:
                            m16, m16T, m32, m32T, m64 = cm(C_S16L), cm(C_S16U), cm(C_O32L), cm(C_O32U), cm(C_O64L)
                        else:
                            m16, m16T, m32, m32T, m64 = cm(C_S16U), cm(C_S16L), cm(C_O32U), cm(C_O32L), cm(C_O64U)
                        pAv = v3(ps[bA][0:64, :], 8)
                        pATv = v3(ps[bAT][0:64, :], 8)
                        tt(Nn[0][:, :, :], pAv, bc8(m16), ALU.mult, r=[psk[bA], 'cst'], w=['rNn0'])
                        tt(NTn[0][:, :, :], pATv, bc8(m16T), ALU.mult, r=[psk[bAT], 'cst'], w=['rNTn0'])
                        tt(O32a[:, :, :], pAv, bc8(m32), ALU.mult, r=[psk[bA], 'cst'], w=['O32a'])
                        tt(O32Ta[:, :, :], pATv, bc8(m32T), ALU.mult, r=[psk[bAT], 'cst'], w=['O32Ta'])
                        tt(O64a[:, :, :], pAv, bc8(m64), ALU.mult, r=[psk[bA], 'cst'], w=['O64a'])
                        tt(TT[:, :, :], NTn[0][:, :, :], bc8(I64), ALU.add, r=['rNTn0', 'cst'], w=['rTT'])
                        tt(Tn[:, :, :], Nn[0][:, :, :], bc8(I64), ALU.add, r=['rNn0', 'cst'], w=['rTn'])
                        tt(BTm[:, :, :], v3(ps[bBT][0:64, :], 8), bc8(mST), ALU.mult, r=[psk[bBT], 'cst'], w=['BTm'])
                        tt(RaT[:, :, :], v3(ps[bRa][0:64, :], 8), bc8(mIT), ALU.mult, r=[psk[bRa], 'cst'], w=['RaT'])
                        tt(RkT[:, :, :], v3(ps[bRk][0:64, :], 8), bc8(mIT), ALU.mult, r=[psk[bRk], 'cst'], w=['RkT'])
                        if RP < 3:
                            continue

                        def mm8(bank, L_, R_, rk):
                            for h in range(8):
                                mm(ps[bank][0:64, h * 64:(h + 1) * 64], L_[:, h, :], R_[:, h, :], r=rk, w=[psk[bank]])
                        for lev in range(1, 4):
                            a_, bp_ = lev % 2, (lev - 1) % 2
                            Np, NTp, Nc, NTc = Nn[bp_], NTn[bp_], Nn[a_], NTn[a_]
                            kp = ['rNn%d' % bp_, 'rNTn%d' % bp_]
                            bn = nextps()
                            mm8(bn, NTp, Np, kp)
                            cp(Nc[:, :, :], v3(ps[bn][0:64, :], 8), r=[psk[bn]], w=['rNn%d' % a_], eng='act')
                            bn2 = nextps()
                            mm8(bn2, Np, NTp, kp)
                            cp(NTc[:, :, :], v3(ps[bn2][0:64, :], 8), r=[psk[bn2]], w=['rNTn%d' % a_], eng='act')
                            b1_ = nextps()
                            mm8(b1_, Nc, TT, ['rNn%d' % a_, 'rTT'])
                            b2_ = nextps()
                            mm8(b2_, NTc, Tn, ['rNTn%d' % a_, 'rTn'])
                            tt(TT[:, :, :], TT[:, :, :], v3(ps[b1_][0:64, :], 8), ALU.add, r=['rTT', psk[b1_]], w=['rTT'])
                            tt(Tn[:, :, :], Tn[:, :, :], v3(ps[b2_][0:64, :], 8), ALU.add, r=['rTn', psk[b2_]], w=['rTn'])
                        Xs, X2s = Nn[0], NTn[0]
                        bx = nextps()
                        mm8(bx, O32a, TT, ['O32a', 'rTT'])
                        cp(Xs[:, :, :], v3(ps[bx][0:64, :], 8), r=[psk[bx]], w=['rNn0'], eng='act')
                        bx2 = nextps()
                        mm8(bx2, O32Ta, Tn, ['O32Ta', 'rTn'])
                        cp(X2s[:, :, :], v3(ps[bx2][0:64, :], 8), r=[psk[bx2]], w=['rNTn0'], eng='act')
                        by1 = nextps()
                        mm8(by1, Tn, Xs, ['rTn', 'rNn0'])
                        by2 = nextps()
                        mm8(by2, TT, X2s, ['rTT', 'rNTn0'])
                        tt(TT[:, :, :], TT[:, :, :], v3(ps[by1][0:64, :], 8), ALU.add, r=['rTT', psk[by1]], w=['rTT'])
                        tt(Tn[:, :, :], Tn[:, :, :], v3(ps[by2][0:64, :], 8), ALU.add, r=['rTn', psk[by2]], w=['rTn'])
                        bx = nextps()
                        mm8(bx, O64a, TT, ['O64a', 'rTT'])
                        cp(Xs[:, :, :], v3(ps[bx][0:64, :], 8), r=[psk[bx]], w=['rNn0'], eng='act')
                        by1 = nextps()
                        mm8(by1, Tn, Xs, ['rTn', 'rNn0'])
                        tt(TT[:, :, :], TT[:, :, :], v3(ps[by1][0:64, :], 8), ALU.add, r=['rTT', psk[by1]], w=['rTT'])
                        cp(TTb[:, :, :], TT[:, :, :], r=['rTT'], w=['TTb'], eng='act')
                        if RP < 4:
                            continue
                        for (src, skey, dst, dkey) in ((bt, 'bt', btok, 'btok'), (aG, 'aG', aGtok, 'aGtok'),
                                                       (kG, 'kG', kGtok, 'kGtok'), (None, None, vtok, 'rvtok')):
                            bb = nextps()
                            pq = ps[bb][:, :].bitcast(BF16)
                            for cq in range(4):
                                if src is None:
                                    tr(pq[0:64, cq * 128:(cq + 1) * 128], v_b[:, cq, csl], ident_bf[:, :],
                                       r=[('v_b', cq), 'ident_bf'], w=[psk[bb]])
                                else:
                                    tr(pq[0:64, cq * 128:(cq + 1) * 128], src[:, cq, :], ident_bf[:, :],
                                       r=[skey, 'ident_bf'], w=[psk[bb]])
                            cp(dst[:, :, :], v3(pq[0:64, 0:512], 4), r=[psk[bb]], w=[dkey], eng='act')
                        if RP < 5:
                            continue
                        bb = nextps()
                        for h in range(8):
                            cq, hp = h // 2, h % 2
                            mm(ps[bb][0:64, h * 64:(h + 1) * 64], BTm[:, h, :], vtok[:, cq, hp * 64:(hp + 1) * 64],
                               r=['BTm', 'rvtok'], w=[psk[bb]])
                        cp(BVb[:, :, :], v3(ps[bb][0:64, :], 8), r=[psk[bb]], w=['BVb'], eng='act')
                        bb = nextps()
                        bw_ = nextps()
                        for h in range(8):
                            mm(ps[bb][0:64, h * 64:(h + 1) * 64], TTb[:, h, :], BVb[:, h, :], r=['TTb', 'BVb'], w=[psk[bb]])
                        for h in HORD:
                            cq, hp = h // 2, h % 2
                            mm(ps[bw_][hp * 64:(hp + 1) * 64, cq * 64:(cq + 1) * 64], btok[:, cq, hp * 64:(hp + 1) * 64], TTb[:, h, :],
                               r=['btok', 'TTb'], w=[psk[bw_]])
                        cp(U0[:, :, :], v3(ps[bb][0:64, :], 8), r=[psk[bb]], w=['U0'], eng='act')
                        cp(WmTb[:, :, :], v3(ps[bw_][:, 0:256], 4), r=[psk[bw_]], w=['WmTb'], eng='act')
                        if RP < 6:
                            continue
                        bu_ = nextps()
                        for h in HORD:
                            cq, hp = h // 2, h % 2
                            pb_ = slice(hp * 64, (hp + 1) * 64)
                            if hp == 1 and os.environ.get('HP0'):
                                continue
                            mm(ps[bu_][0:64, h * 64:(h + 1) * 64], WmTb[pb_, cq, :], Pbf[pb_, cq, :], r=['WmTb', 'Pbf'], w=[psk[bu_]])
                        tt(Ub[:, :, :], U0[:, :, :], v3(ps[bu_][0:64, :], 8), ALU.add, r=['U0', psk[bu_]], w=['Ub'])
                        RQ = int(os.environ.get('RQ', '9'))
                        if RQ < 2:
                            continue
                        by_ = nextps()
                        bp2 = nextps()
                        for h in HORD:
                            cq, hp = h // 2, h % 2
                            pb_ = slice(hp * 64, (hp + 1) * 64)
                            yo_ = ps[by_][pb_, cq * 64:(cq + 1) * 64]
                            mm(yo_, Pbf[pb_, cq, :], rt[pb_, cq, :], start=True, stop=False, r=['Pbf', 'rt'], w=[psk[by_]])
                            if RQ >= 3:
                                mm(yo_, Ub[:, h, :], RaT[:, h, :], start=False, stop=False, r=['Ub', 'RaT'], w=[psk[by_]])
                                mm(yo_, vtok[:, cq, pb_], RkT[:, h, :], start=False, stop=True, r=['rvtok', 'RkT'], w=[psk[by_]])
                            po_ = ps[bp2][pb_, cq * 64:(cq + 1) * 64]
                            mm(po_, aGtok[:, cq, pb_], Ub[:, h, :], start=True, stop=False, r=['aGtok', 'Ub'], w=[psk[bp2]])
                            mm(po_, kGtok[:, cq, pb_], vtok[:, cq, pb_], start=False, stop=True, r=['kGtok', 'rvtok'], w=[psk[bp2]])
                        py = v3(ps[by_][:, 0:256], 4)
                        if d == 0:
                            cp(yacc[:, :, csl], py, r=[psk[by_]], w=[('yacc', c)], eng='act')
                        else:
                            tt(yacc[:, :, csl], yacc[:, :, csl], py, ALU.add, r=[psk[by_], ('yacc', c)], w=[('yacc', c)])
                        for cq in range(4):
                            stt(Pf[:, cq, :], Pf[:, cq, :], eTot[:, cq:cq + 1], ps[bp2][:, cq * 64:(cq + 1) * 64],
                                ALU.mult, ALU.add, r=['Pf', 'eTot', psk[bp2]], w=['Pf'])
                        seg_end = (c % 4 == 3) if d == 0 else (c % 4 == 0)
                        if seg_end:
                            dma('sp', rso_d[d, c // 4], Pf[:, :, :], r=['Pf'])
                            ts(Pf[:, :, :], Pf[:, :, :], flag[:, 0:1], ALU.mult, r=['Pf', 'flag'], w=['Pf'])
                        cp(Pbf[:, :, :], Pf[:, :, :], r=['Pf'], w=['Pbf'], eng='act')
                yk = [('yacc', c) for c in range(16)]
                for cq in range(4):
                    ysrc = yacc[:, cq, :]
                    for t2_ in range(2):
                        tsl = slice(t2_ * 512, (t2_ + 1) * 512)
                        bb = nextps()
                        cp(lw[:, 0, tsl].bitcast(BF16)[:, 0:512], ysrc[:, tsl], r=yk, w=['lw'], eng='act')
                        mm(ps[bb][:, :], bones[:, :], lw[:, 0, tsl].bitcast(BF16)[:, 0:512], r=['bones', 'lw'], w=[psk[bb]])
                        stt(lw[:, 1, tsl], ps[bb][:, :], -1.0 / 64, ysrc[:, tsl], ALU.mult, ALU.add, r=[psk[bb]] + yk, w=['lw'])
                        act(lw[:, 0, tsl].bitcast(BF16)[:, 0:512], lw[:, 1, tsl], AF.Square, r=['lw'], w=['lw'])
                        bb2 = nextps()
                        mm(ps[bb2][:, :], bones[:, :], lw[:, 0, tsl].bitcast(BF16)[:, 0:512], r=['bones', 'lw'], w=[psk[bb2]])
                        act(lw[:, 2, tsl], ps[bb2][:, :], AF.Sqrt, r=[psk[bb2], 'epsb'], w=['lw'], bias=epsb[:, 2:3], scale=1.0 / 64)
                        rcp(lw[:, 2, tsl], lw[:, 2, tsl], r=['lw'], w=['lw'])
                        stt(lw[:, 1, tsl], lw[:, 1, tsl], ppc('ln_w')[:, cq:cq + 1], lw[:, 2, tsl], ALU.mult, ALU.mult,
                            r=['lw', 'lw', 'pp'], w=['lw'])
                        stt(lw[:, 1, tsl], lw[:, 1, tsl], ppc('ln_b')[:, cq:cq + 1], bon[:, cq, tsl], ALU.add, ALU.add,
                            r=['lw', 'pp', ('bon', cq)], w=['lw'])
                        tt(mix1T[:, 4 + cq, tsl], lw[:, 1, tsl], gate[:, cq, tsl], ALU.mult, r=['lw', ('gate', cq)],
                           w=[('mix1T', 4 + cq)])
                pe_mode('end')
                state['drain'] = False
                S.flush()

        if STAGE >= 1:
            with contextlib.ExitStack() as sc:
                alloc_work(sc, 'p1')
                layer0_mixer(sc)
                S.flush()
        if STAGE >= 4:
            with contextlib.ExitStack() as sc:
                alloc_work(sc, 'p2', nwb=3)
                uT = sc.enter_context(nc.sbuf_tensor("uT", [128, 32, NT], BF16))
                mlp(0, uT)
                S.flush()
        if STAGE >= 6:
            layer1_mixer()
        with contextlib.ExitStack() as sc:
            alloc_work(sc, 'p7', nwb=3)
            if STAGE >= 5:
                uT = sc.enter_context(nc.sbuf_tensor("uT2", [128, 32, NT], BF16))
                mlp(1, uT)
            yo = sc.enter_context(nc.sbuf_tensor("yo", [128, 2, 512], F32))
            norm_mod(ppc('norm_final'), None, yo, 'yo', final=True)
            S.flush()
    return nc


def make_l1_consts():
    rm = np.zeros((128, 128), np.float32)
    for hb in range(2):
        for d in range(64):
            if (d % 32) < 16:
                rm[hb * 64 + d + 16, hb * 64 + d] = -1.0
            else:
                rm[hb * 64 + d - 16, hb * 64 + d] = 1.0
    bones = np.zeros((128, 128), np.float32)
    bones[:64, :64] = 1.0
    bones[64:, 64:] = 1.0
    t = np.arange(NT)
    row = (t // 64).astype(np.float32)
    col = (t % 64).astype(np.float32)
    inv = (10000.0 ** (-np.arange(0, 32, 2, dtype=np.float32) / 32)).astype(np.float32)
    cs = np.zeros((128, NT), np.float32)
    sn = np.zeros((128, NT), np.float32)
    for p in range(128):
        d = p % 64
        pos = row if d < 32 else col
        ang = (pos * inv[(d % 32) % 16]).astype(np.float32)
        cs[p] = np.cos(ang)
        sn[p] = np.sin(ang)
    NEG = -30000.0
    mk = np.full((2, 8, 128, 896), NEG, np.float32)
    for n in range(8):
        tq = n * 128 + np.arange(128)[:, None]
        tk = (n - 1) * 128 + np.arange(384)[None, :]
        inr = (tk >= 0) & (tk < NT)
        mk[0, n, :, :384] = np.where(inr & ((tk // 256) == (tq // 256)), 0.0, NEG)
        mk[1, n, :, :384] = np.where(inr & (np.abs(tk - tq) <= 128), 0.0, NEG)
        mk[1, n, :, 384:] = 0.0
    return rm, bones, cs, sn, mk


def kernel(**inp):
    inp = {k: np.asarray(v) for k, v in inp.items()}
    nc = build_nc()
    cst = make_consts()
    pp = make_pp(inp)
    xp = inp['x_prompt'].astype(np.float32)
    xs = inp['x_sample'].astype(np.float32)
    shared = {
        'pp': pp, 'cst': cst,
        'mod_w': np.ascontiguousarray(inp['mod_w'], np.float32),
        'mlp_w1': np.ascontiguousarray(inp['mlp_w1'], np.float32),
        'mlp_w2': np.ascontiguousarray(inp['mlp_w2'], np.float32),
        'ev_w_in': np.ascontiguousarray(inp['ev_w_in'][0], np.float32),
        'ev_w_out': np.ascontiguousarray(inp['ev_w_out'][0], np.float32),
    }
    rm, bones, rcs, rsn, mk = make_l1_consts()
    wi = np.asarray(inp['od_w_in'][0], np.float32)
    shared['od_w_in_x'] = np.ascontiguousarray(np.concatenate(
        [wi[:, 0:512], wi[:, 512:576], wi[:, 512:576], wi[:, 576:640], wi[:, 576:640], wi[:, 640:768], wi[:, 768:]], 1))
    shared['od_w_out'] = np.ascontiguousarray(inp['od_w_out'][0], np.float32)
    shared['w_up'] = np.ascontiguousarray(np.asarray(inp['rwkv_w_up'][0], np.float32).reshape(128, 512))
    shared['a_up'] = np.ascontiguousarray(np.asarray(inp['rwkv_a_up'][0], np.float32).reshape(128, 512))
    shared['g_up'] = np.ascontiguousarray(inp['rwkv_g_up'][0], np.float32)
    shared['bones'] = bones
    shared['rm'] = rm

    def st_in(sv):
        a = np.asarray(sv, np.float32).transpose(0, 2, 1).reshape(4, 2, 64, 64)
        return np.ascontiguousarray(a.transpose(1, 2, 0, 3).reshape(128, 4, 64))

    def st_out(a):
        b = a.reshape(2, 64, 4, 64).transpose(2, 0, 1, 3).reshape(8, 64, 64)
        return b.transpose(0, 2, 1)
    in_maps = []
    for core in range(8):
        m = dict(shared)
        if core < 4:
            xt = xp[core * 4:(core + 1) * 4].reshape(NT, 1024)
            m['cv'] = fm(inp['c_ctx'], 8)
            m['flag'] = np.zeros((128, 1), np.float32)
            m['gs0'] = np.zeros((2, 4, 128, 128), np.float32)
            m['ropec'] = np.ones((128, NT), np.float32)
            m['ropes'] = np.zeros((128, NT), np.float32)
            m['maskb'] = mk[0]
            m['ckT'] = np.zeros((2, 128, 512), np.float32)
            m['cvt'] = np.zeros((128, 4, 128), np.float32)
            m['rs0'] = np.zeros((2, 128, 4, 64), np.float32)
        else:
            b = core - 4
            xt = xs[b]
            m['cv'] = fm(inp['c'][b], 8)
            m['flag'] = np.ones((128, 1), np.float32)
            m['gs0'] = np.ascontiguousarray(
                np.stack([inp['state_gdn_fwd'][b, 0], inp['state_gdn_bwd'][b, 0]]), np.float32)
            m['ropec'] = rcs
            m['ropes'] = rsn
            m['maskb'] = mk[1]
            ck = np.asarray(inp['cache_attn_k'][b, 0], np.float32)
            m['ckT'] = np.ascontiguousarray(np.stack([np.concatenate([ck[k].T, ck[k].T], 0) for k in range(2)]))
            cvv = np.asarray(inp['cache_attn_v'][b, 0], np.float32)
            m['cvt'] = np.ascontiguousarray(cvv.transpose(1, 0, 2).reshape(4, 128, 128).transpose(1, 0, 2))
            m['rs0'] = np.stack([st_in(inp['state_rwkv_fwd'][b, 0]), st_in(inp['state_rwkv_bwd'][b, 0])])
        m['xT'] = np.ascontiguousarray(xt.T)
        in_maps.append(m)
    res = run_bass_kernel_spmd(nc, in_maps, core_ids=list(range(8)))
    R = res.results
    y_prompt = np.stack([R[c]['yT'].T.reshape(4, 256, 1024) for c in range(4)]).reshape(16, 256, 1024)
    y_sample = np.stack([R[4 + b]['yT'].T for b in range(4)])
    gf = np.stack([R[c]['gso'][0] for c in range(4)]).reshape(16, 1, 4, 128, 128)
    gb = np.stack([R[c]['gso'][1] for c in range(4)]).reshape(16, 1, 4, 128, 128)
    def kvout(i):
        o = np.stack([R[c]['kvo'][i] for c in range(4)])
        o = o.reshape(4, 2, 64, 4, 256).transpose(0, 3, 1, 4, 2)
        return np.ascontiguousarray(o.reshape(16, 1, 2, 256, 64)).astype(np.float32)

    def rsout(d):
        o = np.stack([np.stack([st_out(R[c]['rso'][d, sg]) for sg in range(4)]) for c in range(4)])
        return np.ascontiguousarray(o.reshape(16, 1, 8, 64, 64)).astype(np.float32)
    return (y_prompt.astype(np.float32), y_sample.astype(np.float32), gf.astype(np.float32), gb.astype(np.float32),
            kvout(0), kvout(1), rsout(0), rsout(1))
```

```python
import contextlib
import os
import numpy as np
import concourse.bass as bass
import concourse.mybir as mybir
from concourse.bass_utils import run_bass_kernel_spmd

F32 = mybir.dt.float32
BF16 = mybir.dt.bfloat16
AF = mybir.ActivationFunctionType
ALU = mybir.AluOpType

ENGS = ['pe', 'act', 'dve', 'pool', 'sp']
DMAQ = ('sp', 'pool')
KRING = 6
NT = 1024
EPS = 1e-6


class Sched:
    def __init__(self, nc):
        self.nc = nc
        self.prog = {e: [] for e in ENGS}
        self.cnt = {e: 0 for e in ENGS}
        self.known = {e: {} for e in ENGS}
        self.res = {}
        self.dma_i = {q: 0 for q in DMAQ}
        self.sems = {}
        self.eng_obj = {'pe': nc.tensor, 'act': nc.scalar, 'dve': nc.vector,
                        'pool': nc.gpsimd, 'sp': nc.sync}

    def alloc_sems(self, stack):
        for e in ['pe', 'act', 'dve', 'pool']:
            self.sems[e] = stack.enter_context(self.nc.semaphore('c_' + e))
        for q in DMAQ:
            for j in range(KRING):
                self.sems[(q, j)] = stack.enter_context(self.nc.semaphore('d_%s%d' % (q, j)))

    def _r(self, k):
        if k not in self.res:
            self.res[k] = {'w': None, 'r': {}}
        return self.res[k]

    def _need(self, eng, waits, dep, own):
        if dep is None:
            return
        sk, val = dep
        if sk == own:
            return
        if self.known[eng].get(sk, 0) >= val:
            return
        if waits.get(sk, 0) < val:
            waits[sk] = val

    def begin(self):
        self.buf = []

    def end(self):
        b = self.buf
        self.buf = None
        return b

    def merge(self, chains):
        its = [list(c) for c in chains]
        pos = [0] * len(its)
        while any(pos[i] < len(its[i]) for i in range(len(its))):
            for i in range(len(its)):
                if pos[i] < len(its[i]):
                    kind, e, fn, r, w = its[i][pos[i]]
                    pos[i] += 1
                    if kind == 'op':
                        self.op(e, fn, r, w)
                    else:
                        self.dma(e, fn, r, w)

    def op(self, eng, fn, reads=(), writes=()):
        if getattr(self, 'buf', None) is not None:
            self.buf.append(('op', eng, fn, tuple(reads), tuple(writes)))
            return
        own = eng
        skip = eng if eng == 'pe' else None
        waits = {}
        for k in reads:
            self._need(eng, waits, self._r(k)['w'], skip)
        for k in writes:
            r = self._r(k)
            self._need(eng, waits, r['w'], skip)
            for sk, v in r['r'].items():
                self._need(eng, waits, (sk, v), skip)
        self.cnt[eng] += 1
        v = self.cnt[eng]
        for sk, val in waits.items():
            self.known[eng][sk] = val
        self.prog[eng].append((fn, list(waits.items()), (own, 1)))
        for k in reads:
            r = self._r(k)
            if r['r'].get(own, 0) < v:
                r['r'][own] = v
        for k in writes:
            r = self._r(k)
            r['w'] = (own, v)
            r['r'] = {}

    def dma(self, q, fn, reads=(), writes=()):
        if getattr(self, 'buf', None) is not None:
            self.buf.append(('dma', q, fn, tuple(reads), tuple(writes)))
            return
        i = self.dma_i[q]
        self.dma_i[q] += 1
        own = (q, i % KRING)
        val = 16 * (i // KRING + 1)
        waits = {}
        if i >= KRING:
            self._need(q, waits, (own, val - 16), None)
        for k in reads:
            self._need(q, waits, self._r(k)['w'], None)
        for k in writes:
            r = self._r(k)
            self._need(q, waits, r['w'], None)
            for sk, v in r['r'].items():
                self._need(q, waits, (sk, v), None)
        for sk, v in waits.items():
            self.known[q][sk] = v
        self.prog[q].append((fn, list(waits.items()), (own, 16)))
        for k in reads:
            r = self._r(k)
            if r['r'].get(own, 0) < val:
                r['r'][own] = val
        for k in writes:
            r = self._r(k)
            r['w'] = (own, val)
            r['r'] = {}

    def _all_done(self):
        waits = []
        for q in DMAQ:
            n = self.dma_i[q]
            for j in range(KRING):
                cntj = len(range(j, n, KRING))
                if cntj:
                    waits.append(((q, j), 16 * cntj))
        for e in ['pe', 'act', 'dve', 'pool']:
            if self.cnt[e]:
                waits.append((e, self.cnt[e]))
        return waits

    def barrier(self):
        waits = self._all_done()
        for e in ENGS:
            w2 = [(sk, v) for sk, v in waits if sk != e and self.known[e].get(sk, 0) < v]
            for sk, v in w2:
                self.known[e][sk] = v
            self.prog[e].append((None, w2, None))
        self.res = {}

    def finish(self, eng='sp'):
        self.prog[eng].append((None, self._all_done(), None))

    def emit(self, block):
        S = self

        def replay(e):
            def body(_eng):
                eo = S.eng_obj[e]
                for fn, waits, inc in S.prog[e]:
                    for sk, v in waits:
                        eo.wait_ge(S.sems[sk], v)
                    if fn is not None:
                        ins = fn(eo)
                        ins.then_inc(S.sems[inc[0]], inc[1])
            return body
        block.tensor(replay('pe'))
        block.scalar(replay('act'))
        block.vector(replay('dve'))
        block.gpsimd(replay('pool'))
        block.sync(replay('sp'))

    def flush(self):
        self.barrier()
        with self.nc.Block() as block:
            self.emit(block)
        self.prog = {e: [] for e in ENGS}


def _pp_layout():
    ent = [('mod_b', 96), ('norm_mix', 16), ('norm_mlp', 16), ('norm_final', 8),
           ('gdn_conv', 36), ('sc_conv', 12), ('gdn_norm', 1), ('a_log', 1), ('dt_bias', 1),
           ('mu', 30), ('w0', 8), ('a0', 8), ('k_k', 4), ('k_a', 4), ('ln_w', 4), ('ln_b', 4), ('r_k', 4), ('sink', 8)]
    off = {}
    o = 0
    for n, w in ent:
        off[n] = (o, w)
        o += w
    return off, o


PP_OFF, PP_N = _pp_layout()
C_ID, C_U, C_L, C_SU, C_SL, C_N = 0, 128, 192, 256, 320, 768
C_S16L, C_O32L, C_O64L, C_S16U, C_O32U, C_O64U = 384, 448, 512, 576, 640, 704


def make_consts():
    c = np.zeros((128, C_N), np.float32)
    c[:, C_ID:C_ID + 128] = np.eye(128, dtype=np.float32)
    p = np.arange(64)[:, None]
    f = np.arange(64)[None, :]
    c[:64, C_U:C_U + 64] = (p <= f)
    c[:64, C_L:C_L + 64] = (p >= f)
    c[:64, C_SU:C_SU + 64] = (p < f)
    c[:64, C_SL:C_SL + 64] = (p > f)
    c[:64, C_S16L:C_S16L + 64] = (p > f) & (p // 16 == f // 16)
    c[:64, C_O32L:C_O32L + 64] = (p > f) & (p // 32 == f // 32) & (p // 16 != f // 16)
    c[:64, C_O64L:C_O64L + 64] = (p > f) & (p // 32 != f // 32)
    c[:64, C_S16U:C_S16U + 64] = (p < f) & (p // 16 == f // 16)
    c[:64, C_O32U:C_O32U + 64] = (p < f) & (p // 32 == f // 32) & (p // 16 != f // 16)
    c[:64, C_O64U:C_O64U + 64] = (p < f) & (p // 32 != f // 32)
    return c


def fm(v, nch):
    return np.ascontiguousarray(np.asarray(v, np.float32).reshape(nch, 128).T)


def make_pp(inp):
    pp = np.zeros((128, PP_N), np.float32)

    def put(name, arr):
        o, w = PP_OFF[name]
        assert arr.shape == (128, w), (name, arr.shape, w)
        pp[:, o:o + w] = arr
    put('mod_b', np.concatenate([fm(inp['mod_b'][l], 48) for l in range(2)], 1))
    put('norm_mix', np.concatenate([fm(inp['norm_mix'][l], 8) for l in range(2)], 1))
    put('norm_mlp', np.concatenate([fm(inp['norm_mlp'][l], 8) for l in range(2)], 1))
    put('norm_final', fm(inp['norm_final'], 8))
    put('gdn_conv', np.concatenate([fm(inp['gdn_conv'][0][i], 12) for i in range(3)], 1))
    put('sc_conv', np.concatenate([fm(inp['sc_conv'][0][i], 4) for i in range(3)], 1))
    put('gdn_norm', np.asarray(inp['gdn_norm'][0], np.float32).reshape(128, 1))
    a = np.zeros((128, 1), np.float32)
    a[:8, 0] = np.asarray(inp['gdn_a_log'][0], np.float32).reshape(8)
    put('a_log', a)
    a = np.zeros((128, 1), np.float32)
    a[:8, 0] = np.asarray(inp['gdn_dt_bias'][0], np.float32).reshape(8)
    put('dt_bias', a)
    put('mu', np.concatenate([fm(inp['rwkv_mu'][0][d], 15) for d in range(2)], 1))
    put('w0', np.concatenate([fm(inp['rwkv_w0'][0][d], 4) for d in range(2)], 1))
    put('a0', np.concatenate([fm(inp['rwkv_a0'][0][d], 4) for d in range(2)], 1))
    put('k_k', fm(inp['rwkv_k_k'][0], 4))
    put('k_a', fm(inp['rwkv_k_a'][0], 4))
    put('ln_w', fm(inp['rwkv_ln_w'][0], 4))
    put('ln_b', fm(inp['rwkv_ln_b'][0], 4))
    put('r_k', fm(np.asarray(inp['rwkv_r_k'][0]).reshape(512), 4))
    put('sink', np.tile(np.asarray(inp['attn_sink'][0], np.float32).reshape(1, 8), (128, 1)))
    return pp


STAGE = int(os.environ.get('STAGE', '9'))
L1PART = int(os.environ.get('L1PART', '9'))
SUB = 9


def build_nc(do_l1=False):
    nc = bass.Bass("TRN2", target_bir_lowering=False)

    def din(name, shape):
        return nc.dram_tensor(name, list(shape), F32, kind="ExternalInput").ap()

    def dout(name, shape):
        return nc.dram_tensor(name, list(shape), F32, kind="ExternalOutput").ap()

    xT_d = din("xT", [1024, NT])
    cv_d = din("cv", [128, 8])
    flag_d = din("flag", [128, 1])
    s0_d = din("gs0", [2, 4, 128, 128])
    pp_d = din("pp", [128, PP_N])
    cst_d = din("cst", [128, C_N])
    modw_d = din("mod_w", [2, 1024, 6144])
    w1_d = din("mlp_w1", [2, 1024, 4096])
    w2_d = din("mlp_w2", [2, 4096, 1024])
    evin_d = din("ev_w_in", [1024, 3600])
    evout_d = din("ev_w_out", [1024, 1024])
    odin_d = din("od_w_in_x", [1024, 2816])
    odout_d = din("od_w_out", [1024, 1024])
    wup_d = din("w_up", [128, 512])
    aup_d = din("a_up", [128, 512])
    gup_d = din("g_up", [128, 512])
    bones_d = din("bones", [128, 128])
    rm_d = din("rm", [128, 128])
    ropec_d = din("ropec", [128, NT])
    ropes_d = din("ropes", [128, NT])
    maskb_d = din("maskb", [8, 128, 896])
    ckT_d = din("ckT", [2, 128, 512])
    cvt_d = din("cvt", [128, 4, 128])
    rs0_d = din("rs0", [2, 128, 4, 64])
    kvo_d = dout("kvo", [2, 128, NT])
    rso_d = dout("rso", [2, 4, 128, 4, 64])
    yT_d = dout("yT", [1024, NT])
    gso_d = dout("gso", [2, 4, 4, 128, 128])

    with contextlib.ExitStack() as st:
        S = Sched(nc)
        S.alloc_sems(st)
        with nc.Block() as blk0:
            def _clr(_e):
                for sm in S.sems.values():
                    nc.sync.sem_clear(sm)
            blk0.sync(_clr)

        def tile(name, shape, dt=F32):
            return st.enter_context(nc.sbuf_tensor('sb_' + name, list(shape), dt))

        ps = [st.enter_context(nc.psum_tensor("ps%d" % i, [128, 512], F32)) for i in range(8)]
        psk = ["ps%d" % i for i in range(8)]
        state = {'ps': 0, 'wb': 0}

        def nextps():
            rng_ = state.get('psr')
            if rng_ is not None:
                k = 'ps_%d' % rng_[0]
                b = state.get(k, rng_[0])
                state[k] = rng_[0] + (b + 1 - rng_[0]) % (rng_[1] - rng_[0])
                return b
            b = state['ps']
            state['ps'] = (b + 1) % 8
            return b

        def pe_mode(mode):
            if not state.get('drain'):
                return
            if state.get('pemode') != mode:
                state['pemode'] = mode
                if S.cnt['pe'] > 0:
                    S.prog['pe'].append((None, [('pe', S.cnt['pe'])], None))

        def rnd(n):
            return 32 if n <= 32 else (64 if n <= 64 else 128)

        def mm(out, lhsT, rhs, start=True, stop=True, r=(), w=()):
            pe_mode(('mm', rnd(lhsT.shape[0]), rnd(lhsT.shape[-1]), lhsT.start_partition(), out.start_partition()))
            S.op('pe', lambda e: e.matmul(out, lhsT, rhs, start=start, stop=stop), r, w)

        def tr(out, in_, ident, r=(), w=()):
            pe_mode(('tr', rnd(in_.shape[0]), rnd(in_.shape[-1]), in_.start_partition(), out.start_partition()))
            S.op('pe', lambda e: e.transpose(out, in_, ident), r, w)

        def act(out, in_, func, r=(), w=(), bias=None, scale=None):
            kw = {}
            if bias is not None:
                kw['bias'] = bias
            if scale is not None:
                kw['scale'] = scale
            S.op('act', lambda e: e.activation(out, in_, func, **kw), r, w)

        def tt(out, a, b, op, r=(), w=(), eng='dve'):
            S.op(eng, lambda e: e.tensor_tensor(out, a, b, op), r, w)

        def ts(out, a, s1, op0, r=(), w=(), s2=None, op1=None, eng='dve'):
            if op1 is None:
                S.op(eng, lambda e: e.tensor_scalar(out, a, s1, None, op0), r, w)
            else:
                S.op(eng, lambda e: e.tensor_scalar(out, a, s1, s2, op0, op1), r, w)

        def stt(out, a, s, b, op0, op1, r=(), w=()):
            S.op('dve', lambda e: e.scalar_tensor_tensor(out, a, s, b, op0, op1), r, w)

        def cp(out, in_, r=(), w=(), eng='dve'):
            if eng == 'act':
                S.op('act', lambda e: e.copy(out, in_), r, w)
            else:
                S.op(eng, lambda e: e.tensor_scalar(out, in_, 1.0, None, ALU.mult), r, w)

        def rcp(out, in_, r=(), w=()):
            S.op('dve', lambda e: e.reciprocal(out, in_), r, w)

        def scan(out, d0, d1, r=(), w=()):
            S.op('dve', lambda e: e.tensor_tensor_scan(out, d0, d1, 0.0, ALU.mult, ALU.add), r, w)

        def rmax(out, in_, r=(), w=()):
            S.op('dve', lambda e: e.reduce_max(out, in_, mybir.AxisListType.X), r, w)

        def expacc(out, in_, bias, acc, r=(), w=()):
            S.op('act', lambda e: e.activation(out, in_, AF.Exp, bias=bias, accum_out=acc), r, w)

        def mset(ap, val, r=(), w=(), eng='dve'):
            S.op(eng, lambda e: e.memset(ap, val), r, w)

        def dma(q, out, in_, r=(), w=()):
            S.dma(q, lambda e: e.dma_start(out=out, in_=in_), r, w)

        x_sb = tile("x_sb", [128, 8, NT])
        hT = None
        wb = None
        cst = tile("cst", [128, C_N])
        pp = tile("pp", [128, PP_N])
        cv = tile("cv", [128, 8])
        cvs = tile("cvs", [128, 8], BF16)
        flag = tile("flag", [128, 1])
        ident_bf = tile("ident_bf", [128, 128], BF16)
        ones_bf = tile("ones_bf", [128, 128], BF16)
        ones_f = tile("ones_f", [128, 128])
        modv = tile("modv", [128, 2, 48])
        gsA = tile("gsA", [128, 2, 8])
        gsB = tile("gsB", [128, 2, 8])
        sqb = rstd = ntmp = None

        def alloc_work(sc, tag, with_h=True, nwb=3):
            nonlocal hT, wb, sqb, rstd, ntmp
            A = lambda n, s_, d=F32: sc.enter_context(nc.sbuf_tensor('sb_%s_%s' % (n, tag), list(s_), d))
            if with_h:
                hT = A("hT", [128, 8, NT], BF16)
            wb = [A("wb%d" % i, [128, 4096], BF16) for i in range(nwb)]
            state['nwb'] = nwb
            state['wb'] = 0
            sqb = A("sqb", [128, 2, 512], BF16)
            rstd = A("rstd", [128, 512])
            ntmp = [A("ntmp%d" % i, [128, 512]) for i in range(2)]

        sc0 = contextlib.ExitStack()
        alloc_work(sc0, 'p0', with_h=False)

        ident_f = cst[:, C_ID:C_ID + 128]

        def ppc(name, j0=0, n=None):
            o, wd = PP_OFF[name]
            if n is None:
                n = wd - j0
            return pp[:, o + j0:o + j0 + n]

        dma('sp', cst[:, :], cst_d[:, :], w=['cst'])
        dma('sp', pp[:, :], pp_d[:, :], w=['pp'])
        dma('sp', cv[:, :], cv_d[:, :], w=['cv'])
        dma('sp', flag[:, :], flag_d[:, :], w=['flag'])
        for fc in range(8):
            dma('sp', x_sb[:, fc, :], xT_d[fc * 128:(fc + 1) * 128, :], w=[('x', fc)])
        act(cvs[:, :], cv[:, :], AF.Silu, r=['cv'], w=['cvs'])
        cp(ident_bf[:, :], ident_f, r=['cst'], w=['ident_bf'])
        mset(ones_bf[:, :], 1.0, w=['ones_bf'])
        mset(ones_f[:, :], 1.0, w=['ones_f'])

        def load_piece(wd_ap, kcn, wdth):
            slot = state['wb']
            state['wb'] = (slot + 1) % state['nwb']
            view = wb[slot][:, 0:kcn * wdth].rearrange("p (k n) -> p k n", k=kcn)
            dma('pool', view, wd_ap.rearrange("(k p) n -> p k n", p=128), w=[('wb', slot)])
            return slot, view

        for l in range(2):
            bm = nextps()
            for oc in range(12):
                slot, wv = load_piece(modw_d[l, :, oc * 512:(oc + 1) * 512], 8, 512)
                for c4 in range(4):
                    ocn = oc * 4 + c4
                    for kc in range(8):
                        mm(ps[bm][:, ocn:ocn + 1], wv[:, kc, c4 * 128:(c4 + 1) * 128], cvs[:, kc:kc + 1],
                           start=(kc == 0), stop=(kc == 7), r=[('wb', slot), 'cvs'], w=[psk[bm]])
            tt(modv[:, l, :], ps[bm][:, 0:48], ppc('mod_b', l * 48, 48), ALU.add,
               r=[psk[bm], 'pp'], w=['modv'])
            stt(gsA[:, l, :], modv[:, l, 8:16], 1.0, ppc('norm_mix', l * 8, 8), ALU.add, ALU.mult,
                r=['modv', 'pp'], w=['gsA'])
            stt(gsB[:, l, :], modv[:, l, 32:40], 1.0, ppc('norm_mlp', l * 8, 8), ALU.add, ALU.mult,
                r=['modv', 'pp'], w=['gsB'])

        S.flush()
        sc0.close()

        def norm_mod(gs_ap, shift_ap, dst, dst_key, final=False):
            for th in range(2):
                tsl = slice(th * 512, (th + 1) * 512)
                b = nextps()
                for fc in range(8):
                    act(sqb[:, fc % 2, :], x_sb[:, fc, tsl], AF.Square, r=[('x', fc)], w=[('sqb', fc % 2)])
                    mm(ps[b][:, :], ones_bf[:, :], sqb[:, fc % 2, :], start=(fc == 0), stop=(fc == 7),
                       r=['ones_bf', ('sqb', fc % 2)], w=[psk[b]])
                act(rstd[:, :], ps[b][:, :], AF.Sqrt, r=[psk[b], 'epsb'], w=['rstd'], bias=epsb[:, 0:1], scale=1.0 / 1024)
                rcp(rstd[:, :], rstd[:, :], r=['rstd'], w=['rstd'])
                for fc in range(8):
                    k = fc % 2
                    tt(ntmp[k][:, :], x_sb[:, fc, tsl], rstd[:, :], ALU.mult,
                       r=[('x', fc), 'rstd'], w=[('ntmp', k)])
                    if final:
                        act(dst[:, k, :], ntmp[k][:, :], AF.Identity, r=[('ntmp', k), 'pp'],
                            w=[(dst_key, k)], scale=gs_ap[:, fc:fc + 1])
                        dma('sp', yT_d[fc * 128:(fc + 1) * 128, tsl], dst[:, k, :], r=[(dst_key, k)])
                    else:
                        act(dst[:, fc, tsl], ntmp[k][:, :], AF.Identity, r=[('ntmp', k), 'modv', 'gsA', 'gsB'],
                            w=[(dst_key, fc)], scale=gs_ap[:, fc:fc + 1], bias=shift_ap[:, fc:fc + 1])

        epsb = tile("epsb", [128, 4])
        mset(epsb[:, 0:1], EPS, w=['epsb'])
        mset(epsb[:, 1:2], 1e-6, w=['epsb'])
        mset(epsb[:, 2:3], 64e-5, w=['epsb'])

        def linear(wd, kcn, pieces, src, src_key, consumer):
            for (c0, wdth) in pieces:
                slot, wv = load_piece(wd[:, c0:c0 + wdth], kcn, wdth)
                for cs in range(0, wdth, 128):
                    cw = min(128, wdth - cs)
                    for th in range(2):
                        b = nextps()
                        for kc in range(kcn):
                            mm(ps[b][0:cw, :], wv[:, kc, cs:cs + cw], src[:, kc, th * 512:(th + 1) * 512],
                               start=(kc == 0), stop=(kc == kcn - 1),
                               r=[('wb', slot), (src_key, kc)], w=[psk[b]])
                        consumer(c0 + cs, cw, th, b)

        def mlp(l, uT):
            norm_mod(gsB[:, l, :], modv[:, l, 24:32], hT, 'hT')

            def c1(c0, cw, th, b):
                oc = c0 // 128
                act(ntmp[th][:, :], ps[b][:, :], AF.Relu, r=[psk[b]], w=[('ntmp', th)])
                tt(uT[:, oc, th * 512:(th + 1) * 512], ntmp[th][:, :], ntmp[th][:, :], ALU.mult,
                   r=[('ntmp', th)], w=[('uT', oc)])
            linear(w1_d[l], 8, [(i * 512, 512) for i in range(8)], hT, 'hT', c1)

            def c2(c0, cw, th, b):
                fc = c0 // 128
                tsl = slice(th * 512, (th + 1) * 512)
                stt(x_sb[:, fc, tsl], ps[b][:, :], modv[:, l, 40 + fc:41 + fc], x_sb[:, fc, tsl],
                    ALU.mult, ALU.add, r=[psk[b], 'modv', ('x', fc)], w=[('x', fc)])
            linear(w2_d[l], 32, [(i * 128, 128) for i in range(8)], uT, 'uT', c2)

        def layer0_mixer(sc):
            T = lambda n, s, d=F32: sc.enter_context(nc.sbuf_tensor('sb_' + n, list(s), d))
            qT = T("qT", [128, 4, NT], BF16)
            kT = T("kT", [128, 4, NT], BF16)
            vT = T("vT", [128, 4, NT], BF16)
            mixT = T("mixT", [128, 8, NT], BF16)
            oacc = T("oacc", [128, 4, NT])
            betaT = T("betaT", [8, NT])
            gT = T("gT", [8, NT])
            negA = T("negA", [8, 1])
            Sf2 = [T("Sf%d" % i, [128, 4, 128]) for i in range(2)]
            Sb2 = [T("Sb%d" % i, [128, 4, 128], BF16) for i in range(2)]
            sp_ = contextlib.ExitStack()
            Tp = lambda n, s_, d=F32: sp_.enter_context(nc.sbuf_tensor('sb_p_' + n, list(s_), d))
            pad = [Tp("pad%d" % i, [128, 4, 258]) for i in range(2)]
            cacc1 = Tp("cacc", [128, 4, 256])
            cacc = [cacc1, cacc1]
            qsq = Tp("qsq", [128, NT], BF16)
            rinv = Tp("rinv", [128, NT])
            sctmp = Tp("sctmp", [128, NT])

            for i in range(2):
                mset(pad[i][:, :, :], 0.0, w=[('pad', i)])
            act(negA[:, :], ppc('a_log')[0:8, :], AF.Exp, r=['pp'], w=['negA'])
            ts(negA[:, :], negA[:, :], -1.0, ALU.mult, r=['negA'], w=['negA'])

            norm_mod(gsA[:, 0, :], modv[:, 0, 0:8], hT, 'hT')

            cstate = {'i': 0}
            if SUB < 1:
                return

            def conv3(pi, wname, nch, ch):
                p = pad[pi]
                ts(p[:, 1:4, 0:1], p[:, 0:3, 256:257], flag[:, 0:1], ALU.mult,
                   r=[('pad', pi), 'flag'], w=[('pad', pi)])
                ts(p[:, 0:3, 257:258], p[:, 1:4, 1:2], flag[:, 0:1], ALU.mult,
                   r=[('pad', pi), 'flag'], w=[('pad', pi)])
                o, _ = PP_OFF[wname]
                w0 = pp[:, o + 0 * nch + ch:o + 0 * nch + ch + 1]
                w1 = pp[:, o + 1 * nch + ch:o + 1 * nch + ch + 1]
                w2 = pp[:, o + 2 * nch + ch:o + 2 * nch + ch + 1]
                ts(cacc[pi][:, :, :], p[:, :, 0:256], w0, ALU.mult, r=[('pad', pi), 'pp'], w=['cacc'])
                stt(cacc[pi][:, :, :], p[:, :, 1:257], w1, cacc[pi][:, :, :], ALU.mult, ALU.add,
                    r=[('pad', pi), 'pp', 'cacc'], w=['cacc'])
                stt(cacc[pi][:, :, :], p[:, :, 2:258], w2, cacc[pi][:, :, :], ALU.mult, ALU.add,
                    r=[('pad', pi), 'pp', 'cacc'], w=['cacc'])

            def pad_in(pi, th):
                return pad[pi][:, 2 * th:2 * th + 2, 1:257]

            def ps3(b):
                return ps[b][:, :].rearrange("p (s t) -> p s t", s=2)

            def c_qkv(c0, cw, th, b):
                ch = c0 // 128
                pi = ch % 2
                cp(pad_in(pi, th), ps3(b), r=[psk[b]], w=[('pad', pi)], eng='act')
                if th == 0:
                    return
                conv3(pi, 'gdn_conv', 12, ch)
                flat = cacc[pi][:, :, :].rearrange("p s t -> p (s t)")
                h = ch % 4
                if ch >= 8:
                    act(vT[:, h, :], flat, AF.Silu, r=['cacc'], w=[('vT', h)])
                    return
                act(sctmp[:, :], flat, AF.Silu, r=['cacc'], w=['sctmp'])
                act(qsq[:, :], sctmp[:, :], AF.Square, r=['sctmp'], w=['qsq'])
                for t2 in range(2):
                    bb = nextps()
                    mm(ps[bb][:, :], ones_bf[:, :], qsq[:, t2 * 512:(t2 + 1) * 512], r=['ones_bf', 'qsq'], w=[psk[bb]])
                    act(rinv[:, t2 * 512:(t2 + 1) * 512], ps[bb][:, :], AF.Sqrt, r=[psk[bb], 'epsb'],
                        w=['rinv'], bias=epsb[:, 1:2])
                rcp(rinv[:, :], rinv[:, :], r=['rinv'], w=['rinv'])
                dst, key = (qT, 'qT') if ch < 4 else (kT, 'kT')
                scl = (128.0 ** -0.5) if ch < 4 else 1.0
                stt(dst[:, h, :], sctmp[:, :], scl, rinv[:, :], ALU.mult, ALU.mult,
                    r=['sctmp', 'rinv'], w=[(key, h)])
            linear(evin_d, 8, [(i * 512, 512) for i in range(3)], hT, 'hT', c_qkv)

            if SUB < 2:
                return
            def c_z(c0, cw, th, b):
                h = (c0 - 1536) // 128
                act(mixT[:, h, th * 512:(th + 1) * 512], ps[b][:, :], AF.Silu, r=[psk[b]], w=[('mixT', h)])
            linear(evin_d, 8, [(1536, 512)], hT, 'hT', c_z)

            if SUB < 3:
                return
            def c_beta(c0, cw, th, b):
                act(betaT[:, th * 512:(th + 1) * 512], ps[b][0:8, :], AF.Sigmoid, r=[psk[b]], w=['betaT'])

            def c_a(c0, cw, th, b):
                tsl = slice(th * 512, (th + 1) * 512)
                act(rinv[0:8, tsl], ps[b][0:8, :], AF.Exp, r=[psk[b], 'pp'], w=['rinv'], bias=ppc('dt_bias')[0:8, :])
                act(rinv[0:8, tsl], rinv[0:8, tsl], AF.Ln, r=['rinv'], w=['rinv'], bias=1.0)
                ts(gT[:, tsl], rinv[0:8, tsl], negA[:, 0:1], ALU.mult, r=['rinv', 'negA'], w=['gT'])
            linear(evin_d, 8, [(2048, 8)], hT, 'hT', c_beta)
            linear(evin_d, 8, [(2056, 8)], hT, 'hT', c_a)

            if SUB < 4:
                return
            def mk_sc(j):
                def c_c(c0, cw, th, b):
                    cp(sctmp[:, th * 512:(th + 1) * 512], ps[b][:, :], r=[psk[b]], w=['sctmp'], eng='act')

                def c_h(c0, cw, th, b):
                    pi = j % 2
                    tt(pad_in(pi, th), ps3(b), sctmp[:, th * 512:(th + 1) * 512].rearrange("p (s t) -> p s t", s=2),
                       ALU.mult, r=[psk[b], 'sctmp'], w=[('pad', pi)])
                    if th == 1:
                        conv3(pi, 'sc_conv', 4, j)

                def c_b(c0, cw, th, b):
                    pi = j % 2
                    tt(mixT[:, 4 + j, th * 512:(th + 1) * 512].rearrange("p (s t) -> p s t", s=2), ps3(b),
                       cacc[pi][:, 2 * th:2 * th + 2, :], ALU.mult, r=[psk[b], 'cacc'], w=[('mixT', 4 + j)])
                return c_c, c_h, c_b
            import os
            SCJ = int(os.environ.get('SCJ', '4'))
            SCP = int(os.environ.get('SCP', '3'))
            for j in range(SCJ):
                c_c, c_h, c_b = mk_sc(j)
                linear(evin_d, 8, [(2064 + 512 + j * 128, 128)], hT, 'hT', c_c)
                if SCP >= 2:
                    linear(evin_d, 8, [(2064 + 1024 + j * 128, 128)], hT, 'hT', c_h)
                if SCP >= 3:
                    linear(evin_d, 8, [(2064 + j * 128, 128)], hT, 'hT', c_b)

            S.flush()
            sp_.close()
            if STAGE < 2:
                return
            sg_ = contextlib.ExitStack()
            Tg = lambda n, s_, d=F32: sg_.enter_context(nc.sbuf_tensor('sb_g_' + n, list(s_), d))

            def T2(n, s, d=F32):
                return [Tg(n + "_0", s, d), Tg(n + "_1", s, d)]
            for c_ in range(16):
                mset(oacc[:, :, c_ * 64:(c_ + 1) * 64], 0.0, w=[('oacc', c_)])
            chains = []
            gbtok = T2("gbtok", [64, 16])
            gcc = T2("gcc", [64, 4])
            Dgb = T2("Dgb", [64, 2, 4, 64])
            Dm = T2("Dm", [64, 4, 64])
            t1 = T2("t1", [64, 4, 64])
            t2_ = T2("t2", [64, 4, 64])
            E1 = T2("E1", [64, 4, 64])
            E2 = T2("E2", [64, 4, 64])
            decIT = T2("decIT", [64, 4, 64])
            Nn = [T2("Nn%d" % k, [64, 4, 64]) for k in range(2)]
            NTn = [T2("NTn%d" % k, [64, 4, 64]) for k in range(2)]
            TT = T2("TT", [64, 4, 64])
            TTb = T2("TTb", [64, 4, 64], BF16)
            intraT = T2("intraT", [64, 4, 64], BF16)
            ktok = T2("ktok", [64, 4, 128], BF16)
            vtok = T2("vtok", [64, 4, 128], BF16)
            vb = T2("vb", [64, 4, 128], BF16)
            kbg = T2("kbg", [64, 4, 128], BF16)
            kd = T2("kd", [64, 4, 128], BF16)
            bg = T2("bg", [64, 4])
            ekd = T2("ekd", [64, 4])
            egr = T2("egr", [128, 4, 64])
            qdT = T2("qdT", [128, 4, 64], BF16)
            u_sb = T2("u_sb", [64, 4, 128])
            wTb = T2("wTb", [128, 4, 64], BF16)
            e_b = T2("e_b", [64, 4, 128], BF16)

            def v3(ap, a):
                return ap.rearrange("p (a b) -> p a b", a=a)

            for d in range(2):
                S.begin()
                state['psr'] = (d * 4, d * 4 + 4)
                Sf, Sb = Sf2[d], Sb2[d]
                kSf, kSb = ('Sf', d), ('Sb', d)
                mS = cst[0:64, C_SL:C_SL + 64] if d == 0 else cst[0:64, C_SU:C_SU + 64]
                mST = cst[0:64, C_SU:C_SU + 64] if d == 0 else cst[0:64, C_SL:C_SL + 64]
                mIT = cst[0:64, C_U:C_U + 64] if d == 0 else cst[0:64, C_L:C_L + 64]
                cum = cst[0:64, C_U:C_U + 64] if d == 0 else cst[0:64, C_L:C_L + 64]
                last = 63 if d == 0 else 0
                I64 = cst[0:64, C_ID:C_ID + 64]

                def bc1(ap2):
                    return ap2.unsqueeze(1).to_broadcast([64, 4, 64])

                def bc2(ap2, n=64):
                    return ap2.unsqueeze(2).to_broadcast([64, 4, n])
                dma('sp', Sf[:, :, :], s0_d[d].rearrange("h k v -> k h v"), w=[kSf])
                cp(Sb[:, :, :], Sf[:, :, :], r=[kSf], w=[kSb], eng='act')
                order = list(range(16)) if d == 0 else list(range(15, -1, -1))
                GCH = int(os.environ.get('GCH', '16'))
                GP = int(os.environ.get('GP', '99'))
                for step, c in enumerate(order[:GCH]):
                    p = d
                    K = lambda n: (n, p)
                    csl = slice(c * 64, (c + 1) * 64)
                    b0 = nextps()
                    tr(ps[b0][0:64, 0:8], gT[0:8, csl], ident_f[0:8, 0:8], r=['gT', 'cst'], w=[psk[b0]])
                    tr(ps[b0][0:64, 8:16], betaT[0:8, csl], ident_f[0:8, 0:8], r=['betaT', 'cst'], w=[psk[b0]])
                    cp(gbtok[p][:, :], ps[b0][0:64, 0:16], r=[psk[b0]], w=[K('gbtok')], eng='act')
                    gtok = gbtok[p][:, d * 4:d * 4 + 4]
                    btok = gbtok[p][:, 8 + d * 4:8 + d * 4 + 4]
                    b1 = nextps()
                    mm(ps[b1][0:64, 0:4], cum, gtok, r=['cst', K('gbtok')], w=[psk[b1]])
                    cp(gcc[p][:, :], ps[b1][0:64, 0:4], r=[psk[b1]], w=[K('gcc')], eng='act')
                    if GP < 1:
                        continue
                    tt(Dgb[p][:, 0, :, :], bc1(I64), bc2(gcc[p][:, :]), ALU.mult, r=['cst', K('gcc')], w=[K('Dgb')])
                    tt(Dgb[p][:, 1, :, :], bc1(I64), bc2(btok), ALU.mult, r=['cst', K('gbtok')], w=[K('Dgb')])
                    bR = nextps()
                    mm(ps[bR][:, 0:256], ones_f[0:64, 0:128], Dgb[p][:, 0, :, :].rearrange('p a b -> p (a b)'), r=['ones_f', K('Dgb')], w=[psk[bR]])
                    mm(ps[bR][0:64, 256:512], ones_f[0:64, 0:64], Dgb[p][:, 1, :, :].rearrange('p a b -> p (a b)'), r=['ones_f', K('Dgb')], w=[psk[bR]])
                    grow = v3(ps[bR][:, 0:256], 4)
                    brow = v3(ps[bR][0:64, 256:512], 4)
                    tt(Dm[p][:, :, :], bc2(gcc[p][:, :]), grow[0:64], ALU.subtract, r=[K('gcc'), psk[bR]], w=[K('Dm')])
                    ts(t1[p][:, :, :], Dm[p][:, :, :], 0.0, ALU.min, r=[K('Dm')], w=[K('t1')])
                    ts(t2_[p][:, :, :], Dm[p][:, :, :], -1.0, ALU.mult, r=[K('Dm')], w=[K('t2')], s2=0.0, op1=ALU.min)
                    act(E1[p][:, :, :], t1[p][:, :, :], AF.Exp, r=[K('t1')], w=[K('E1')])
                    act(E2[p][:, :, :], t2_[p][:, :, :], AF.Exp, r=[K('t2')], w=[K('E2')])
                    act(egr[p][:, :, :], grow, AF.Exp, r=[psk[bR]], w=[K('egr')])
                    tt(ekd[p][:, :], grow[0:64, :, last], gcc[p][:, :], ALU.subtract, r=[psk[bR], K('gcc')], w=[K('ekd')])
                    act(ekd[p][:, :], ekd[p][:, :], AF.Exp, r=[K('ekd')], w=[K('ekd')])
                    act(bg[p][:, :], gcc[p][:, :], AF.Exp, r=[K('gcc')], w=[K('bg')])
                    tt(bg[p][:, :], bg[p][:, :], btok, ALU.mult, r=[K('bg'), K('gbtok')], w=[K('bg')])
                    tt(decIT[p][:, :, :], E2[p][:, :, :], bc1(mIT), ALU.mult, r=[K('E2'), 'cst'], w=[K('decIT')])
                    tt(E1[p][:, :, :], E1[p][:, :, :], bc1(mS), ALU.mult, r=[K('E1'), 'cst'], w=[K('E1')])
                    tt(E2[p][:, :, :], E2[p][:, :, :], bc1(mST), ALU.mult, r=[K('E2'), 'cst'], w=[K('E2')])
                    if GP < 2:
                        continue
                    bK = nextps()
                    for h in range(4):
                        mm(ps[bK][0:64, h * 64:(h + 1) * 64], kT[:, h, csl], kT[:, h, csl], r=[('kT', h)], w=[psk[bK]])
                        mm(ps[bK][0:64, 256 + h * 64:256 + (h + 1) * 64], kT[:, h, csl], qT[:, h, csl],
                           r=[('kT', h), ('qT', h)], w=[psk[bK]])
                    pKK = v3(ps[bK][0:64, 0:256], 4)
                    pQK = v3(ps[bK][0:64, 256:512], 4)
                    N0, NT0 = Nn[0][p], NTn[0][p]
                    tt(t1[p][:, :, :], pKK, E1[p][:, :, :], ALU.mult, r=[psk[bK], K('E1')], w=[K('t1')])
                    stt(N0[:, :, :], t1[p][:, :, :], -1.0, bc2(btok), ALU.mult, ALU.mult,
                        r=[K('t1'), K('gbtok')], w=[K('Nn0')])
                    tt(t2_[p][:, :, :], pKK, E2[p][:, :, :], ALU.mult, r=[psk[bK], K('E2')], w=[K('t2')])
                    stt(NT0[:, :, :], t2_[p][:, :, :], -1.0, brow, ALU.mult, ALU.mult,
                        r=[K('t2'), psk[bR]], w=[K('NTn0')])
                    tt(TT[p][:, :, :], NT0[:, :, :], bc1(I64), ALU.add, r=[K('NTn0'), 'cst'], w=[K('TT')])
                    tt(intraT[p][:, :, :], pQK, decIT[p][:, :, :], ALU.mult, r=[psk[bK], K('decIT')], w=[K('intraT')])
                    if GP < 3:
                        continue
                    NLEV = int(os.environ.get('NLEV', '5'))
                    NPART = int(os.environ.get('NPART', '3'))
                    for lev in range(1, NLEV + 1):
                        a, bprev = lev % 2, (lev - 1) % 2
                        Np, NTp = Nn[bprev][p], NTn[bprev][p]
                        Nc, NTc = Nn[a][p], NTn[a][p]
                        bn = nextps()
                        for h in range(4):
                            mm(ps[bn][0:64, h * 64:(h + 1) * 64], NTp[:, h, :], Np[:, h, :],
                               r=[K('Nn%d' % bprev), K('NTn%d' % bprev)], w=[psk[bn]])
                        if lev < 5 and NPART >= 2:
                            for h in range(4):
                                mm(ps[bn][0:64, 256 + h * 64:256 + (h + 1) * 64], Np[:, h, :], NTp[:, h, :],
                                   r=[K('Nn%d' % bprev), K('NTn%d' % bprev)], w=[psk[bn]])
                        cp(Nc[:, :, :], v3(ps[bn][0:64, 0:256], 4), r=[psk[bn]], w=[K('Nn%d' % a)], eng='act')
                        if lev < 5 and NPART >= 2:
                            cp(NTc[:, :, :], v3(ps[bn][0:64, 256:512], 4), r=[psk[bn]], w=[K('NTn%d' % a)], eng='act')
                        if NPART < 3:
                            continue
                        bt = nextps()
                        for h in range(4):
                            mm(ps[bt][0:64, h * 64:(h + 1) * 64], Nc[:, h, :], TT[p][:, h, :],
                               r=[K('Nn%d' % a), K('TT')], w=[psk[bt]])
                        tt(TT[p][:, :, :], TT[p][:, :, :], v3(ps[bt][0:64, 0:256], 4), ALU.add,
                           r=[K('TT'), psk[bt]], w=[K('TT')])
                    cp(TTb[p][:, :, :], TT[p][:, :, :], r=[K('TT')], w=[K('TTb')], eng='act')
                    if GP < 4:
                        continue
                    bkv = nextps()
                    pkv = ps[bkv][:, :].bitcast(BF16)
                    for h in range(4):
                        tr(pkv[0:64, h * 128:(h + 1) * 128], kT[:, h, csl], ident_bf[:, :], r=[('kT', h), 'ident_bf'], w=[psk[bkv]])
                        tr(pkv[0:64, 512 + h * 128:512 + (h + 1) * 128], vT[:, h, csl], ident_bf[:, :],
                           r=[('vT', h), 'ident_bf'], w=[psk[bkv]])
                    cp(ktok[p][:, :, :], v3(pkv[0:64, 0:512], 4), r=[psk[bkv]], w=[K('ktok')], eng='act')
                    cp(vtok[p][:, :, :], v3(pkv[0:64, 512:1024], 4), r=[psk[bkv]], w=[K('vtok')], eng='act')
                    tt(vb[p][:, :, :], vtok[p][:, :, :], bc2(btok, 128), ALU.mult, r=[K('vtok'), K('gbtok')], w=[K('vb')])
                    tt(kbg[p][:, :, :], ktok[p][:, :, :], bc2(bg[p][:, :], 128), ALU.mult, r=[K('ktok'), K('bg')], w=[K('kbg')])
                    tt(kd[p][:, :, :], ktok[p][:, :, :], bc2(ekd[p][:, :], 128), ALU.mult, r=[K('ktok'), K('ekd')], w=[K('kd')])
                    tt(qdT[p][:, :, :], qT[:, :, csl], egr[p][:, :, :], ALU.mult,
                       r=[('qT', 0), ('qT', 1), ('qT', 2), ('qT', 3), K('egr')], w=[K('qdT')])
                    if GP < 5:
                        continue
                    bu = nextps()
                    bw = nextps()
                    for h in range(4):
                        mm(ps[bu][0:64, h * 128:(h + 1) * 128], TTb[p][:, h, :], vb[p][:, h, :],
                           r=[K('TTb'), K('vb')], w=[psk[bu]])
                        mm(ps[bw][:, h * 64:(h + 1) * 64], kbg[p][:, h, :], TTb[p][:, h, :],
                           r=[K('kbg'), K('TTb')], w=[psk[bw]])
                    cp(u_sb[p][:, :, :], v3(ps[bu][0:64, :], 4), r=[psk[bu]], w=[K('u_sb')], eng='act')
                    cp(wTb[p][:, :, :], v3(ps[bw][:, 0:256], 4), r=[psk[bw]], w=[K('wTb')], eng='act')
                    if GP < 6:
                        continue
                    be = nextps()
                    for h in range(4):
                        mm(ps[be][0:64, h * 128:(h + 1) * 128], wTb[p][:, h, :], Sb[:, h, :], r=[K('wTb'), kSb], w=[psk[be]])
                    tt(e_b[p][:, :, :], u_sb[p][:, :, :], v3(ps[be][0:64, :], 4), ALU.subtract,
                       r=[K('u_sb'), psk[be]], w=[K('e_b')])
                    bo = nextps()
                    for h in range(4):
                        mm(ps[bo][:, h * 64:(h + 1) * 64], Sb[:, h, :], qdT[p][:, h, :], start=True, stop=False,
                           r=[kSb, K('qdT')], w=[psk[bo]])
                        mm(ps[bo][:, h * 64:(h + 1) * 64], e_b[p][:, h, :], intraT[p][:, h, :], start=False, stop=True,
                           r=[K('e_b'), K('intraT')], w=[psk[bo]])
                    po = v3(ps[bo][:, 0:256], 4)
                    tt(oacc[:, :, csl], oacc[:, :, csl], po, ALU.add, r=[psk[bo], ('oacc', c)], w=[('oacc', c)])
                    bs = nextps()
                    for h in range(4):
                        mm(ps[bs][:, h * 128:(h + 1) * 128], kd[p][:, h, :], e_b[p][:, h, :], r=[K('kd'), K('e_b')], w=[psk[bs]])
                    for h in range(4):
                        stt(Sf[:, h, :], Sf[:, h, :], egr[p][:, h, last:last + 1], ps[bs][:, h * 128:(h + 1) * 128],
                            ALU.mult, ALU.add, r=[kSf, K('egr'), psk[bs]], w=[kSf])
                    seg_end = (c % 4 == 3) if d == 0 else (c % 4 == 0)
                    if seg_end:
                        seg = c // 4
                        dma('sp', gso_d[d, seg].rearrange("h k v -> k h v"), Sf[:, :, :], r=[kSf])
                        ts(Sf[:, :, :], Sf[:, :, :], flag[:, 0:1], ALU.mult, r=[kSf, 'flag'], w=[kSf])
                    cp(Sb[:, :, :], Sf[:, :, :], r=[kSf], w=[kSb], eng='act')
                chains.append(S.end())
            state['psr'] = None
            S.merge(chains)
            S.flush()
            sg_.close()
            qsq = T("qsq2", [128, NT], BF16)
            rinv = T("rinv2", [128, NT])
            sctmp = T("sctmp2", [128, NT])

            if STAGE < 3:
                return
            for h in range(4):
                act(qsq[:, :], oacc[:, h, :], AF.Square, r=[('oacc', c) for c in range(16)], w=['qsq'])
                for t2 in range(2):
                    bb = nextps()
                    tsl = slice(t2 * 512, (t2 + 1) * 512)
                    mm(ps[bb][:, :], ones_bf[:, :], qsq[:, tsl], r=['ones_bf', 'qsq'], w=[psk[bb]])
                    act(rinv[:, tsl], ps[bb][:, :], AF.Sqrt, r=[psk[bb], 'epsb'], w=['rinv'],
                        bias=epsb[:, 0:1], scale=1.0 / 128)
                rcp(rinv[:, :], rinv[:, :], r=['rinv'], w=['rinv'])
                stt(sctmp[:, :], oacc[:, h, :], ppc('gdn_norm')[:, 0:1], rinv[:, :], ALU.mult, ALU.mult,
                    r=[('oacc', c) for c in range(16)] + ['pp', 'rinv'], w=['sctmp'])
                tt(mixT[:, h, :], sctmp[:, :], mixT[:, h, :], ALU.mult, r=['sctmp', ('mixT', h)], w=[('mixT', h)])

            def c_out(c0, cw, th, b):
                fc = c0 // 128
                tsl = slice(th * 512, (th + 1) * 512)
                stt(x_sb[:, fc, tsl], ps[b][:, :], modv[:, 0, 16 + fc:17 + fc], x_sb[:, fc, tsl],
                    ALU.mult, ALU.add, r=[psk[b], 'modv', ('x', fc)], w=[('x', fc)])
            linear(evout_d, 8, [(i * 512, 512) for i in range(2)], mixT, 'mixT', c_out)

        def layer1_mixer():
            with contextlib.ExitStack() as so:
                mix1T = so.enter_context(nc.sbuf_tensor("sb_mix1T", [128, 8, NT], BF16))
                with contextlib.ExitStack() as sc:
                    alloc_work(sc, 'p3')
                    attention_phase(sc, mix1T)
                    S.flush()
                if L1PART >= 2:
                    with contextlib.ExitStack() as sr:
                        rwkv_phases(sr, mix1T)
                with contextlib.ExitStack() as sc:
                    alloc_work(sc, 'p6')

                    def c_out(c0, cw, th, b):
                        fc = c0 // 128
                        tsl = slice(th * 512, (th + 1) * 512)
                        stt(x_sb[:, fc, tsl], ps[b][:, :], modv[:, 1, 16 + fc:17 + fc], x_sb[:, fc, tsl],
                            ALU.mult, ALU.add, r=[psk[b], 'modv', ('x', fc)], w=[('x', fc)])
                    linear(odout_d, 8, [(i * 512, 512) for i in range(2)], mix1T, 'mix1T', c_out)
                    S.flush()

        def attention_phase(sc, mix1T):
            T = lambda n, s, d=F32: sc.enter_context(nc.sbuf_tensor('sb_a_' + n, list(s), d))
            qTr = T("qTr", [128, 4, NT], BF16)
            kpad = T("kpad", [128, 2, NT + 256], BF16)
            vTb = T("vTb", [128, NT], BF16)
            vtokp = T("vtokp", [128, 10, 128], BF16)
            ckT = T("ckT", [128, 2, 512], BF16)
            cvt = T("cvt", [128, 4, 128], BF16)
            cosT = T("cosT", [128, NT])
            sinT = T("sinT", [128, NT])
            Rm = T("Rm", [128, 128])
            xf = T("xf", [128, NT])
            xr = T("xr", [128, NT])
            mb = [T("mb%d" % i, [128, 896]) for i in range(2)]
            scs2 = [T("scs%d" % i, [128, 896]) for i in range(2)]
            Pb2 = [T("Pb%d" % i, [128, 896], BF16) for i in range(2)]
            PT = T("PT", [128, 7, 128], BF16)
            atok2 = [T("atok%d" % i, [128, 512], BF16) for i in range(2)]
            sm2 = [T("sm%d" % i, [128, 8]) for i in range(2)]

            dma('sp', cosT[:, :], ropec_d[:, :], w=['cosT'])
            dma('sp', sinT[:, :], ropes_d[:, :], w=['sinT'])
            dma('sp', Rm[:, :], rm_d[:, :], w=['Rm'])
            for kvh in range(2):
                dma('pool', ckT[:, kvh, :], ckT_d[kvh], w=['ckT'])
            dma('pool', cvt[:, :, :], cvt_d[:, :, :], w=['cvt'])
            mset(kpad[:, :, 0:128], 0.0, w=['kpad'])
            mset(kpad[:, :, 128 + NT:256 + NT], 0.0, w=['kpad'])
            mset(vtokp[:, 0, :], 0.0, w=['vtokp'])
            mset(vtokp[:, 9, :], 0.0, w=['vtokp'])

            norm_mod(gsA[:, 1, :], modv[:, 1, 0:8], hT, 'hT')

            def rope(dst_ap_fn, dkey):
                for t2 in range(2):
                    tsl = slice(t2 * 512, (t2 + 1) * 512)
                    bb = nextps()
                    mm(ps[bb][:, :], Rm[:, :], xf[:, tsl], r=['Rm', 'xf'], w=[psk[bb]])
                    tt(xr[:, tsl], ps[bb][:, :], sinT[:, tsl], ALU.mult, r=[psk[bb], 'sinT'], w=['xr'])
                    tt(xf[:, tsl], xf[:, tsl], cosT[:, tsl], ALU.mult, r=['xf', 'cosT'], w=['xf'])
                    tt(dst_ap_fn(tsl), xf[:, tsl], xr[:, tsl], ALU.add, r=['xf', 'xr'], w=[dkey])

            def c_q(c0, cw, th, b):
                a = c0 // 128
                cp(xf[:, th * 512:(th + 1) * 512], ps[b][:, :], r=[psk[b]], w=['xf'], eng='act')
                if th == 1:
                    rope(lambda tsl: qTr[:, a, tsl], 'qTr')
            linear(odin_d, 8, [(0, 512)], hT, 'hT', c_q)

            def c_k(c0, cw, th, b):
                kvh = (c0 - 512) // 128
                cp(xf[:, th * 512:(th + 1) * 512], ps[b][:, :], r=[psk[b]], w=['xf'], eng='act')
                if th == 1:
                    dma('sp', kvo_d[0, kvh * 64:(kvh + 1) * 64, :], xf[0:64, :], r=['xf'])
                    rope(lambda tsl: kpad[:, kvh, 128 + tsl.start:128 + tsl.stop], 'kpad')
            linear(odin_d, 8, [(512, 256)], hT, 'hT', c_k)

            def c_v(c0, cw, th, b):
                cp(xf[:, th * 512:(th + 1) * 512], ps[b][:, :], r=[psk[b]], w=['xf'], eng='act')
                if th == 1:
                    dma('sp', kvo_d[1, :, :], xf[:, :], r=['xf'])
                    cp(vTb[:, :], xf[:, :], r=['xf'], w=['vTb'], eng='act')
                    for blk in range(8):
                        bb = nextps()
                        pv_ = ps[bb][:, :].bitcast(BF16)
                        tr(pv_[:, 0:128], vTb[:, blk * 128:(blk + 1) * 128], ident_bf[:, :], r=['vTb', 'ident_bf'], w=[psk[bb]])
                        cp(vtokp[:, 1 + blk, :], pv_[:, 0:128], r=[psk[bb]], w=['vtokp'], eng='act')
            linear(odin_d, 8, [(768, 128)], hT, 'hT', c_v)

            sink = ppc('sink')

            def stageA(u):
                n, h = u // 8, u % 8
                ub = u % 2
                mbi = n % 2
                if h == 0:
                    dma('sp', mb[mbi][:, :], maskb_d[n], w=[('mb', mbi)])
                kvh, hp, a = h // 4, h % 2, h // 2
                pb_ = slice(hp * 64, (hp + 1) * 64)
                qap = qTr[pb_, a, n * 128:(n + 1) * 128]
                b1 = nextps()
                b2 = nextps()
                mm(ps[b1][:, 0:384], qap, kpad[pb_, kvh, n * 128:n * 128 + 384], r=['qTr', 'kpad'], w=[psk[b1]])
                mm(ps[b2][:, :], qap, ckT[pb_, kvh, :], r=['qTr', 'ckT'], w=[psk[b2]])
                sc_, P_, sm_ = scs2[ub], Pb2[ub], sm2[ub]
                ks, kp, km = ('scs', ub), ('Pb', ub), ('sm', ub)
                stt(sc_[:, 0:384], ps[b1][:, 0:384], 0.125, mb[mbi][:, 0:384], ALU.mult, ALU.add,
                    r=[psk[b1], ('mb', mbi)], w=[ks])
                stt(sc_[:, 384:896], ps[b2][:, :], 0.125, mb[mbi][:, 384:896], ALU.mult, ALU.add,
                    r=[psk[b2], ('mb', mbi)], w=[ks])
                rmax(sm_[:, 0:1], sc_[:, :], r=[ks], w=[km])
                tt(sm_[:, 1:2], sm_[:, 0:1], sink[:, h:h + 1], ALU.max, r=[km, 'pp'], w=[km])
                ts(sm_[:, 2:3], sm_[:, 1:2], -1.0, ALU.mult, r=[km], w=[km])
                expacc(P_[:, :], sc_[:, :], sm_[:, 2:3], sm_[:, 3:4], r=[ks, km], w=[kp, km])
                act(sm_[:, 4:5], sink[:, h:h + 1], AF.Exp, r=['pp', km], w=[km], bias=sm_[:, 2:3])
                tt(sm_[:, 5:6], sm_[:, 3:4], sm_[:, 4:5], ALU.add, r=[km], w=[km])
                rcp(sm_[:, 6:7], sm_[:, 5:6], r=[km], w=[km])

            def stageB(u):
                n, h = u // 8, u % 8
                ub = u % 2
                kvh = h // 4
                P_, sm_ = Pb2[ub], sm2[ub]
                kp, km = ('Pb', ub), ('sm', ub)
                at_ = atok2[n % 2]
                ka_ = ('atok', n % 2)
                bt_ = nextps()
                ptp = ps[bt_][:, :].bitcast(BF16)
                for kb in range(7):
                    tr(ptp[:, kb * 128:(kb + 1) * 128], P_[:, kb * 128:(kb + 1) * 128], ident_bf[:, :],
                       r=[kp, 'ident_bf'], w=[psk[bt_]])
                cp(PT[:, :, :], ptp[:, 0:896].rearrange("p (a b) -> p a b", a=7), r=[psk[bt_]], w=['PT'], eng='act')
                bo_ = nextps()
                for kb in range(7):
                    if kb < 3:
                        vap = vtokp[:, n + kb, kvh * 64:(kvh + 1) * 64]
                        rk = 'vtokp'
                    else:
                        vap = cvt[:, kb - 3, kvh * 64:(kvh + 1) * 64]
                        rk = 'cvt'
                    mm(ps[bo_][:, 0:64], PT[:, kb, :], vap, start=(kb == 0), stop=(kb == 6), r=['PT', rk], w=[psk[bo_]])
                act(at_[:, h * 64:(h + 1) * 64], ps[bo_][:, 0:64], AF.Identity, r=[psk[bo_], km], w=[ka_],
                    scale=sm_[:, 6:7])
                if h == 7:
                    ba_ = nextps()
                    pa_ = ps[ba_][:, :].bitcast(BF16)
                    for a in range(4):
                        tr(pa_[:, a * 128:(a + 1) * 128], at_[:, a * 128:(a + 1) * 128], ident_bf[:, :],
                           r=[ka_, 'ident_bf'], w=[psk[ba_]])
                    cp(mix1T[:, 0:4, n * 128:(n + 1) * 128], pa_[:, 0:512].rearrange("p (a b) -> p a b", a=4),
                       r=[psk[ba_]], w=[('mix1T', 0), ('mix1T', 1), ('mix1T', 2), ('mix1T', 3)], eng='act')
            stageA(0)
            for u in range(1, 64):
                stageA(u)
                stageB(u - 1)
            stageB(63)

        def rwkv_phases(sr, mix1T):
            TR = lambda n, s, d=F32: sr.enter_context(nc.sbuf_tensor('sb_r_' + n, list(s), d))
            r_b = TR("r_b", [128, 4, NT], BF16)
            k_b = TR("k_b", [128, 4, NT], BF16)
            kk_b = TR("kk_b", [128, 4, NT], BF16)
            v_b = TR("v_b", [128, 4, NT], BF16)
            a_b = TR("a_b", [128, 2, 4, NT], BF16)
            wlt = TR("wlt", [128, NT], BF16)
            bon = TR("bon", [128, 4, NT], BF16)
            gate = TR("gate", [128, 4, NT], BF16)
            wup = TR("wup", [128, 512], BF16)
            bones = TR("bones", [128, 128], BF16)
            wmid = TR("wmid", [128, 15])
            dma('pool', wup[:, :], wup_d[:, :], w=['wup'])
            dma('pool', bones[:, :], bones_d[:, :], w=['bones'])
            mu0 = ppc('mu', 0, 15)
            mu1 = ppc('mu', 15, 15)
            tt(wmid[:, :], mu0, mu1, ALU.add, r=['pp'], w=['wmid'])
            ts(wmid[:, :], wmid[:, :], -1.0, ALU.mult, r=['wmid'], w=['wmid'], s2=1.0, op1=ALU.add)

            with contextlib.ExitStack() as sc:
                alloc_work(sc, 'p4')
                T = lambda n, s, d=F32: sc.enter_context(nc.sbuf_tensor('sb_rp_' + n, list(s), d))
                pad = [T("pad%d" % i, [128, 4, 258]) for i in range(2)]
                xs_ = T("xs", [128, 4, 256])
                t1 = T("t1", [128, NT])
                t2 = T("t2", [128, NT])
                sqh = T("sqh", [128, NT], BF16)
                alb = T("alb", [128, NT], BF16)
                glb = T("glb", [128, NT], BF16)
                aup = T("aup", [128, 512], BF16)
                gup = T("gup", [128, 512], BF16)
                dma('pool', aup[:, :], aup_d[:, :], w=['aup'])
                dma('pool', gup[:, :], gup_d[:, :], w=['gup'])
                for i in range(2):
                    mset(pad[i][:, :, :], 0.0, w=[('rpad', i)])
                norm_mod(gsA[:, 1, :], modv[:, 1, 0:8], hT, 'hT')
                xsf = xs_[:, :, :].rearrange("p s t -> p (s t)")

                def hsum(dst_ps_fn, src_bf):
                    for t2_ in range(2):
                        bb = nextps()
                        mm(ps[bb][:, :], bones[:, :], src_bf[:, t2_ * 512:(t2_ + 1) * 512], r=['bones', 'sqh'], w=[psk[bb]])
                        dst_ps_fn(t2_, bb)

                def c_rw(c0, cw, th, b):
                    j = (c0 - 896) // 128
                    pi = j % 2
                    p = pad[pi]
                    cp(p[:, 2 * th:2 * th + 2, 1:257], ps[b][:, :].rearrange("p (s t) -> p s t", s=2),
                       r=[psk[b]], w=[('rpad', pi)], eng='act')
                    if th == 0:
                        return
                    ts(p[:, 1:4, 0:1], p[:, 0:3, 256:257], flag[:, 0:1], ALU.mult, r=[('rpad', pi), 'flag'], w=[('rpad', pi)])
                    ts(p[:, 0:3, 257:258], p[:, 1:4, 1:2], flag[:, 0:1], ALU.mult, r=[('rpad', pi), 'flag'], w=[('rpad', pi)])
                    ts(xs_[:, :, :], p[:, :, 0:256], mu0[:, j:j + 1], ALU.mult, r=[('rpad', pi), 'pp'], w=['xs'])
                    stt(xs_[:, :, :], p[:, :, 1:257], wmid[:, j:j + 1], xs_[:, :, :], ALU.mult, ALU.add,
                        r=[('rpad', pi), 'wmid', 'xs'], w=['xs'])
                    stt(xs_[:, :, :], p[:, :, 2:258], mu1[:, j:j + 1], xs_[:, :, :], ALU.mult, ALU.add,
                        r=[('rpad', pi), 'pp', 'xs'], w=['xs'])
                    cq = j % 4
                    if j < 4:
                        cp(r_b[:, cq, :], xsf, r=['xs'], w=[('r_b', cq)], eng='act')
                    elif j < 8:
                        cp(k_b[:, cq, :], xsf, r=['xs'], w=[('k_b', cq)], eng='act')
                        ts(t1[:, :], xsf, ppc('k_k')[:, cq:cq + 1], ALU.mult, r=['xs', 'pp'], w=['t1'])
                        act(sqh[:, :], t1[:, :], AF.Square, r=['t1'], w=['sqh'])

                        def d1(t2_, bb):
                            act(t2[:, t2_ * 512:(t2_ + 1) * 512], ps[bb][:, :], AF.Sqrt, r=[psk[bb], 'epsb'], w=['t2'],
                                bias=epsb[:, 1:2])
                        hsum(d1, sqh)
                        rcp(t2[:, :], t2[:, :], r=['t2'], w=['t2'])
                        tt(kk_b[:, cq, :], t1[:, :], t2[:, :], ALU.mult, r=['t1', 't2'], w=[('kk_b', cq)])
                    elif j < 12:
                        cp(v_b[:, cq, :], xsf, r=['xs'], w=[('v_b', cq)], eng='act')
                    elif j == 12:
                        act(wlt[:, :], xsf, AF.Tanh, r=['xs'], w=['wlt'])
                    elif j == 13:
                        cp(alb[:, :], xsf, r=['xs'], w=['alb'], eng='act')
                        for d in range(2):
                            dsl = slice(d * 64, (d + 1) * 64)
                            for cq2 in range(4):
                                for t2_ in range(2):
                                    bb = nextps()
                                    mm(ps[bb][:, :], aup[dsl, cq2 * 128:(cq2 + 1) * 128], alb[dsl, t2_ * 512:(t2_ + 1) * 512],
                                       r=['aup', 'alb'], w=[psk[bb]])
                                    act(a_b[:, d, cq2, t2_ * 512:(t2_ + 1) * 512], ps[bb][:, :], AF.Sigmoid,
                                        r=[psk[bb], 'pp'], w=[('a_b', d, cq2)], bias=ppc('a0')[:, d * 4 + cq2:d * 4 + cq2 + 1])
                    else:
                        act(glb[:, :], xsf, AF.Sigmoid, r=['xs'], w=['glb'])
                        for cq2 in range(4):
                            for t2_ in range(2):
                                bb = nextps()
                                mm(ps[bb][:, :], gup[:, cq2 * 128:(cq2 + 1) * 128], glb[:, t2_ * 512:(t2_ + 1) * 512],
                                   r=['gup', 'glb'], w=[psk[bb]])
                                cp(gate[:, cq2, t2_ * 512:(t2_ + 1) * 512], ps[bb][:, :], r=[psk[bb]], w=[('gate', cq2)], eng='act')
                linear(odin_d, 8, [(896 + j * 128, 128) for j in range(15)], hT, 'hT', c_rw)
                for cq in range(4):
                    for d in range(2):
                        ts(t1[:, :], a_b[:, d, cq, :], -1.0, ALU.add, r=[('a_b', d, cq), 'pp'], w=['t1'],
                           s2=ppc('k_a')[:, cq:cq + 1], op1=ALU.mult)
                        stt(t2[:, :] if d == 0 else t1[:, :], t1[:, :], 1.0, k_b[:, cq, :], ALU.add, ALU.mult,
                            r=['t1', ('k_b', cq)], w=['t2' if d == 0 else 't1'])
                    tt(t2[:, :], t2[:, :], t1[:, :], ALU.add, r=['t1', 't2'], w=['t2'])
                    stt(sqh[:, :], t2[:, :], ppc('r_k')[:, cq:cq + 1], r_b[:, cq, :], ALU.mult, ALU.mult,
                        r=['t2', 'pp', ('r_b', cq)], w=['sqh'])

                    def d2(t2_, bb):
                        tsl = slice(t2_ * 512, (t2_ + 1) * 512)
                        tt(bon[:, cq, tsl], ps[bb][:, :], v_b[:, cq, tsl], ALU.mult, r=[psk[bb], ('v_b', cq)], w=[('bon', cq)])
                    hsum(d2, sqh)
                S.flush()

            if L1PART < 3:
                return
            with contextlib.ExitStack() as sc:
                state['drain'] = True
                state['pemode'] = None
                T = lambda n, s, d=F32: sc.enter_context(nc.sbuf_tensor('sb_rs_' + n, list(s), d))
                lw = T("lw", [128, 4, NT])
                yacc = T("yacc", [128, 4, NT])
                Pf = T("Pf", [128, 4, 64])
                Pbf = T("Pbf", [128, 4, 64], BF16)
                Lf = T("Lf", [128, 4, 64])
                Lam = T("Lam", [128, 4, 64])
                e1 = T("e1", [128, 4, 64])
                e2 = T("e2", [128, 4, 64])
                e3 = T("e3", [128, 4, 64])
                e4 = T("e4", [128, 4, 64])
                eTot = T("eTot", [128, 4])
                ka = T("ka", [128, 4, 64])
                k2 = T("k2", [128, 4, 64])
                rt = T("rt", [128, 4, 64], BF16)
                bt = T("bt", [128, 4, 64], BF16)
                at = T("at", [128, 4, 64], BF16)
                kt = T("kt", [128, 4, 64], BF16)
                aG = T("aG", [128, 4, 64], BF16)
                kG = T("kG", [128, 4, 64], BF16)
                Nn = [T("Nn%d" % i, [64, 8, 64]) for i in range(2)]
                NTn = [T("NTn%d" % i, [64, 8, 64]) for i in range(2)]
                TT = T("TT", [64, 8, 64])
                TTb = T("TTb", [64, 8, 64], BF16)
                Tn = T("Tn", [64, 8, 64])
                O32a = T("O32a", [64, 8, 64])
                O32Ta = T("O32Ta", [64, 8, 64])
                O64a = T("O64a", [64, 8, 64])
                BTm = T("BTm", [64, 8, 64], BF16)
                RaT = T("RaT", [64, 8, 64], BF16)
                RkT = T("RkT", [64, 8, 64], BF16)
                btok = T("btok", [64, 4, 128], BF16)
                aGtok = T("aGtok", [64, 4, 128], BF16)
                kGtok = T("kGtok", [64, 4, 128], BF16)
                vtok = T("vtok", [64, 4, 128], BF16)
                BVb = T("BVb", [64, 8, 64], BF16)
                U0 = T("U0", [64, 8, 64])
                Ub = T("Ub", [64, 8, 64], BF16)
                WmTb = T("WmTb", [128, 4, 64], BF16)
                onesc = T("onesc", [128, 64])
                mset(onesc[:, :], 1.0, w=['onesc'])
                I64 = cst[0:64, C_ID:C_ID + 64]
                HORD = [0, 2, 4, 6, 1, 3, 5, 7]

                def v3(ap, a):
                    return ap.rearrange("p (a b) -> p a b", a=a)

                def bc8(ap2):
                    return ap2.unsqueeze(1).to_broadcast([64, 8, 64])

                for d in range(2):
                    dsl = slice(d * 64, (d + 1) * 64)
                    mS = cst[0:64, C_SL:C_SL + 64] if d == 0 else cst[0:64, C_SU:C_SU + 64]
                    mST = cst[0:64, C_SU:C_SU + 64] if d == 0 else cst[0:64, C_SL:C_SL + 64]
                    mIT = cst[0:64, C_U:C_U + 64] if d == 0 else cst[0:64, C_L:C_L + 64]
                    for cq in range(4):
                        for t2_ in range(2):
                            bb = nextps()
                            mm(ps[bb][:, :], wup[dsl, cq * 128:(cq + 1) * 128], wlt[dsl, t2_ * 512:(t2_ + 1) * 512],
                               r=['wup', 'wlt'], w=[psk[bb]])
                            act(lw[:, cq, t2_ * 512:(t2_ + 1) * 512], ps[bb][:, :], AF.Sigmoid, r=[psk[bb], 'pp'], w=['lw'],
                                bias=ppc('w0')[:, d * 4 + cq:d * 4 + cq + 1])
                    ts(lw[:, :, :], lw[:, :, :], -0.6065306597126334, ALU.mult, r=['lw'], w=['lw'])
                    dma('sp', Pf[:, :, :], rs0_d[d], w=['Pf'])
                    cp(Pbf[:, :, :], Pf[:, :, :], r=['Pf'], w=['Pbf'], eng='act')
                    order = list(range(16)) if d == 0 else list(range(15, -1, -1))
                    RCH = int(os.environ.get('RCH', '16'))
                    for c in order[:RCH]:
                        csl = slice(c * 64, (c + 1) * 64)
                        lwc = lw[:, :, csl]
                        for cq in range(4):
                            scan(Lf[:, cq, :], onesc[:, :], lw[:, cq, csl], r=['lw', 'onesc'], w=['Lf'])
                        tot = Lf[:, :, 63:64]
                        if d == 0:
                            LamT, lk = Lf, 'Lf'
                        else:
                            tt(Lam[:, :, :], tot.to_broadcast([128, 4, 64]), Lf[:, :, :], ALU.subtract, r=['Lf'], w=['Lam'])
                            tt(Lam[:, :, :], Lam[:, :, :], lwc, ALU.add, r=['Lam', 'lw'], w=['Lam'])
                            LamT, lk = Lam, 'Lam'
                        act(e1[:, :, :], LamT[:, :, :], AF.Exp, r=[lk], w=['e1'])
                        tt(e2[:, :, :], LamT[:, :, :], lwc, ALU.subtract, r=[lk, 'lw'], w=['e2'])
                        act(e2[:, :, :], e2[:, :, :], AF.Exp, r=['e2'], w=['e2'])
                        act(e3[:, :, :], LamT[:, :, :], AF.Exp, r=[lk], w=['e3'], scale=-1.0)
                        tt(e4[:, :, :], tot.to_broadcast([128, 4, 64]), LamT[:, :, :], ALU.subtract, r=['Lf', lk], w=['e4'])
                        act(e4[:, :, :], e4[:, :, :], AF.Exp, r=['e4'], w=['e4'])
                        act(eTot[:, :], Lf[:, :, 63], AF.Exp, r=['Lf'], w=['eTot'])
                        stt(ka[:, :, :], kk_b[:, :, csl], -1.0, a_b[:, d, :, csl], ALU.mult, ALU.mult,
                            r=[('kk_b', q_) for q_ in range(4)] + [('a_b', d, q_) for q_ in range(4)], w=['ka'])
                        for cq in range(4):
                            ts(k2[:, cq, :], a_b[:, d, cq, csl], -1.0, ALU.add, r=[('a_b', d, cq), 'pp'], w=['k2'],
                               s2=ppc('k_a')[:, cq:cq + 1], op1=ALU.mult)
                        stt(k2[:, :, :], k2[:, :, :], 1.0, k_b[:, :, csl], ALU.add, ALU.mult,
                            r=['k2'] + [('k_b', q_) for q_ in range(4)], w=['k2'])
                        tt(rt[:, :, :], r_b[:, :, csl], e1[:, :, :], ALU.mult, r=[('r_b', q_) for q_ in range(4)] + ['e1'], w=['rt'])
                        tt(bt[:, :, :], kk_b[:, :, csl], e2[:, :, :], ALU.mult, r=[('kk_b', q_) for q_ in range(4)] + ['e2'], w=['bt'])
                        tt(at[:, :, :], ka[:, :, :], e3[:, :, :], ALU.mult, r=['ka', 'e3'], w=['at'])
                        tt(kt[:, :, :], k2[:, :, :], e3[:, :, :], ALU.mult, r=['k2', 'e3'], w=['kt'])
                        tt(aG[:, :, :], ka[:, :, :], e4[:, :, :], ALU.mult, r=['ka', 'e4'], w=['aG'])
                        tt(kG[:, :, :], k2[:, :, :], e4[:, :, :], ALU.mult, r=['k2', 'e4'], w=['kG'])
                        RP = int(os.environ.get('RP', '9'))
                        if RP < 2:
                            continue
                        bA, bAT, bBT, bRa, bRk = nextps(), nextps(), nextps(), nextps(), nextps()
                        for h in HORD:
                            cq, hp = h // 2, h % 2
                            pb_ = slice(hp * 64, (hp + 1) * 64)
                            hs = slice(h * 64, (h + 1) * 64)
                            mm(ps[bA][0:64, hs], bt[pb_, cq, :], at[pb_, cq, :], r=['bt', 'at'], w=[psk[bA]])
                            mm(ps[bAT][0:64, hs], at[pb_, cq, :], bt[pb_, cq, :], r=['bt', 'at'], w=[psk[bAT]])
                            mm(ps[bBT][0:64, hs], kt[pb_, cq, :], bt[pb_, cq, :], r=['bt', 'kt'], w=[psk[bBT]])
                            mm(ps[bRa][0:64, hs], at[pb_, cq, :], rt[pb_, cq, :], r=['rt', 'at'], w=[psk[bRa]])
                            mm(ps[bRk][0:64, hs], kt[pb_, cq, :], rt[pb_, cq, :], r=['rt', 'kt'], w=[psk[bRk]])
                        cm = lambda off: cst[0:64, off:off + 64]
                        if d == 0:
                            m16, m16T, m32, m32T, m64 = cm(C_S16L), cm(C_S16U), cm(C_O32L), cm(C_O32U), cm(C_O64L)
                        else:
                            m16, m16T, m32, m32T, m64 = cm(C_S16U), cm(C_S16L), cm(C_O32U), cm(C_O32L), cm(C_O64U)
                        pAv = v3(ps[bA][0:64, :], 8)
                        pATv = v3(ps[bAT][0:64, :], 8)
                        tt(Nn[0][:, :, :], pAv, bc8(m16), ALU.mult, r=[psk[bA], 'cst'], w=['rNn0'])
                        tt(NTn[0][:, :, :], pATv, bc8(m16T), ALU.mult, r=[psk[bAT], 'cst'], w=['rNTn0'])
                        tt(O32a[:, :, :], pAv, bc8(m32), ALU.mult, r=[psk[bA], 'cst'], w=['O32a'])
                        tt(O32Ta[:, :, :], pATv, bc8(m32T), ALU.mult, r=[psk[bAT], 'cst'], w=['O32Ta'])
                        tt(O64a[:, :, :], pAv, bc8(m64), ALU.mult, r=[psk[bA], 'cst'], w=['O64a'])
                        tt(TT[:, :, :], NTn[0][:, :, :], bc8(I64), ALU.add, r=['rNTn0', 'cst'], w=['rTT'])
                        tt(Tn[:, :, :], Nn[0][:, :, :], bc8(I64), ALU.add, r=['rNn0', 'cst'], w=['rTn'])
                        tt(BTm[:, :, :], v3(ps[bBT][0:64, :], 8), bc8(mST), ALU.mult, r=[psk[bBT], 'cst'], w=['BTm'])
                        tt(RaT[:, :, :], v3(ps[bRa][0:64, :], 8), bc8(mIT), ALU.mult, r=[psk[bRa], 'cst'], w=['RaT'])
                        tt(RkT[:, :, :], v3(ps[bRk][0:64, :], 8), bc8(mIT), ALU.mult, r=[psk[bRk], 'cst'], w=['RkT'])
                        if RP < 3:
                            continue

                        def mm8(bank, L_, R_, rk):
                            for h in range(8):
                                mm(ps[bank][0:64, h * 64:(h + 1) * 64], L_[:, h, :], R_[:, h, :], r=rk, w=[psk[bank]])
                        for lev in range(1, 4):
                            a_, bp_ = lev % 2, (lev - 1) % 2
                            Np, NTp, Nc, NTc = Nn[bp_], NTn[bp_], Nn[a_], NTn[a_]
                            kp = ['rNn%d' % bp_, 'rNTn%d' % bp_]
                            bn = nextps()
                            mm8(bn, NTp, Np, kp)
                            cp(Nc[:, :, :], v3(ps[bn][0:64, :], 8), r=[psk[bn]], w=['rNn%d' % a_], eng='act')
                            bn2 = nextps()
                            mm8(bn2, Np, NTp, kp)
                            cp(NTc[:, :, :], v3(ps[bn2][0:64, :], 8), r=[psk[bn2]], w=['rNTn%d' % a_], eng='act')
                            b1_ = nextps()
                            mm8(b1_, Nc, TT, ['rNn%d' % a_, 'rTT'])
                            b2_ = nextps()
                            mm8(b2_, NTc, Tn, ['rNTn%d' % a_, 'rTn'])
                            tt(TT[:, :, :], TT[:, :, :], v3(ps[b1_][0:64, :], 8), ALU.add, r=['rTT', psk[b1_]], w=['rTT'])
                            tt(Tn[:, :, :], Tn[:, :, :], v3(ps[b2_][0:64, :], 8), ALU.add, r=['rTn', psk[b2_]], w=['rTn'])
                        Xs, X2s = Nn[0], NTn[0]
                        bx = nextps()
                        mm8(bx, O32a, TT, ['O32a', 'rTT'])
                        cp(Xs[:, :, :], v3(ps[bx][0:64, :], 8), r=[psk[bx]], w=['rNn0'], eng='act')
                        bx2 = nextps()
                        mm8(bx2, O32Ta, Tn, ['O32Ta', 'rTn'])
                        cp(X2s[:, :, :], v3(ps[bx2][0:64, :], 8), r=[psk[bx2]], w=['rNTn0'], eng='act')
                        by1 = nextps()
                        mm8(by1, Tn, Xs, ['rTn', 'rNn0'])
                        by2 = nextps()
                        mm8(by2, TT, X2s, ['rTT', 'rNTn0'])
                        tt(TT[:, :, :], TT[:, :, :], v3(ps[by1][0:64, :], 8), ALU.add, r=['rTT', psk[by1]], w=['rTT'])
                        tt(Tn[:, :, :], Tn[:, :, :], v3(ps[by2][0:64, :], 8), ALU.add, r=['rTn', psk[by2]], w=['rTn'])
                        bx = nextps()
                        mm8(bx, O64a, TT, ['O64a', 'rTT'])
                        cp(Xs[:, :, :], v3(ps[bx][0:64, :], 8), r=[psk[bx]], w=['rNn0'], eng='act')
                        by1 = nextps()
                        mm8(by1, Tn, Xs, ['rTn', 'rNn0'])
                        tt(TT[:, :, :], TT[:, :, :], v3(ps[by1][0:64, :], 8), ALU.add, r=['rTT', psk[by1]], w=['rTT'])
                        cp(TTb[:, :, :], TT[:, :, :], r=['rTT'], w=['TTb'], eng='act')
                        if RP < 4:
                            continue
                        for (src, skey, dst, dkey) in ((bt, 'bt', btok, 'btok'), (aG, 'aG', aGtok, 'aGtok'),
                                                       (kG, 'kG', kGtok, 'kGtok'), (None, None, vtok, 'rvtok')):
                            bb = nextps()
                            pq = ps[bb][:, :].bitcast(BF16)
                            for cq in range(4):
                                if src is None:
                                    tr(pq[0:64, cq * 128:(cq + 1) * 128], v_b[:, cq, csl], ident_bf[:, :],
                                       r=[('v_b', cq), 'ident_bf'], w=[psk[bb]])
                                else:
                                    tr(pq[0:64, cq * 128:(cq + 1) * 128], src[:, cq, :], ident_bf[:, :],
                                       r=[skey, 'ident_bf'], w=[psk[bb]])
                            cp(dst[:, :, :], v3(pq[0:64, 0:512], 4), r=[psk[bb]], w=[dkey], eng='act')
                        if RP < 5:
                            continue
                        bb = nextps()
                        for h in range(8):
                            cq, hp = h // 2, h % 2
                            mm(ps[bb][0:64, h * 64:(h + 1) * 64], BTm[:, h, :], vtok[:, cq, hp * 64:(hp + 1) * 64],
                               r=['BTm', 'rvtok'], w=[psk[bb]])
                        cp(BVb[:, :, :], v3(ps[bb][0:64, :], 8), r=[psk[bb]], w=['BVb'], eng='act')
                        bb = nextps()
                        bw_ = nextps()
                        for h in range(8):
                            mm(ps[bb][0:64, h * 64:(h + 1) * 64], TTb[:, h, :], BVb[:, h, :], r=['TTb', 'BVb'], w=[psk[bb]])
                        for h in HORD:
                            cq, hp = h // 2, h % 2
                            mm(ps[bw_][hp * 64:(hp + 1) * 64, cq * 64:(cq + 1) * 64], btok[:, cq, hp * 64:(hp + 1) * 64], TTb[:, h, :],
                               r=['btok', 'TTb'], w=[psk[bw_]])
                        cp(U0[:, :, :], v3(ps[bb][0:64, :], 8), r=[psk[bb]], w=['U0'], eng='act')
                        cp(WmTb[:, :, :], v3(ps[bw_][:, 0:256], 4), r=[psk[bw_]], w=['WmTb'], eng='act')
                        if RP < 6:
                            continue
                        bu_ = nextps()
                        for h in HORD:
                            cq, hp = h // 2, h % 2
                            pb_ = slice(hp * 64, (hp + 1) * 64)
                            if hp == 1 and os.environ.get('HP0'):
                                continue
                            mm(ps[bu_][0:64, h * 64:(h + 1) * 64], WmTb[pb_, cq, :], Pbf[pb_, cq, :], r=['WmTb', 'Pbf'], w=[psk[bu_]])
                        tt(Ub[:, :, :], U0[:, :, :], v3(ps[bu_][0:64, :], 8), ALU.add, r=['U0', psk[bu_]], w=['Ub'])
                        RQ = int(os.environ.get('RQ', '9'))
                        if RQ < 2:
                            continue
                        by_ = nextps()
                        bp2 = nextps()
                        for h in HORD:
                            cq, hp = h // 2, h % 2
                            pb_ = slice(hp * 64, (hp + 1) * 64)
                            yo_ = ps[by_][pb_, cq * 64:(cq + 1) * 64]
                            mm(yo_, Pbf[pb_, cq, :], rt[pb_, cq, :], start=True, stop=False, r=['Pbf', 'rt'], w=[psk[by_]])
                            if RQ >= 3:
                                mm(yo_, Ub[:, h, :], RaT[:, h, :], start=False, stop=False, r=['Ub', 'RaT'], w=[psk[by_]])
                                mm(yo_, vtok[:, cq, pb_], RkT[:, h, :], start=False, stop=True, r=['rvtok', 'RkT'], w=[psk[by_]])
                            po_ = ps[bp2][pb_, cq * 64:(cq + 1) * 64]
                            mm(po_, aGtok[:, cq, pb_], Ub[:, h, :], start=True, stop=False, r=['aGtok', 'Ub'], w=[psk[bp2]])
                            mm(po_, kGtok[:, cq, pb_], vtok[:, cq, pb_], start=False, stop=True, r=['kGtok', 'rvtok'], w=[psk[bp2]])
                        py = v3(ps[by_][:, 0:256], 4)
                        if d == 0:
                            cp(yacc[:, :, csl], py, r=[psk[by_]], w=[('yacc', c)], eng='act')
                        else:
                            tt(yacc[:, :, csl], yacc[:, :, csl], py, ALU.add, r=[psk[by_], ('yacc', c)], w=[('yacc', c)])
                        for cq in range(4):
                            stt(Pf[:, cq, :], Pf[:, cq, :], eTot[:, cq:cq + 1], ps[bp2][:, cq * 64:(cq + 1) * 64],
                                ALU.mult, ALU.add, r=['Pf', 'eTot', psk[bp2]], w=['Pf'])
                        seg_end = (c % 4 == 3) if d == 0 else (c % 4 == 0)
                        if seg_end:
                            dma('sp', rso_d[d, c // 4], Pf[:, :, :], r=['Pf'])
                            ts(Pf[:, :, :], Pf[:, :, :], flag[:, 0:1], ALU.mult, r=['Pf', 'flag'], w=['Pf'])
                        cp(Pbf[:, :, :], Pf[:, :, :], r=['Pf'], w=['Pbf'], eng='act')
                yk = [('yacc', c) for c in range(16)]
                for cq in range(4):
                    ysrc = yacc[:, cq, :]
                    for t2_ in range(2):
                        tsl = slice(t2_ * 512, (t2_ + 1) * 512)
                        bb = nextps()
                        cp(lw[:, 0, tsl].bitcast(BF16)[:, 0:512], ysrc[:, tsl], r=yk, w=['lw'], eng='act')
                        mm(ps[bb][:, :], bones[:, :], lw[:, 0, tsl].bitcast(BF16)[:, 0:512], r=['bones', 'lw'], w=[psk[bb]])
                        stt(lw[:, 1, tsl], ps[bb][:, :], -1.0 / 64, ysrc[:, tsl], ALU.mult, ALU.add, r=[psk[bb]] + yk, w=['lw'])
                        act(lw[:, 0, tsl].bitcast(BF16)[:, 0:512], lw[:, 1, tsl], AF.Square, r=['lw'], w=['lw'])
                        bb2 = nextps()
                        mm(ps[bb2][:, :], bones[:, :], lw[:, 0, tsl].bitcast(BF16)[:, 0:512], r=['bones', 'lw'], w=[psk[bb2]])
                        act(lw[:, 2, tsl], ps[bb2][:, :], AF.Sqrt, r=[psk[bb2], 'epsb'], w=['lw'], bias=epsb[:, 2:3], scale=1.0 / 64)
                        rcp(lw[:, 2, tsl], lw[:, 2, tsl], r=['lw'], w=['lw'])
                        stt(lw[:, 1, tsl], lw[:, 1, tsl], ppc('ln_w')[:, cq:cq + 1], lw[:, 2, tsl], ALU.mult, ALU.mult,
                            r=['lw', 'lw', 'pp'], w=['lw'])
                        stt(lw[:, 1, tsl], lw[:, 1, tsl], ppc('ln_b')[:, cq:cq + 1], bon[:, cq, tsl], ALU.add, ALU.add,
                            r=['lw', 'pp', ('bon', cq)], w=['lw'])
                        tt(mix1T[:, 4 + cq, tsl], lw[:, 1, tsl], gate[:, cq, tsl], ALU.mult, r=['lw', ('gate', cq)],
                           w=[('mix1T', 4 + cq)])
                pe_mode('end')
                state['drain'] = False
                S.flush()

        if STAGE >= 1:
            with contextlib.ExitStack() as sc:
                alloc_work(sc, 'p1')
                layer0_mixer(sc)
                S.flush()
        if STAGE >= 4:
            with contextlib.ExitStack() as sc:
                alloc_work(sc, 'p2', nwb=3)
                uT = sc.enter_context(nc.sbuf_tensor("uT", [128, 32, NT], BF16))
                mlp(0, uT)
                S.flush()
        if STAGE >= 6:
            layer1_mixer()
        with contextlib.ExitStack() as sc:
            alloc_work(sc, 'p7', nwb=3)
            if STAGE >= 5:
                uT = sc.enter_context(nc.sbuf_tensor("uT2", [128, 32, NT], BF16))
                mlp(1, uT)
            yo = sc.enter_context(nc.sbuf_tensor("yo", [128, 2, 512], F32))
            norm_mod(ppc('norm_final'), None, yo, 'yo', final=True)
            S.flush()
    return nc


def make_l1_consts():
    rm = np.zeros((128, 128), np.float32)
    for hb in range(2):
        for d in range(64):
            if (d % 32) < 16:
                rm[hb * 64 + d + 16, hb * 64 + d] = -1.0
            else:
                rm[hb * 64 + d - 16, hb * 64 + d] = 1.0
    bones = np.zeros((128, 128), np.float32)
    bones[:64, :64] = 1.0
    bones[64:, 64:] = 1.0
    t = np.arange(NT)
    row = (t // 64).astype(np.float32)
    col = (t % 64).astype(np.float32)
    inv = (10000.0 ** (-np.arange(0, 32, 2, dtype=np.float32) / 32)).astype(np.float32)
    cs = np.zeros((128, NT), np.float32)
    sn = np.zeros((128, NT), np.float32)
    for p in range(128):
        d = p % 64
        pos = row if d < 32 else col
        ang = (pos * inv[(d % 32) % 16]).astype(np.float32)
        cs[p] = np.cos(ang)
        sn[p] = np.sin(ang)
    NEG = -30000.0
    mk = np.full((2, 8, 128, 896), NEG, np.float32)
    for n in range(8):
        tq = n * 128 + np.arange(128)[:, None]
        tk = (n - 1) * 128 + np.arange(384)[None, :]
        inr = (tk >= 0) & (tk < NT)
        mk[0, n, :, :384] = np.where(inr & ((tk // 256) == (tq // 256)), 0.0, NEG)
        mk[1, n, :, :384] = np.where(inr & (np.abs(tk - tq) <= 128), 0.0, NEG)
        mk[1, n, :, 384:] = 0.0
    return rm, bones, cs, sn, mk


def kernel(**inp):
    inp = {k: np.asarray(v) for k, v in inp.items()}
    nc = build_nc()
    cst = make_consts()
    pp = make_pp(inp)
    xp = inp['x_prompt'].astype(np.float32)
    xs = inp['x_sample'].astype(np.float32)
    shared = {
        'pp': pp, 'cst': cst,
        'mod_w': np.ascontiguousarray(inp['mod_w'], np.float32),
        'mlp_w1': np.ascontiguousarray(inp['mlp_w1'], np.float32),
        'mlp_w2': np.ascontiguousarray(inp['mlp_w2'], np.float32),
        'ev_w_in': np.ascontiguousarray(inp['ev_w_in'][0], np.float32),
        'ev_w_out': np.ascontiguousarray(inp['ev_w_out'][0], np.float32),
    }
    rm, bones, rcs, rsn, mk = make_l1_consts()
    wi = np.asarray(inp['od_w_in'][0], np.float32)
    shared['od_w_in_x'] = np.ascontiguousarray(np.concatenate(
        [wi[:, 0:512], wi[:, 512:576], wi[:, 512:576], wi[:, 576:640], wi[:, 576:640], wi[:, 640:768], wi[:, 768:]], 1))
    shared['od_w_out'] = np.ascontiguousarray(inp['od_w_out'][0], np.float32)
    shared['w_up'] = np.ascontiguousarray(np.asarray(inp['rwkv_w_up'][0], np.float32).reshape(128, 512))
    shared['a_up'] = np.ascontiguousarray(np.asarray(inp['rwkv_a_up'][0], np.float32).reshape(128, 512))
    shared['g_up'] = np.ascontiguousarray(inp['rwkv_g_up'][0], np.float32)
    shared['bones'] = bones
    shared['rm'] = rm

    def st_in(sv):
        a = np.asarray(sv, np.float32).transpose(0, 2, 1).reshape(4, 2, 64, 64)
        return np.ascontiguousarray(a.transpose(1, 2, 0, 3).reshape(128, 4, 64))

    def st_out(a):
        b = a.reshape(2, 64, 4, 64).transpose(2, 0, 1, 3).reshape(8, 64, 64)
        return b.transpose(0, 2, 1)
    in_maps = []
    for core in range(8):
        m = dict(shared)
        if core < 4:
            xt = xp[core * 4:(core + 1) * 4].reshape(NT, 1024)
            m['cv'] = fm(inp['c_ctx'], 8)
            m['flag'] = np.zeros((128, 1), np.float32)
            m['gs0'] = np.zeros((2, 4, 128, 128), np.float32)
            m['ropec'] = np.ones((128, NT), np.float32)
            m['ropes'] = np.zeros((128, NT), np.float32)
            m['maskb'] = mk[0]
            m['ckT'] = np.zeros((2, 128, 512), np.float32)
            m['cvt'] = np.zeros((128, 4, 128), np.float32)
            m['rs0'] = np.zeros((2, 128, 4, 64), np.float32)
        else:
            b = core - 4
            xt = xs[b]
            m['cv'] = fm(inp['c'][b], 8)
            m['flag'] = np.ones((128, 1), np.float32)
            m['gs0'] = np.ascontiguousarray(
                np.stack([inp['state_gdn_fwd'][b, 0], inp['state_gdn_bwd'][b, 0]]), np.float32)
            m['ropec'] = rcs
            m['ropes'] = rsn
            m['maskb'] = mk[1]
            ck = np.asarray(inp['cache_attn_k'][b, 0], np.float32)
            m['ckT'] = np.ascontiguousarray(np.stack([np.concatenate([ck[k].T, ck[k].T], 0) for k in range(2)]))
            cvv = np.asarray(inp['cache_attn_v'][b, 0], np.float32)
            m['cvt'] = np.ascontiguousarray(cvv.transpose(1, 0, 2).reshape(4, 128, 128).transpose(1, 0, 2))
            m['rs0'] = np.stack([st_in(inp['state_rwkv_fwd'][b, 0]), st_in(inp['state_rwkv_bwd'][b, 0])])
        m['xT'] = np.ascontiguousarray(xt.T)
        in_maps.append(m)
    res = run_bass_kernel_spmd(nc, in_maps, core_ids=list(range(8)))
    R = res.results
    y_prompt = np.stack([R[c]['yT'].T.reshape(4, 256, 1024) for c in range(4)]).reshape(16, 256, 1024)
    y_sample = np.stack([R[4 + b]['yT'].T for b in range(4)])
    gf = np.stack([R[c]['gso'][0] for c in range(4)]).reshape(16, 1, 4, 128, 128)
    gb = np.stack([R[c]['gso'][1] for c in range(4)]).reshape(16, 1, 4, 128, 128)
    def kvout(i):
        o = np.stack([R[c]['kvo'][i] for c in range(4)])
        o = o.reshape(4, 2, 64, 4, 256).transpose(0, 3, 1, 4, 2)
        return np.ascontiguousarray(o.reshape(16, 1, 2, 256, 64)).astype(np.float32)

    def rsout(d):
        o = np.stack([np.stack([st_out(R[c]['rso'][d, sg]) for sg in range(4)]) for c in range(4)])
        return np.ascontiguousarray(o.reshape(16, 1, 8, 64, 64)).astype(np.float32)
    return (y_prompt.astype(np.float32), y_sample.astype(np.float32), gf.astype(np.float32), gb.astype(np.float32),
            kvout(0), kvout(1), rsout(0), rsout(1))
```
